# Optimizing a Trainium2 kernel written in Bass

```python
import math
import jax, jax.numpy as jnp
from jax import lax
import numpy as np

D_MODEL = 2048
BATCH = 2
SEQ = 8192
DEPTH = 1

N_Q_HEADS = 16
N_KV_HEADS = 2
HEAD_DIM = 64
Q_PER_KV = N_Q_HEADS // N_KV_HEADS
WINDOW = 128
BLOCK = 128
ATTN_WIDTH = N_Q_HEADS * HEAD_DIM
KV_WIDTH = N_KV_HEADS * HEAD_DIM
SSM_GROUP = 16
SSM_GROUPS = 32
SSM_WIDTH = SSM_GROUP * SSM_GROUPS
SSM_STATE = 64
DT_MIN = 0.001
DT_MAX = 0.1
D_FF = 5632
CONV_WIDTH = 3
RMS_EPS = 1e-6
IN_COLS = ATTN_WIDTH + 2 * KV_WIDTH + SSM_WIDTH + 2 * D_MODEL
SPLIT_POINTS = (ATTN_WIDTH, ATTN_WIDTH + KV_WIDTH, ATTN_WIDTH + 2 * KV_WIDTH,
                ATTN_WIDTH + 2 * KV_WIDTH + SSM_WIDTH,
                ATTN_WIDTH + 2 * KV_WIDTH + SSM_WIDTH + D_MODEL)
NEG_BIG = -1e30

kernel_name = "hybrid_swa_s5_convffn_block"


def rms_norm(x, g):
    xf = x.astype(jnp.float32)
    y = xf * lax.rsqrt(jnp.mean(xf * xf, axis=-1, keepdims=True) + RMS_EPS)
    return (y * g.astype(jnp.float32)).astype(x.dtype)


def sliding_window_attention(q, k, v, sinks):
    b, l = q.shape[0], q.shape[1]
    nb = l // BLOCK
    qb = q.reshape(b, nb, BLOCK, N_KV_HEADS, Q_PER_KV, HEAD_DIM)
    kb = k.reshape(b, nb, BLOCK, N_KV_HEADS, HEAD_DIM)
    vb = v.reshape(b, nb, BLOCK, N_KV_HEADS, HEAD_DIM)

    def with_prev(t):
        prev = jnp.pad(t, ((0, 0), (1, 0), (0, 0), (0, 0), (0, 0)))[:, :-1]
        return jnp.concatenate([prev, t], axis=2)

    kx = with_prev(kb)
    vx = with_prev(vb)
    scores = jnp.einsum('bnqgrd,bnsgd->bngrqs', qb, kx).astype(jnp.float32) * (HEAD_DIM ** -0.5)
    q_idx = jnp.arange(BLOCK)[:, None]
    s_idx = jnp.arange(2 * BLOCK)[None, :]
    dist = q_idx + BLOCK - s_idx
    band = (dist >= 0) & (dist < WINDOW)
    valid = band[None] & ((jnp.arange(nb)[:, None, None] > 0) | (s_idx[None] >= BLOCK))
    slopes = 2.0 ** (-8.0 * jnp.arange(1, N_Q_HEADS + 1, dtype=jnp.float32) / N_Q_HEADS)
    slopes = slopes.reshape(N_KV_HEADS, Q_PER_KV)
    alibi = -slopes[:, :, None, None] * dist.astype(jnp.float32)[None, None]
    scores = scores + alibi[None, None]
    scores = jnp.where(valid[None, :, None, None], scores, NEG_BIG)
    sink = sinks.astype(jnp.float32).reshape(N_KV_HEADS, Q_PER_KV)[None, None, :, :, None, None]
    m = jnp.maximum(jnp.max(scores, axis=-1, keepdims=True), sink)
    p = jnp.exp(scores - m)
    p = p / (jnp.sum(p, axis=-1, keepdims=True) + jnp.exp(sink - m))
    out = jnp.einsum('bngrqs,bnsgd->bnqgrd', p.astype(v.dtype), vx)
    return out.reshape(b, l, ATTN_WIDTH)


def s5_ssm(u, a_re, a_im, log_dt, b_re, b_im, c_re, c_im, d_skip):
    bsz, l = u.shape[0], u.shape[1]
    ug = u.reshape(bsz, l, SSM_GROUPS, SSM_GROUP)
    dt = jnp.exp(log_dt)[:, None]
    mag = jnp.exp(a_re * dt)
    ab_re = mag * jnp.cos(a_im * dt)
    ab_im = mag * jnp.sin(a_im * dt)
    nr = ab_re - 1.0
    ni = ab_im
    den = a_re * a_re + a_im * a_im
    z_re = (nr * a_re + ni * a_im) / den
    z_im = (ni * a_re - nr * a_im) / den
    bb_re = z_re[..., None] * b_re - z_im[..., None] * b_im
    bb_im = z_re[..., None] * b_im + z_im[..., None] * b_re
    bu_re = jnp.einsum('gph,blgh->blgp', bb_re, ug)
    bu_im = jnp.einsum('gph,blgh->blgp', bb_im, ug)
    a_re_t = jnp.broadcast_to(ab_re, bu_re.shape)
    a_im_t = jnp.broadcast_to(ab_im, bu_im.shape)

    def combine(left, right):
        a1r, a1i, b1r, b1i = left
        a2r, a2i, b2r, b2i = right
        return (a1r * a2r - a1i * a2i,
                a1r * a2i + a1i * a2r,
                a2r * b1r - a2i * b1i + b2r,
                a2r * b1i + a2i * b1r + b2i)

    _, _, xs_re, xs_im = lax.associative_scan(combine, (a_re_t, a_im_t, bu_re, bu_im), axis=1)
    y = (jnp.einsum('ghp,blgp->blgh', c_re, xs_re)
         - jnp.einsum('ghp,blgp->blgh', c_im, xs_im)
         + d_skip.reshape(SSM_GROUPS, SSM_GROUP) * ug)
    return y.reshape(bsz, l, SSM_WIDTH)


def causal_depthwise_conv(x, w, b):
    l = x.shape[1]
    xp = jnp.pad(x, ((0, 0), (CONV_WIDTH - 1, 0), (0, 0)))
    y = b
    for k in range(CONV_WIDTH):
        y = y + w[k] * xp[:, k:k + l]
    return y


def setup_inputs(seed: int = 0) -> dict:
    key = jax.random.key(seed)
    ks = jax.random.split(key, 24)
    f32 = jnp.float32
    nrm = lambda k, shape, s: jax.random.normal(k, shape, f32) * s
    x = jax.random.normal(ks[0], (BATCH, SEQ, D_MODEL), f32)
    attn_norm_g = 1.0 + nrm(ks[1], (DEPTH, D_MODEL), 0.02)
    w_in = nrm(ks[2], (DEPTH, D_MODEL, IN_COLS), D_MODEL ** -0.5)
    b_in = nrm(ks[3], (DEPTH, IN_COLS), 0.02)
    attn_sinks = nrm(ks[4], (DEPTH, N_Q_HEADS), 0.5)
    ssm_a_re = -0.5 + nrm(ks[5], (DEPTH, SSM_GROUPS, SSM_STATE), 0.01)
    ssm_a_im = (jnp.pi * jnp.arange(SSM_STATE, dtype=f32))[None, None, :] + nrm(ks[6], (DEPTH, SSM_GROUPS, SSM_STATE), 0.01)
    ssm_log_dt = jax.random.uniform(ks[7], (DEPTH, SSM_GROUPS), f32, minval=math.log(DT_MIN), maxval=math.log(DT_MAX))
    ssm_b_re = nrm(ks[8], (DEPTH, SSM_GROUPS, SSM_STATE, SSM_GROUP), (2 * SSM_GROUP) ** -0.5)
    ssm_b_im = nrm(ks[9], (DEPTH, SSM_GROUPS, SSM_STATE, SSM_GROUP), (2 * SSM_GROUP) ** -0.5)
    ssm_c_re = nrm(ks[10], (DEPTH, SSM_GROUPS, SSM_GROUP, SSM_STATE), (2 * SSM_STATE) ** -0.5)
    ssm_c_im = nrm(ks[11], (DEPTH, SSM_GROUPS, SSM_GROUP, SSM_STATE), (2 * SSM_STATE) ** -0.5)
    ssm_d = nrm(ks[12], (DEPTH, SSM_WIDTH), 1.0)
    w_glu = nrm(ks[13], (DEPTH, SSM_WIDTH, 2 * SSM_WIDTH), SSM_WIDTH ** -0.5)
    b_glu = nrm(ks[14], (DEPTH, 2 * SSM_WIDTH), 0.02)
    w_branch_attn = nrm(ks[15], (DEPTH, ATTN_WIDTH, D_MODEL), ATTN_WIDTH ** -0.5)
    w_branch_ssm = nrm(ks[16], (DEPTH, SSM_WIDTH, D_MODEL), SSM_WIDTH ** -0.5)
    w_out = nrm(ks[17], (DEPTH, D_MODEL, D_MODEL), D_MODEL ** -0.5)
    ffn_norm_g = 1.0 + nrm(ks[18], (DEPTH, D_MODEL), 0.02)
    w_up = nrm(ks[19], (DEPTH, D_MODEL, 2 * D_FF), D_MODEL ** -0.5)
    conv_w = nrm(ks[20], (DEPTH, CONV_WIDTH, D_FF), CONV_WIDTH ** -0.5)
    conv_b = nrm(ks[21], (DEPTH, D_FF), 0.02)
    w_down = nrm(ks[22], (DEPTH, D_FF, D_MODEL), D_FF ** -0.5)
    final_norm_g = 1.0 + nrm(ks[23], (D_MODEL,), 0.02)
    return {"x": x, "attn_norm_g": attn_norm_g, "w_in": w_in, "b_in": b_in, "attn_sinks": attn_sinks,
            "ssm_a_re": ssm_a_re, "ssm_a_im": ssm_a_im, "ssm_log_dt": ssm_log_dt,
            "ssm_b_re": ssm_b_re, "ssm_b_im": ssm_b_im, "ssm_c_re": ssm_c_re, "ssm_c_im": ssm_c_im,
            "ssm_d": ssm_d, "w_glu": w_glu, "b_glu": b_glu, "w_branch_attn": w_branch_attn,
            "w_branch_ssm": w_branch_ssm, "w_out": w_out, "ffn_norm_g": ffn_norm_g, "w_up": w_up,
            "conv_w": conv_w, "conv_b": conv_b, "w_down": w_down, "final_norm_g": final_norm_g}


def reference(x, attn_norm_g, w_in, b_in, attn_sinks, ssm_a_re, ssm_a_im, ssm_log_dt, ssm_b_re, ssm_b_im,
              ssm_c_re, ssm_c_im, ssm_d, w_glu, b_glu, w_branch_attn, w_branch_ssm, w_out, ffn_norm_g,
              w_up, conv_w, conv_b, w_down, final_norm_g):
    bsz, seq = x.shape[0], x.shape[1]
    for i in range(DEPTH):
        h = rms_norm(x, attn_norm_g[i])
        proj = h @ w_in[i] + b_in[i]
        q, k, v, u, gate_attn, gate_ssm = jnp.split(proj, SPLIT_POINTS, axis=-1)
        q = q.reshape(bsz, seq, N_Q_HEADS, HEAD_DIM)
        k = k.reshape(bsz, seq, N_KV_HEADS, HEAD_DIM)
        v = v.reshape(bsz, seq, N_KV_HEADS, HEAD_DIM)
        attn = sliding_window_attention(q, k, v, attn_sinks[i])
        y = s5_ssm(u, ssm_a_re[i], ssm_a_im[i], ssm_log_dt[i], ssm_b_re[i], ssm_b_im[i],
                   ssm_c_re[i], ssm_c_im[i], ssm_d[i])
        y_val, y_gate = jnp.split(jax.nn.gelu(y, approximate=False) @ w_glu[i] + b_glu[i], 2, axis=-1)
        ssm = y_val * jax.nn.sigmoid(y_gate)
        merged = (jax.nn.sigmoid(gate_attn) * (attn @ w_branch_attn[i])
                  + jax.nn.sigmoid(gate_ssm) * (ssm @ w_branch_ssm[i]))
        x = x + merged @ w_out[i]
        h = rms_norm(x, ffn_norm_g[i])
        val, gate = jnp.split(h @ w_up[i], 2, axis=-1)
        gate = causal_depthwise_conv(gate, conv_w[i], conv_b[i])
        x = x + (val * jax.nn.gelu(gate, approximate=False)) @ w_down[i]
    return rms_norm(x, final_norm_g)
```

```python
import contextlib
import math
import numpy as np
import concourse.bass as bass
import concourse.mybir as mybir
from concourse.bass_utils import run_bass_kernel_spmd

F32 = mybir.dt.float32
BF16 = mybir.dt.bfloat16
AF = mybir.ActivationFunctionType
ALU = mybir.AluOpType

D = 2048
NCORES = 8
WIN = 8192
NBLK = 64
MB0 = 46
NMAIN = 2304
OWN0 = 256
NCH0 = MB0 * 8
NCHM = 144
KVALS = list(range(17)) + [64]
NK = len(KVALS)
DFF = 5632
KB = 1024


class Sem:
    def __init__(self, h):
        self.h = h
        self.cnt = 0


class Buf:
    __slots__ = ("name", "w", "r", "dsem")

    def __init__(self, name):
        self.name = name
        self.w = None
        self.r = {}
        self.dsem = None


class Sched:
    ENG = ("pe", "act", "dve", "pool", "sp")

    def __init__(self, nc, stack):
        self.nc = nc
        self.stack = stack
        self.ops = {e: [] for e in self.ENG}
        self.esem = {e: Sem(stack.enter_context(nc.semaphore("es_" + e))) for e in self.ENG}
        self.future = {e: [self.esem[e], None] for e in self.ENG}
        self.last = {e: None for e in self.ENG}
        self.extra = {e: [] for e in self.ENG}
        self.dsems = []
        self.nbuf = 0

    def buf(self, name=None):
        self.nbuf += 1
        return Buf(name or "b%d" % self.nbuf)

    def bufs(self, n, name="b"):
        return [self.buf("%s%d" % (name, i)) for i in range(n)]

    def _dsem(self, b):
        if b.dsem is None:
            b.dsem = Sem(self.stack.enter_context(self.nc.semaphore("ds_%d" % len(self.dsems))))
            self.dsems.append(b.dsem)
        return b.dsem

    def _collect(self, eng, reads, writes):
        waits = list(self.extra[eng])
        self.extra[eng] = []
        for b in reads:
            if b.w is not None:
                waits.append(b.w)
        for b in writes:
            if b.w is not None:
                waits.append(b.w)
            waits.extend(b.r.values())
        out = []
        for t in waits:
            if eng == "pe" and t[0] is self.esem[eng]:
                continue
            if t[0] in self.dsems:
                t = [t[0], t[0].cnt]
            out.append(t)
        return out

    def op(self, eng, fn, reads=(), writes=(), sig=True):
        waits = self._collect(eng, reads, writes)
        tok = self.future[eng]
        inc = None
        if sig:
            s = self.esem[eng]
            s.cnt += 1
            tok[1] = s.cnt
            self.future[eng] = [s, None]
            inc = (s, 1)
            self.last[eng] = tok
        self.ops[eng].append((waits, fn, inc))
        for b in writes:
            b.w = tok
            b.r = {}
        for b in reads:
            b.r[id(tok[0])] = tok
        return tok

    def dma(self, eng, out_ap, in_ap, reads=(), writes=(), sem_buf=None, nonc=False):
        waits = self._collect(eng, reads, writes)
        s = self._dsem(sem_buf)
        s.cnt += 16
        tok = [s, s.cnt]
        nc = self.nc

        def fn(e, out_ap=out_ap, in_ap=in_ap):
            if nonc:
                with nc.allow_non_contiguous_dma(reason="small param gather"):
                    return e.dma_start(out=out_ap, in_=in_ap)
            return e.dma_start(out=out_ap, in_=in_ap)

        self.ops[eng].append((waits, fn, (s, 16)))
        for b in writes:
            b.w = tok
            b.r = {}
        for b in reads:
            b.r[id(s)] = tok
        return tok

    def barrier(self):
        toks = []
        for e in self.ENG:
            if self.last[e] is not None:
                toks.append(self.last[e])
        for s in self.dsems:
            if s.cnt:
                toks.append([s, s.cnt])
        for e in self.ENG:
            self.extra[e].extend(toks)

    def emit(self, final_toks):
        nc = self.nc
        for e in self.ENG:
            assert self.future[e][1] is None
        with nc.Block() as block:
            def replay(eng, h):
                waited = {}
                for waits, fn, inc in self.ops[eng]:
                    for t in waits:
                        assert t[1] is not None, "unresolved token"
                        k = id(t[0])
                        if waited.get(k, 0) < t[1]:
                            h.wait_ge(t[0].h, t[1])
                            waited[k] = t[1]
                    ins = fn(h)
                    if inc is not None:
                        ins.then_inc(inc[0].h, inc[1])
                if eng == "sp":
                    for t in final_toks:
                        h.wait_ge(t[0].h, t[0].cnt)

            @block.tensor
            def _(h):
                replay("pe", h)

            @block.scalar
            def _(h):
                replay("act", h)

            @block.vector
            def _(h):
                replay("dve", h)

            @block.gpsimd
            def _(h):
                replay("pool", h)

            @block.sync
            def _(h):
                replay("sp", h)


class K:
    def __init__(self, debug=()):
        self.debug = set(debug)
        self.stack = contextlib.ExitStack()
        self.nc = bass.Bass("TRN2", target_bir_lowering=False)
        self.S = None
        self.stop = None

    def din(self, name, shape, dt=F32):
        return self.nc.dram_tensor(name, list(shape), dt, kind="ExternalInput").ap()

    def dscr(self, name, shape, dt, out=False):
        kind = "ExternalOutput" if (out or name in self.debug) else "Internal"
        return self.nc.dram_tensor(name, list(shape), dt, kind=kind).ap()

    def sb(self, name, shape, dt, off):
        return self.nc.alloc_sbuf_tensor_at(name, list(shape), dt, offset=off + 16 * KB)

    def build(self):
        nc = self.nc
        st = self.stack
        S = self.S = Sched(nc, st)
        dbg = self.debug
        xw = self.din("xw", [WIN, D])
        maskw = self.din("maskw", [WIN])
        g1 = self.din("attn_norm_g", [D])
        w_in = self.din("w_in", [D, 5888])
        b_in = self.din("b_in", [5888])
        sinks = self.din("attn_sinks", [16])
        a_re = self.din("ssm_a_re", [32, 64])
        a_im = self.din("ssm_a_im", [32, 64])
        log_dt = self.din("ssm_log_dt", [32])
        b_re = self.din("ssm_b_re", [32, 64, 16])
        b_im = self.din("ssm_b_im", [32, 64, 16])
        c_re = self.din("ssm_c_re", [32, 16, 64])
        c_im = self.din("ssm_c_im", [32, 16, 64])
        ssm_d = self.din("ssm_d", [512])
        w_glu = self.din("w_glu", [512, 1024])
        b_glu = self.din("b_glu", [1024])
        w_ba = self.din("w_branch_attn", [1024, D])
        w_bs = self.din("w_branch_ssm", [512, D])
        w_out = self.din("w_out", [D, D])
        g2 = self.din("ffn_norm_g", [D])
        w_up = self.din("w_up", [D, 2 * DFF])
        conv_w = self.din("conv_w", [3, DFF])
        conv_b = self.din("conv_b", [DFF])
        w_down = self.din("w_down", [DFF, D])
        g3 = self.din("final_norm_g", [D])
        abias = self.din("abias", [128, 16, 256])
        out = self.nc.dram_tensor("out", [2048, D], F32, kind="ExternalOutput").ap()
        FFd = self.dscr("FFd", [128, 16 * 2 * 16 * 32], BF16)
        TZd = self.dscr("TZd", [128, 16 * 4 * 128], BF16)
        XT = self.dscr("XT", [16, 128, NMAIN], F32)
        HT = self.dscr("HT", [16, 128, NMAIN], BF16)
        YG = self.dscr("YG", [4, 128, NMAIN], BF16) if "YG" in dbg else None
        UM = self.dscr("UM", [4, 128, NMAIN], BF16) if "UM" in dbg else None
        XSd = self.dscr("XSd", [128, 2 * 16 * NCHM], F32) if "XSd" in dbg else None

        o = 0

        def cal(name, shape, dt):
            nonlocal o
            nbytes = int(np.prod(shape[1:])) * (4 if dt == F32 else 2)
            t = self.sb(name, shape, dt, o)
            o += (nbytes + 31) // 32 * 32
            return t

        ident = cal("ident", [128, 128], F32)
        identb = cal("identb", [128, 128], BF16)
        onesb = cal("onesb", [128, 128], BF16)
        onesf = cal("onesf", [128, 128], F32)
        g1c = cal("g1c", [128, 16], F32)
        g2c = cal("g2c", [128, 16], F32)
        g3c = cal("g3c", [128, 16], F32)
        binc = cal("binc", [128, 46], F32)
        bgluc = cal("bgluc", [128, 8], F32)
        cwc = cal("cwc", [128, 3, 44], F32)
        cbc = cal("cbc", [128, 44], F32)
        dcol = cal("dcol", [128, 4], F32)
        AAt = cal("AAt", [128, 2, 16], F32)
        BBt = cal("BBt", [128, 2, 16], F32)
        X4 = [cal("X4a", [128, 3, 16], F32), cal("X4b", [128, 3, 16], F32)]
        P1 = cal("P1", [128, 2, 16], F32)
        P2 = cal("P2", [128, 2, 16], F32)
        A4re = cal("A4re", [128, 16], F32)
        A4im = cal("A4im", [128, 16], F32)
        A16re = cal("A16re", [128, 16], F32)
        A16im = cal("A16im", [128, 16], F32)
        AA64 = cal("AA64", [128, 2, 16], F32)
        BB64 = cal("BB64", [128, 2, 16], F32)
        assert o <= 8 * KB, o
        cbuf = S.buf("consts")
        identity_src = self.din("identity", [128, 128])
        S.dma("sp", ident[:], identity_src, writes=[cbuf], sem_buf=cbuf)
        cbuf2 = S.buf("consts2")
        S.dma("pool", identb[:], identity_src, writes=[cbuf2], sem_buf=cbuf2)
        S.op("dve", lambda e: e.memset(onesb[:], 1.0), writes=[cbuf])
        S.op("dve", lambda e: e.memset(onesf[:], 1.0), writes=[cbuf])
        for t, src in ((g1c, g1), (g2c, g2), (g3c, g3)):
            S.dma("sp", t[:], src.rearrange("(k p) -> p k", p=128), writes=[cbuf], sem_buf=cbuf, nonc=True)
        S.dma("sp", binc[:], b_in.rearrange("(k p) -> p k", p=128), writes=[cbuf], sem_buf=cbuf, nonc=True)
        S.dma("sp", bgluc[:], b_glu.rearrange("(k p) -> p k", p=128), writes=[cbuf], sem_buf=cbuf, nonc=True)
        S.dma("sp", cwc[:], conv_w.rearrange("w (k p) -> p w k", p=128), writes=[cbuf], sem_buf=cbuf, nonc=True)
        S.dma("sp", cbc[:], conv_b.rearrange("(k p) -> p k", p=128), writes=[cbuf], sem_buf=cbuf, nonc=True)
        S.dma("sp", dcol[:], ssm_d.rearrange("(k p) -> p k", p=128), writes=[cbuf], sem_buf=cbuf, nonc=True)

        PS = [st.enter_context(nc.psum_tensor("ps%d" % i, [128, 512], F32)) for i in range(8)]
        PSB = S.bufs(8, "psb")

        base = 12 * KB
        SW = self.sb("SW", [128, 16, 2, 4, 128], BF16, 30 * KB)
        swb = S.buf("SW")
        po = 62 * KB

        def pal(name, shape, dt):
            nonlocal po
            nbytes = int(np.prod(shape[1:])) * (4 if dt == F32 else 2)
            t = self.sb(name, shape, dt, po)
            po += (nbytes + 31) // 32 * 32
            return t

        CAre = pal("CAre", [128, 17, 16, 32], F32)
        CAim = pal("CAim", [128, 17, 16, 32], F32)
        SWT = pal("SWT", [128, 16, 16, 32], F32)
        TMPB = pal("TMPB", [128, 17, 16, 32], F32)
        assert po <= 200 * KB, po
        po = base
        are = pal("are", [128, 16], F32)
        aim = pal("aim", [128, 16], F32)
        ldt = pal("ldt", [128, 16], F32)
        dtv = pal("dtv", [128, 16], F32)
        adr = pal("adr", [128, 16], F32)
        ang = pal("ang", [128, 16], F32)
        KM = pal("KM", [128, NK, 16], F32)
        PWm = pal("PWm", [128, NK, 16], F32)
        ANG = pal("ANG", [128, NK, 16], F32)
        T17 = pal("T17", [128, NK, 16], F32)
        XS17 = self.sb("XS17", [128, NK, 16], F32, 174 * KB)
        NI = self.sb("NI", [128, NK, 16], mybir.dt.int32, 176 * KB)
        PWre = pal("PWre", [128, NK, 16], F32)
        PWim = pal("PWim", [128, NK, 16], F32)
        t16 = [pal("t16_%d" % i, [128, 16], F32) for i in range(6)]
        zre = pal("zre", [128, 16], F32)
        zim = pal("zim", [128, 16], F32)
        BEre = self.sb("BEre", [128, 16, 32], F32, 170 * KB)
        BEim = self.sb("BEim", [128, 16, 32], F32, 172 * KB)
        Bbre = pal("Bbre", [128, 16, 32], F32)
        Bbim = pal("Bbim", [128, 16, 32], F32)
        CEre = pal("CEre", [128, 16, 32], F32)
        CEim = pal("CEim", [128, 16, 32], F32)
        assert po <= 30 * KB, po
        CN4 = [self.sb("CN4re", [128, 4, 128], F32, 196 * KB), self.sb("CN4im", [128, 4, 128], F32, 198 * KB)]
        BZre = self.sb("BZre", [128, 16, 128], F32, 130 * KB)
        BZim = self.sb("BZim", [128, 16, 128], F32, 138 * KB)
        FFs = self.sb("FFs", [128, 16, 16 * 32], BF16, 130 * KB)
        p0 = S.buf("p0")

        PI = math.pi
        pin = []

        def pin_new():
            pin.append(S.buf("pin%d" % len(pin)))
            return pin[-1]

        for gl in range(2):
            sl = slice(64 * gl, 64 * gl + 64)
            S.dma("sp", are[sl, :], a_re.rearrange("(gp gl) p -> gl p gp", gl=2)[gl], writes=[pin_new()], sem_buf=pin[-1], nonc=True)
            S.dma("sp", aim[sl, :], a_im.rearrange("(gp gl) p -> gl p gp", gl=2)[gl], writes=[pin_new()], sem_buf=pin[-1], nonc=True)
            S.dma("sp", ldt[sl, :], log_dt.rearrange("(gp gl) -> gl gp", gl=2)[gl].partition_broadcast(64),
                  writes=[pin_new()], sem_buf=pin[-1], nonc=True)
        S.op("dve", lambda e: e.memset(BEre[:], 0.0), writes=[p0])
        S.op("dve", lambda e: e.memset(BEim[:], 0.0), writes=[p0])
        S.op("dve", lambda e: e.memset(CN4[0][:], 0.0), writes=[p0])
        S.op("dve", lambda e: e.memset(CN4[1][:], 0.0), writes=[p0])
        for gl in range(2):
            sl = slice(64 * gl, 64 * gl + 64)
            for t, src in ((BEre, b_re), (BEim, b_im)):
                S.dma("sp", t[sl, :, 16 * gl:16 * gl + 16], src.rearrange("(gp gl) p h -> gl p gp h", gl=2)[gl],
                      reads=[p0], writes=[pin_new()], sem_buf=pin[-1], nonc=True)
        for q in range(4):
            for gl in range(2):
                for t, src in ((CN4[0], c_re), (CN4[1], c_im)):
                    S.dma("sp", t[32 * q + 16 * gl:32 * q + 16 * gl + 16, :, 64 * gl:64 * gl + 64],
                          src.rearrange("(k q gl) h p -> q gl h k p", q=4, gl=2)[q, gl],
                          reads=[p0], writes=[pin_new()], sem_buf=pin[-1], nonc=True)
        for ki, kv in enumerate(KVALS):
            S.op("dve", lambda e, ki=ki, kv=kv: e.memset(KM[:, ki, :], float(kv)), writes=[p0])
        S.op("act", lambda e: e.activation(out=dtv[:], in_=ldt[:], func=AF.Exp), reads=[p0] + pin, writes=[p0] + pin)
        S.op("dve", lambda e: e.tensor_tensor(out=adr[:], in0=are[:], in1=dtv[:], op=ALU.mult), reads=[p0], writes=[p0])
        S.op("dve", lambda e: e.tensor_tensor(out=ang[:], in0=aim[:], in1=dtv[:], op=ALU.mult), reads=[p0], writes=[p0])

        def bc17(t):
            return t[:].unsqueeze(1).to_broadcast([128, NK, 16])

        S.op("dve", lambda e: e.tensor_tensor(out=T17[:], in0=KM[:], in1=bc17(adr), op=ALU.mult), reads=[p0], writes=[p0])
        S.op("act", lambda e: e.activation(out=PWm[:], in_=T17[:], func=AF.Exp), reads=[p0], writes=[p0])
        S.op("dve", lambda e: e.tensor_tensor(out=ANG[:], in0=KM[:], in1=bc17(ang), op=ALU.mult), reads=[p0], writes=[p0])
        def sincos(dst, shift):
            S.op("dve", lambda e: e.tensor_scalar(out=T17[:], in0=ANG[:], scalar1=1.0 / (2 * PI), scalar2=0.5 + shift / (2 * PI),
                                                   op0=ALU.mult, op1=ALU.add), reads=[p0], writes=[p0])
            S.op("dve", lambda e: e.tensor_copy(out=NI[:], in_=T17[:]), reads=[p0], writes=[p0])
            S.op("dve", lambda e: e.tensor_copy(out=T17[:], in_=NI[:]), reads=[p0], writes=[p0])
            S.op("dve", lambda e: e.tensor_scalar_add(out=XS17[:], in0=ANG[:], scalar1=shift), reads=[p0], writes=[p0])
            S.op("dve", lambda e: e.scalar_tensor_tensor(out=T17[:], in0=T17[:], scalar=-2 * PI, in1=XS17[:],
                                                          op0=ALU.mult, op1=ALU.add), reads=[p0], writes=[p0])
            S.op("dve", lambda e: e.tensor_scalar(out=XS17[:], in0=T17[:], scalar1=-PI, scalar2=2 * PI,
                                                   op0=ALU.is_lt, op1=ALU.mult), reads=[p0], writes=[p0])
            S.op("dve", lambda e: e.tensor_tensor(out=T17[:], in0=T17[:], in1=XS17[:], op=ALU.add), reads=[p0], writes=[p0])
            S.op("dve", lambda e: e.tensor_scalar(out=T17[:], in0=T17[:], scalar1=-PI, scalar2=PI,
                                                   op0=ALU.max, op1=ALU.min), reads=[p0], writes=[p0])
            S.op("act", lambda e: e.activation(out=dst[:], in_=T17[:], func=AF.Sin), reads=[p0], writes=[p0])

        sincos(PWim, 0.0)
        sincos(PWre, 0.5 * PI)
        S.op("dve", lambda e: e.tensor_tensor(out=PWre[:], in0=PWre[:], in1=PWm[:], op=ALU.mult), reads=[p0], writes=[p0])
        S.op("dve", lambda e: e.tensor_tensor(out=PWim[:], in0=PWim[:], in1=PWm[:], op=ALU.mult), reads=[p0], writes=[p0])
        nr, ni, den, ta, tb, tc = [t[:] for t in t16]

        def dv(fn):
            S.op("dve", fn, reads=[p0], writes=[p0])

        dv(lambda e: e.tensor_scalar_add(out=nr, in0=PWre[:, 1, :], scalar1=-1.0))
        dv(lambda e: e.tensor_copy(out=ni, in_=PWim[:, 1, :]))
        dv(lambda e: e.tensor_tensor(out=den, in0=are[:], in1=are[:], op=ALU.mult))
        dv(lambda e: e.tensor_tensor(out=ta, in0=aim[:], in1=aim[:], op=ALU.mult))
        dv(lambda e: e.tensor_tensor(out=den, in0=den, in1=ta, op=ALU.add))
        dv(lambda e: e.reciprocal(out=den, in_=den))
        dv(lambda e: e.tensor_tensor(out=ta, in0=nr, in1=are[:], op=ALU.mult))
        dv(lambda e: e.tensor_tensor(out=tb, in0=ni, in1=aim[:], op=ALU.mult))
        dv(lambda e: e.tensor_tensor(out=ta, in0=ta, in1=tb, op=ALU.add))
        dv(lambda e: e.tensor_tensor(out=zre[:], in0=ta, in1=den, op=ALU.mult))
        dv(lambda e: e.tensor_tensor(out=ta, in0=ni, in1=are[:], op=ALU.mult))
        dv(lambda e: e.tensor_tensor(out=tb, in0=nr, in1=aim[:], op=ALU.mult))
        dv(lambda e: e.tensor_tensor(out=ta, in0=ta, in1=tb, op=ALU.subtract))
        dv(lambda e: e.tensor_tensor(out=zim[:], in0=ta, in1=den, op=ALU.mult))
        dv(lambda e: e.tensor_copy(out=AAt[:, 0, :], in_=PWre[:, 16, :]))
        dv(lambda e: e.tensor_copy(out=AAt[:, 1, :], in_=PWre[:, 16, :]))
        dv(lambda e: e.tensor_copy(out=BBt[:, 1, :], in_=PWim[:, 16, :]))
        dv(lambda e: e.tensor_scalar_mul(out=BBt[:, 0, :], in0=PWim[:, 16, :], scalar1=-1.0))
        dv(lambda e: e.tensor_copy(out=A4re[:], in_=PWre[:, 4, :]))
        dv(lambda e: e.tensor_copy(out=A4im[:], in_=PWim[:, 4, :]))
        dv(lambda e: e.tensor_copy(out=A16re[:], in_=PWre[:, 16, :]))
        dv(lambda e: e.tensor_copy(out=A16im[:], in_=PWim[:, 16, :]))
        dv(lambda e: e.tensor_copy(out=AA64[:, 0, :], in_=PWre[:, 17, :]))
        dv(lambda e: e.tensor_copy(out=AA64[:, 1, :], in_=PWre[:, 17, :]))
        dv(lambda e: e.tensor_copy(out=BB64[:, 1, :], in_=PWim[:, 17, :]))
        dv(lambda e: e.tensor_scalar_mul(out=BB64[:, 0, :], in0=PWim[:, 17, :], scalar1=-1.0))

        def bc32(t):
            return t[:].unsqueeze(2).to_broadcast([128, 16, 32])

        T32a = TMPB[:, 0, :, :]
        T32b = TMPB[:, 1, :, :]
        dv(lambda e: e.tensor_tensor(out=T32a, in0=BEre[:], in1=bc32(zre), op=ALU.mult))
        dv(lambda e: e.tensor_tensor(out=T32b, in0=BEim[:], in1=bc32(zim), op=ALU.mult))
        dv(lambda e: e.tensor_tensor(out=Bbre[:], in0=T32a, in1=T32b, op=ALU.subtract))
        dv(lambda e: e.tensor_tensor(out=T32a, in0=BEim[:], in1=bc32(zre), op=ALU.mult))
        dv(lambda e: e.tensor_tensor(out=T32b, in0=BEre[:], in1=bc32(zim), op=ALU.mult))
        dv(lambda e: e.tensor_tensor(out=Bbim[:], in0=T32a, in1=T32b, op=ALU.add))
        for i, (cn, ce) in enumerate(((CN4[0], CEre), (CN4[1], CEim))):
            for k in range(4):
                S.op("pe", lambda e, cn=cn, k=k: e.transpose(out=PS[0][:, 128 * k:128 * k + 128], in_=cn[:, k, :], identity=ident[:]),
                     reads=[p0, cbuf], writes=[PSB[0]], sig=(k == 3))
            S.op("act", lambda e, ce=ce: e.activation(out=ce[:].rearrange("p g c -> p (g c)"), in_=PS[0][:], func=AF.Copy),
                 reads=[PSB[0]], writes=[p0])

        def bck(t):
            return t[:].unsqueeze(1).to_broadcast([128, 17, 16, 32])

        def bcc(t):
            return t[:, 0:17, :].unsqueeze(3).to_broadcast([128, 17, 16, 32])

        dv(lambda e: e.tensor_tensor(out=CAre[:], in0=bck(CEre), in1=bcc(PWre), op=ALU.mult))
        dv(lambda e: e.tensor_tensor(out=TMPB[:], in0=bck(CEim), in1=bcc(PWim), op=ALU.mult))
        dv(lambda e: e.tensor_tensor(out=CAre[:], in0=CAre[:], in1=TMPB[:], op=ALU.subtract))
        dv(lambda e: e.tensor_tensor(out=CAim[:], in0=bck(CEre), in1=bcc(PWim), op=ALU.mult))
        dv(lambda e: e.tensor_tensor(out=TMPB[:], in0=bck(CEim), in1=bcc(PWre), op=ALU.mult))
        dv(lambda e: e.tensor_tensor(out=CAim[:], in0=CAim[:], in1=TMPB[:], op=ALU.add))
        FFv = FFd.rearrange("p (t r x) -> p t r x", t=16, r=2)
        ffb = p0
        for ri in range(2):
            src = (CAre if ri == 0 else CAim)[:, 1:17, :, :].rearrange("p t g c -> p t (g c)")
            S.op("act", lambda e, src=src, ri=ri: e.activation(out=FFs[:], in_=src, func=AF.Copy, scale=(1.0 if ri == 0 else -1.0)),
                 reads=[p0], writes=[ffb])
            S.dma("sp", FFv[:, :, ri, :], FFs[:], reads=[ffb], writes=[p0], sem_buf=ffb)
        def bcs(t):
            return t[:].unsqueeze(1).to_broadcast([128, 16, 16, 32])

        def bcp(t):
            return t[:, 0:16, :].unsqueeze(3).to_broadcast([128, 16, 16, 32])

        TM16 = TMPB[:, 0:16, :, :]
        for ri in range(2):
            if ri == 0:
                dv(lambda e: e.tensor_tensor(out=SWT[:], in0=bcs(Bbre), in1=bcp(PWre), op=ALU.mult))
                dv(lambda e: e.tensor_tensor(out=TM16, in0=bcs(Bbim), in1=bcp(PWim), op=ALU.mult))
                dv(lambda e: e.tensor_tensor(out=SWT[:], in0=SWT[:], in1=TM16, op=ALU.subtract))
            else:
                dv(lambda e: e.tensor_tensor(out=SWT[:], in0=bcs(Bbre), in1=bcp(PWim), op=ALU.mult))
                dv(lambda e: e.tensor_tensor(out=TM16, in0=bcs(Bbim), in1=bcp(PWre), op=ALU.mult))
                dv(lambda e: e.tensor_tensor(out=SWT[:], in0=SWT[:], in1=TM16, op=ALU.add))
            for kk in range(16):
                pb = kk % 2 + 1
                for k in range(4):
                    S.op("pe", lambda e, kk=kk, k=k, pb=pb: e.transpose(
                        out=PS[pb][:, 128 * k:128 * k + 128],
                        in_=SWT[:, kk, 4 * k:4 * k + 4, :].rearrange("p g c -> p (g c)"), identity=ident[:]),
                        reads=[p0, cbuf], writes=[PSB[pb]], sig=(k == 3))
                S.op("act", lambda e, kk=kk, ri=ri, pb=pb: e.activation(
                    out=SW[:, kk, ri, :, :].rearrange("p k c -> p (k c)"), in_=PS[pb][:], func=AF.Copy),
                    reads=[PSB[pb]], writes=[swb])
        dv(lambda e: e.memset(BZre[:], 0.0))
        dv(lambda e: e.memset(BZim[:], 0.0))
        for q in range(4):
            dv(lambda e, q=q: e.tensor_copy(
                out=BZre[:].rearrange("p (k q) c -> p k q c", q=4)[:, :, q, 32 * q:32 * q + 32],
                in_=Bbre[:].rearrange("p (k q) c -> p k q c", q=4)[:, :, q, :]))
            dv(lambda e, q=q: e.tensor_scalar_mul(
                out=BZim[:].rearrange("p (k q) c -> p k q c", q=4)[:, :, q, 32 * q:32 * q + 32],
                in0=Bbim[:].rearrange("p (k q) c -> p k q c", q=4)[:, :, q, :], scalar1=-1.0))
        TZs = self.sb("TZs", [128, 16, 512], BF16, 162 * KB)
        tzb = p0
        TZv = TZd.rearrange("p (l x) -> p l x", l=16)
        for lag in range(16):
            pb = 3 + lag % 2
            for k in range(4):
                for q in range(4):
                    gp = 4 * k + q
                    osl = PS[pb][:, 128 * k + 32 * q:128 * k + 32 * q + 32]
                    S.op("pe", lambda e, osl=osl, gp=gp, lag=lag: e.matmul(
                        osl, lhsT=BZre[:, gp, :], rhs=CAre[:, lag, gp, :], start=True, stop=False, skip_group_check=True),
                        reads=[p0], writes=[PSB[pb]], sig=False)
                    S.op("pe", lambda e, osl=osl, gp=gp, lag=lag: e.matmul(
                        osl, lhsT=BZim[:, gp, :], rhs=CAim[:, lag, gp, :], start=False, stop=True, skip_group_check=True),
                        reads=[p0], writes=[PSB[pb]], sig=(k == 3 and q == 3))
            S.op("act", lambda e, lag=lag, pb=pb: e.activation(out=TZs[:, lag, :], in_=PS[pb][:], func=AF.Copy),
                 reads=[PSB[pb], p0], writes=[tzb])
        S.dma("sp", TZv, TZs[:], reads=[tzb], writes=[p0], sem_buf=tzb)
        S.barrier()

        Wu = self.sb("Wu", [128, 16, 512], BF16, 62 * KB)
        wub = S.buf("Wu")
        xt = [self.sb("xt%d" % i, [128, D], F32, (78 + 8 * i) * KB) for i in range(2)]
        xtb = S.bufs(2, "xt")
        xTs2 = [self.sb("xTs0", [128, 16, 128], F32, 94 * KB)] * 2
        xTb2 = [S.buf("xTs")] * 2
        sq2 = [self.sb("sq0", [128, 16, 128], BF16, 102 * KB), self.sb("sq1", [128, 16, 128], BF16, 196 * KB)]
        sqb2 = S.bufs(2, "sq")
        hTs = [self.sb("hTs%d" % i, [128, 16, 512], BF16, (106 + 16 * i) * KB) for i in range(2)]
        hTb = S.bufs(2, "hTs")
        ub = self.sb("ubatch", [128, 4, 1024], BF16, 138 * KB)
        ubb = S.buf("ubatch")
        Ssb2 = [self.sb("Ssb%d" % i, [128, 64, 2, 16], BF16, (146 + 4 * i) * KB) for i in range(2)]
        Ssbb2 = S.bufs(2, "Ssb")
        s4q = [self.sb("s4q%d" % i, [128, 2, 4, 256], BF16, (20 + 4 * i) * KB) for i in range(2)] + \
              [self.sb("s4q%d" % (2 + i), [128, 2, 4, 256], BF16, (12 + 4 * i) * KB) for i in range(2)]
        s4qb = S.bufs(4, "s4q")
        Yh = self.sb("Yh", [128, 2, 4, 64], F32, 28 * KB)
        HT4 = [self.sb("HT4_%d" % i, [128, 256], F32, (200 + i) * KB) for i in range(4)]
        Zhs = [self.sb("Zh0", [128, 2, 16, 16], F32, 204 * KB), self.sb("Zh1", [128, 2, 16, 16], F32, 8 * KB)]
        zhb = S.bufs(2, "Zh")
        hb = S.buf("horner")

        def cma(eng, o_re, o_im, x_re, x_im, a_re, a_im, s_re, s_im, T, rb, wb):
            t1, t2, t3, t4 = T
            f = lambda fn, r=rb, w=wb: S.op(eng, fn, reads=list(r), writes=list(w))
            f(lambda e: e.tensor_tensor(out=t1, in0=a_re, in1=x_re, op=ALU.mult))
            f(lambda e: e.tensor_tensor(out=t2, in0=a_im, in1=x_im, op=ALU.mult))
            f(lambda e: e.tensor_tensor(out=t3, in0=a_re, in1=x_im, op=ALU.mult))
            f(lambda e: e.tensor_tensor(out=t4, in0=a_im, in1=x_re, op=ALU.mult))
            f(lambda e: e.tensor_tensor(out=t1, in0=t1, in1=t2, op=ALU.subtract))
            f(lambda e: e.tensor_tensor(out=t3, in0=t3, in1=t4, op=ALU.add))
            f(lambda e: e.tensor_tensor(out=o_re, in0=t1, in1=s_re, op=ALU.add))
            f(lambda e: e.tensor_tensor(out=o_im, in0=t3, in1=s_im, op=ALU.add))
        um = self.sb("u_main", [128, 4, NMAIN], BF16, 154 * KB)
        umb = S.buf("u_main")
        Xs = self.sb("Xs", [128, 2, 16, NCHM], BF16, 172 * KB)
        Xsb = S.buf("Xs")
        rsA = [self.sb("rsA%d" % i, [128, 128], F32, 181 * KB + 1024 * i) for i in range(2)]
        rsB = [self.sb("rsB%d" % i, [128, 128], F32, 181 * KB + 1024 * i + 512) for i in range(2)]
        rsb2 = S.bufs(2, "rs")
        xtmp = self.sb("xtmp", [128, 16, 128], F32, 184 * KB)
        xtmpb = S.buf("xtmp")
        mk = [self.sb("mk%d" % i, [128, 512], F32, (192 + 2 * i) * KB) for i in range(2)]
        mkb = S.bufs(2, "mk")
        S.dma("pool", Wu[:], w_in[:, 1280:1792].rearrange("(kt p) c -> p kt c", p=128), writes=[wub], sem_buf=wub)
        stb = S.buf("state")
        S.op("pool", lambda e: e.memset(X4[0][:], 0.0), writes=[stb])
        S.op("pool", lambda e: e.memset(X4[1][:], 0.0), writes=[stb])
        cur = 0
        XTv = XT.rearrange("k p c -> p k c")
        HTv = HT.rearrange("k p c -> p k c")
        deferred = {}

        def ph1_load(b_):
            S.dma("sp", xt[b_ % 2][:], xw[b_ * 128:(b_ + 1) * 128, :], writes=[xtb[b_ % 2]], sem_buf=xtb[b_ % 2])

        junk = [self.sb("junk0", [128, D], BF16, 102 * KB)] * 2
        junkb = [S.buf("junk")] * 2
        HT4b = [self.sb("HT4b_%d" % i, [128, 256], F32, (196 + i) * KB) for i in range(4)]
        Yhb = self.sb("Yhb", [128, 2, 4, 64], F32, 10 * KB)
        hb2 = S.buf("horner2")
        xsb = [self.sb("xsb%d" % i, [128, D], BF16, (184 + 4 * i) * KB) for i in range(2)]
        xsbb = S.bufs(2, "xsb")
        ssqs = [self.sb("ssq%d" % i, [128, 1], F32, 183 * KB + 64 * i) for i in range(2)]
        ssqb = S.bufs(2, "ssq")

        def ph1_sq(b_):
            ii = b_ % 2
            S.op("act", lambda e, ii=ii: e.activation(out=junk[ii][:], in_=xt[ii][:], func=AF.Square, accum_out=ssqs[ii][:]),
                 reads=[xtb[ii]], writes=[junkb[ii], ssqb[ii]])

        def ph1_transposes(b_):
            ii = b_ % 2
            for j in range(4):
                for kk in range(4):
                    kt = 4 * j + kk
                    S.op("pe", lambda e, j=j, kk=kk, kt=kt, ii=ii: e.transpose(
                        out=PS[j][:, 128 * kk:128 * kk + 128], in_=xt[ii][:, 128 * kt:128 * kt + 128], identity=ident[:]),
                        reads=[xtb[ii], cbuf], writes=[PSB[j]], sig=(kk == 3))

        for blk in range(NBLK):
            i2 = blk % 2
            sub = (blk // 4) % 2
            tcol = (blk % 4) * 128
            xTs, xTb = xTs2[i2], xTb2[i2]
            rs1, rs2, rsb = rsA[i2], rsB[i2], rsb2[i2]
            if blk == 0:
                ph1_load(0)
                ph1_sq(0)
            if blk % 4 == 0:
                mi = (blk // 4) % 2
                S.dma("sp", mk[mi][:], maskw[blk * 128:blk * 128 + 512].partition_broadcast(128), writes=[mkb[mi]],
                      sem_buf=mkb[mi], nonc=True)
            if blk + 1 < NBLK:
                ph1_load(blk + 1)
            S.op("dve", lambda e, rs1=rs1, i2=i2: e.tensor_scalar(out=rs1[:, 0:1], in0=ssqs[i2][:], scalar1=1.0 / D, scalar2=1e-6,
                                                                   op0=ALU.mult, op1=ALU.add), reads=[ssqb[i2]], writes=[rsb])
            S.op("act", lambda e, rs1=rs1, rs2=rs2: e.activation(out=rs2[:, 0:1], in_=rs1[:, 0:1], func=AF.Sqrt), reads=[rsb], writes=[rsb])
            S.op("dve", lambda e, rs1=rs1, rs2=rs2: e.reciprocal(out=rs1[:, 0:1], in_=rs2[:, 0:1]), reads=[rsb], writes=[rsb])
            S.op("act", lambda e, rs1=rs1, i2=i2: e.activation(out=xsb[i2][:], in_=xt[i2][:], func=AF.Copy, scale=rs1[:, 0:1]),
                 reads=[xtb[i2], rsb], writes=[xsbb[i2]])
            if blk + 1 < NBLK:
                ph1_sq(blk + 1)
            if blk >= MB0:
                for j in range(4):
                    pbk = 2 + j % 2
                    for kk in range(4):
                        kt = 4 * j + kk
                        S.op("pe", lambda e, pbk=pbk, kk=kk, kt=kt, i2=i2: e.transpose(
                            out=PS[pbk][:, 128 * kk:128 * kk + 128], in_=xt[i2][:, 128 * kt:128 * kt + 128], identity=ident[:]),
                            reads=[xtb[i2], cbuf], writes=[PSB[pbk]], sig=(kk == 3))
                    S.op("act", lambda e, j=j, pbk=pbk, xTs=xTs: e.activation(
                        out=xTs[:, 4 * j:4 * j + 4, :].rearrange("p k c -> p (k c)"), in_=PS[pbk][:], func=AF.Copy),
                        reads=[PSB[pbk]], writes=[xTb])
                col = (blk - MB0) * 128
                S.dma("sp", XTv[:, :, col:col + 128], xTs[:], reads=[xTb], writes=[], sem_buf=xTb)
            for j in range(2):
                psv = PS[j][:].bitcast(BF16)
                for kk in range(8):
                    kt = 8 * j + kk
                    S.op("pe", lambda e, psv=psv, kk=kk, kt=kt, i2=i2: e.transpose(
                        out=psv[:, 128 * kk:128 * kk + 128], in_=xsb[i2][:, 128 * kt:128 * kt + 128], identity=identb[:]),
                        reads=[xsbb[i2], cbuf2], writes=[PSB[j]], sig=(kk == 7))
                S.op("dve", lambda e, psv=psv, j=j, sub=sub, tcol=tcol: e.tensor_tensor(
                    out=hTs[sub][:, 8 * j:8 * j + 8, tcol:tcol + 128], in0=psv[:, 0:1024].rearrange("p (k c) -> p k c", k=8),
                    in1=g1c[:, 8 * j:8 * j + 8].unsqueeze(2).to_broadcast([128, 8, 128]), op=ALU.mult),
                    reads=[PSB[j], cbuf], writes=[hTb[sub]])
            for fn_ in deferred.pop(blk, []):
                fn_()
            if blk >= MB0:
                S.dma("sp", HTv[:, :, col:col + 128], hTs[sub][:, :, tcol:tcol + 128], reads=[hTb[sub]], writes=[],
                      sem_buf=hTb[sub])
            if blk % 4 == 3:
                mi = (blk // 4) % 2
                boff = ((blk // 4) % 2) * 512
                for m in range(4):
                    pb = 5
                    for kt in range(16):
                        S.op("pe", lambda e, m=m, kt=kt, sub=sub, pb=pb: e.matmul(
                            PS[pb][:], lhsT=Wu[:, kt, 128 * m:128 * m + 128], rhs=hTs[sub][:, kt, :],
                            start=(kt == 0), stop=(kt == 15)),
                            reads=[wub, hTb[sub]], writes=[PSB[pb]], sig=(kt == 15))
                    S.op("dve", lambda e, m=m, pb=pb, mi=mi, boff=boff: e.scalar_tensor_tensor(
                        out=ub[:, m, boff:boff + 512], in0=PS[pb][:], scalar=binc[:, 10 + m:11 + m], in1=mk[mi][:],
                        op0=ALU.add, op1=ALU.mult), reads=[PSB[pb], mkb[mi], cbuf], writes=[ubb])
                    if blk >= MB0 - 2:
                        b0 = blk - 3
                        lo = max(b0, MB0)
                        n = (blk + 1 - lo) * 128
                        so = boff + (lo - b0) * 128
                        do = (lo - MB0) * 128
                        S.op("act", lambda e, m=m, so=so, do=do, n=n: e.activation(
                            out=um[:, m, do:do + n], in_=ub[:, m, so:so + n], func=AF.Copy), reads=[ubb], writes=[umb])
            if blk % 8 == 7:
                bi = blk // 8
                Ssb, Ssbb = Ssb2[bi % 2], Ssbb2[bi % 2]
                Zh, zb = Zhs[bi % 2], zhb[bi % 2]
                def level01(q, bi=bi, Ssb=Ssb, Ssbb=Ssbb):
                    s4, s4b = s4q[q], s4qb[q]
                    for k in range(4):
                        pq = 6 + k % 2
                        for ri in range(2):
                            for b4 in range(4):
                                S.op("pe", lambda e, q=q, k=k, ri=ri, b4=b4, pq=pq: e.matmul(
                                    PS[pq][:, 256 * ri:256 * ri + 256], lhsT=SW[32 * q:32 * q + 32, 3 - b4, ri, k, :],
                                    rhs=ub[32 * q:32 * q + 32, k, b4:1024:4], start=(b4 == 0), stop=(b4 == 3),
                                    tile_position=(32 * q, 0), skip_group_check=True),
                                    reads=[swb, ubb], writes=[PSB[pq]], sig=(ri == 1 and b4 == 3))
                        S.op("act", lambda e, k=k, pq=pq, s4=s4: e.activation(
                            out=s4[:, :, k, :], in_=PS[pq][:].rearrange("p (r c) -> p r c", r=2), func=AF.Copy),
                            reads=[PSB[pq]], writes=[s4b])
                    s4v = s4[:].rearrange("p r k (c a) -> p r k c a", a=4)
                    a4r = A4re[:].rearrange("p (k q) -> p q k", q=4)[:, q, :].unsqueeze(2).to_broadcast([128, 4, 64])
                    a4i = A4im[:].rearrange("p (k q) -> p q k", q=4)[:, q, :].unsqueeze(2).to_broadcast([128, 4, 64])
                    onp = True
                    en_ = "pool" if onp else "dve"
                    Y_ = Yh if onp else Yhb
                    hb_ = hb if onp else hb2
                    tq = [t[:].rearrange("p (k c) -> p k c", k=4) for t in (HT4 if onp else HT4b)]
                    Sq = Ssb[:].rearrange("p c r (k q) -> p r q k c", q=4)
                    cma(en_, Y_[:, 0], Y_[:, 1], s4v[:, 0, :, :, 0], s4v[:, 1, :, :, 0], a4r, a4i,
                        s4v[:, 0, :, :, 1], s4v[:, 1, :, :, 1], tq, [s4b, hb_, cbuf], [hb_])
                    cma(en_, Y_[:, 0], Y_[:, 1], Y_[:, 0], Y_[:, 1], a4r, a4i,
                        s4v[:, 0, :, :, 2], s4v[:, 1, :, :, 2], tq, [s4b, hb_, cbuf], [hb_])
                    cma(en_, Sq[:, 0, q], Sq[:, 1, q], Y_[:, 0], Y_[:, 1], a4r, a4i,
                        s4v[:, 0, :, :, 3], s4v[:, 1, :, :, 3], tq, [s4b, hb_, cbuf], [hb_, Ssbb])

                def batch_tail(bi=bi, Ssb=Ssb, Ssbb=Ssbb, Zh=Zh, zb=zb):
                    nonlocal cur
                    if bi < 5:
                        Sv = Ssb[:].rearrange("p (m a) r g -> p a r m g", a=4)
                        a16r = A16re[:].unsqueeze(1).to_broadcast([128, 16, 16])
                        a16i = A16im[:].unsqueeze(1).to_broadcast([128, 16, 16])
                        tz = [t[:].rearrange("p (m g) -> p m g", m=16) for t in HT4]
                        cma("pool", Zh[:, 0], Zh[:, 1], Sv[:, 0, 0], Sv[:, 0, 1], a16r, a16i, Sv[:, 1, 0], Sv[:, 1, 1], tz,
                            [Ssbb, hb, cbuf], [hb, zb])
                        cma("pool", Zh[:, 0], Zh[:, 1], Zh[:, 0], Zh[:, 1], a16r, a16i, Sv[:, 2, 0], Sv[:, 2, 1], tz,
                            [Ssbb, hb, cbuf, zb], [hb, zb])
                        cma("pool", Zh[:, 0], Zh[:, 1], Zh[:, 0], Zh[:, 1], a16r, a16i, Sv[:, 3, 0], Sv[:, 3, 1], tz,
                            [Ssbb, hb, cbuf, zb], [hb, zb])
                        for m in range(16):
                            xa, xb = X4[cur], X4[1 - cur]
                            pls = lambda fn, r=(stb, cbuf), w=(stb,): S.op("pool", fn, reads=list(r), writes=list(w))
                            pls(lambda e, xa=xa: e.tensor_tensor(out=P1[:], in0=AA64[:], in1=xa[:, 0:2, :], op=ALU.mult))
                            pls(lambda e, xa=xa: e.tensor_tensor(out=P2[:], in0=BB64[:], in1=xa[:, 1:3, :], op=ALU.mult))
                            pls(lambda e: e.tensor_tensor(out=P1[:], in0=P1[:], in1=P2[:], op=ALU.add))
                            S.op("pool", lambda e, xb=xb, m=m, Zh=Zh: e.tensor_tensor(out=xb[:, 0:2, :], in0=P1[:], in1=Zh[:, :, m, :], op=ALU.add),
                                 reads=[stb, zb], writes=[stb])
                            pls(lambda e, xb=xb: e.tensor_copy(out=xb[:, 2, :], in_=xb[:, 0, :]))
                            cur = 1 - cur
                    else:
                        jb = bi * 64
                        for jj in range(64):
                            j = jb + jj
                            xa, xb = X4[cur], X4[1 - cur]
                            if j >= NCH0:
                                S.op("pool", lambda e, xa=xa, j=j: e.tensor_copy(out=Xs[:, :, :, j - NCH0], in_=xa[:, 0:2, :]),
                                     reads=[stb], writes=[Xsb])
                            pls = lambda fn, r=(stb, cbuf), w=(stb,): S.op("pool", fn, reads=list(r), writes=list(w))
                            pls(lambda e, xa=xa: e.tensor_tensor(out=P1[:], in0=AAt[:], in1=xa[:, 0:2, :], op=ALU.mult))
                            pls(lambda e, xa=xa: e.tensor_tensor(out=P2[:], in0=BBt[:], in1=xa[:, 1:3, :], op=ALU.mult))
                            pls(lambda e: e.tensor_tensor(out=P1[:], in0=P1[:], in1=P2[:], op=ALU.add))
                            S.op("pool", lambda e, xb=xb, jj=jj, Ssb=Ssb: e.tensor_tensor(out=xb[:, 0:2, :], in0=P1[:], in1=Ssb[:, jj, :, :], op=ALU.add),
                                 reads=[stb, Ssbb], writes=[stb])
                            pls(lambda e, xb=xb: e.tensor_copy(out=xb[:, 2, :], in_=xb[:, 0, :]))
                            cur = 1 - cur
                level01(0)
                level01(1)
                if blk + 2 < NBLK:
                    deferred.setdefault(blk + 1, []).append(lambda f=level01: f(2))
                    deferred.setdefault(blk + 2, []).append(lambda f=level01: f(3))
                    deferred.setdefault(blk + 2, []).append(batch_tail)
                else:
                    level01(2)
                    level01(3)
                    batch_tail()
        if UM is not None:
            S.dma("sp", UM.rearrange("k p c -> p k c"), um[:], reads=[umb], writes=[], sem_buf=umb)
        S.barrier()
        finals = []
        if UM is not None:
            finals.append(umb)
        if self.stop == "ph1":
            return S, finals
        FF = self.sb("FF", [128, 16, 2, 16, 32], BF16, 62 * KB)
        TZ = self.sb("TZ", [128, 16, 4, 128], BF16, 94 * KB)
        ffb2 = S.buf("FF")
        tzb2 = S.buf("TZ")
        S.dma("sp", FF[:].rearrange("p t r g c -> p (t r g c)"), FFd, writes=[ffb2], sem_buf=ffb2)
        S.dma("sp", TZ[:].rearrange("p l k c -> p (l k c)"), TZd, writes=[tzb2], sem_buf=tzb2)
        yg = self.sb("yg", [128, 4, NMAIN], BF16, 12 * KB)
        ygb = S.buf("yg")
        ytmp = [self.sb("ytmp%d" % i, [128, NCHM], F32, 110 * KB + i * 1024) for i in range(2)]
        ytb = S.bufs(2, "ytmp")
        it = 0
        for k in range(4):
            for tau in range(16):
                pb = it % 4
                yi = it % 2
                it += 1
                for s in range(tau + 1):
                    S.op("pe", lambda e, pb=pb, tau=tau, s=s, k=k: e.matmul(
                        PS[pb][:, 0:NCHM], lhsT=TZ[:, tau - s, k, :], rhs=um[:, k, s:NMAIN:16],
                        start=(s == 0), stop=False, skip_group_check=True),
                        reads=[tzb2, umb], writes=[PSB[pb]], sig=False)
                for q in range(4):
                    for ri in range(2):
                        last = (q == 3 and ri == 1)
                        S.op("pe", lambda e, pb=pb, tau=tau, q=q, ri=ri, k=k, last=last: e.matmul(
                            PS[pb][32 * q:32 * q + 32, 0:NCHM], lhsT=FF[:, tau, ri, 4 * k + q, :], rhs=Xs[:, ri, 4 * k + q, :],
                            start=False, stop=last, tile_position=(0, 32 * q), skip_group_check=True),
                            reads=[ffb2, Xsb], writes=[PSB[pb]], sig=last)
                S.op("dve", lambda e, pb=pb, tau=tau, k=k, yi=yi: e.scalar_tensor_tensor(
                    out=ytmp[yi][:], in0=um[:, k, tau:NMAIN:16], scalar=dcol[:, k:k + 1], in1=PS[pb][:, 0:NCHM],
                    op0=ALU.mult, op1=ALU.add), reads=[PSB[pb], umb, cbuf], writes=[ytb[yi]])
                S.op("act", lambda e, tau=tau, k=k, yi=yi: e.activation(
                    out=yg[:, k, tau:NMAIN:16], in_=ytmp[yi][:], func=AF.Gelu), reads=[ytb[yi]], writes=[ygb])
        if YG is not None:
            S.dma("sp", YG.rearrange("k p c -> p k c"), yg[:], reads=[ygb], writes=[], sem_buf=ygb)
            finals.append(ygb)
        S.barrier()
        if self.stop == "ph1b":
            return S, finals
        self.pbn = 0

        def nb():
            self.pbn = (self.pbn + 1) % 8
            return self.pbn

        MT = [(0, 512), (512, 512), (1024, 512), (1536, 512), (2048, 256)]
        VT = [(128, 512), (640, 512), (1152, 512), (1664, 512), (2176, 128)]

        def loadw(eng, dst, dbuf, src, nkt):
            S.dma(eng, dst[:, 0:nkt, :], src.rearrange("(kt p) c -> p kt c", p=128), writes=[dbuf], sem_buf=dbuf)

        SG = self.dscr("SG", [32, 128, NMAIN], BF16)
        X1T = self.dscr("X1T", [16, 128, NMAIN], F32)
        ACTd = self.dscr("ACTd", [44, 128, 2048], BF16)
        X2T = self.dscr("X2T", [16, 128, 2048], F32)
        qT = self.sb("qT", [128, 8, NMAIN], BF16, 30 * KB)
        qb = S.buf("qT")
        kd = [self.sb("kd%d" % g, [128, NMAIN], BF16, 66 * KB + g * 4608) for g in range(2)]
        kdb = S.bufs(2, "kd")
        hTm = self.sb("hTm", [128, 16, NMAIN], BF16, 75 * KB)
        hTmb = S.buf("hTm")
        wt = [self.sb("wt%d" % i, [128, 16, 128], BF16, (147 + 4 * i) * KB) for i in range(2)]
        wtb = S.bufs(2, "wt")
        vsb = self.sb("vsb", [128, 18, 128], BF16, 155 * KB)
        vsbb = S.buf("vsb")
        gst = [self.sb("gst%d" % i, [128, 512], BF16, (160 + i) * KB) for i in range(2)]
        gstb = S.bufs(2, "gst")
        bvb = self.sb("bvb", [128, 128], F32, 162 * KB)
        bkd = self.sb("bkd", [128, 2], F32, 163 * KB)
        sinkc = self.sb("sinkc", [128, 16], F32, 163 * KB + 64)
        NM = self.sb("NM", [128, 128], F32, 164 * KB)
        c2b = S.buf("c2")
        S.dma("sp", hTm[:], HT.rearrange("k p c -> p k c"), writes=[hTmb], sem_buf=hTmb)
        S.dma("sp", bvb[:], b_in[1152:1280].partition_broadcast(128), writes=[c2b], sem_buf=c2b, nonc=True)
        for g in range(2):
            for hh in range(2):
                S.dma("sp", bkd[64 * hh:64 * hh + 64, g:g + 1], b_in[1024 + 64 * g:1088 + 64 * g].rearrange("(p o) -> p o", o=1),
                      writes=[c2b], sem_buf=c2b, nonc=True)
        S.dma("sp", sinkc[:], sinks.partition_broadcast(128), writes=[c2b], sem_buf=c2b, nonc=True)
        S.dma("sp", NM[:], maskw[MB0 * 128 + 128:MB0 * 128 + 256].partition_broadcast(128), writes=[c2b], sem_buf=c2b, nonc=True)
        S.op("dve", lambda e: e.tensor_scalar(out=NM[:], in0=NM[:], scalar1=-1.0, scalar2=30000.0, op0=ALU.add, op1=ALU.mult),
             reads=[c2b], writes=[c2b])
        wi = 0
        jobs = [("q", cb) for cb in range(8)] + [("k", g) for g in range(2)] + [("g", cb) for cb in range(14, 46)]
        gi = 0
        for kind, cb in jobs:
            w = wt[wi % 2]
            wb = wtb[wi % 2]
            wi += 1
            if kind == "k":
                for hh in range(2):
                    S.dma("pool", w[:, :, 64 * hh:64 * hh + 64],
                          w_in[:, 1024 + 64 * cb:1088 + 64 * cb].rearrange("(kt p) c -> p kt c", p=128), writes=[wb], sem_buf=wb)
            else:
                loadw("pool", w, wb, w_in[:, 128 * cb:128 * cb + 128], 16)
            for (c0, n) in (MT if kind == "k" else VT):
                pb = nb()
                for kt in range(16):
                    S.op("pe", lambda e, pb=pb, w=w, kt=kt, c0=c0, n=n: e.matmul(
                        PS[pb][:, 0:n], lhsT=w[:, kt, :], rhs=hTm[:, kt, c0:c0 + n], start=(kt == 0), stop=(kt == 15)),
                        reads=[wb, hTmb], writes=[PSB[pb]], sig=(kt == 15))
                if kind == "q":
                    S.op("act", lambda e, pb=pb, cb=cb, c0=c0, n=n: e.activation(
                        out=qT[:, cb, c0:c0 + n], in_=PS[pb][:, 0:n], func=AF.Identity, bias=binc[:, cb:cb + 1]),
                        reads=[PSB[pb], cbuf], writes=[qb])
                elif kind == "k":
                    S.op("act", lambda e, pb=pb, cb=cb, c0=c0, n=n: e.activation(
                        out=kd[cb][:, c0:c0 + n], in_=PS[pb][:, 0:n], func=AF.Identity, bias=bkd[:, cb:cb + 1]),
                        reads=[PSB[pb], c2b], writes=[kdb[cb]])
                else:
                    gs_ = gst[gi % 2]
                    gsb = gstb[gi % 2]
                    gi += 1
                    S.op("act", lambda e, pb=pb, cb=cb, n=n, gs_=gs_: e.activation(
                        out=gs_[:, 0:n], in_=PS[pb][:, 0:n], func=AF.Sigmoid, bias=binc[:, cb:cb + 1]),
                        reads=[PSB[pb], cbuf], writes=[gsb])
                    S.dma("sp", SG[cb - 14][:, c0:c0 + n], gs_[:, 0:n], reads=[gsb], writes=[], sem_buf=gsb)
        w = wt[wi % 2]
        wb = wtb[wi % 2]
        wi += 1
        loadw("pool", w, wb, w_in[:, 1152:1280], 16)
        for mb in range(18):
            pb = nb()
            for kt in range(16):
                S.op("pe", lambda e, pb=pb, w=w, kt=kt, mb=mb: e.matmul(
                    PS[pb][:, 0:128], lhsT=hTm[:, kt, 128 * mb:128 * mb + 128], rhs=w[:, kt, :], start=(kt == 0), stop=(kt == 15)),
                    reads=[wb, hTmb], writes=[PSB[pb]], sig=(kt == 15))
            S.op("dve", lambda e, pb=pb, mb=mb: e.tensor_tensor(out=vsb[:, mb, :], in0=PS[pb][:, 0:128], in1=bvb[:], op=ALU.add),
                 reads=[PSB[pb], c2b], writes=[vsbb])
        S.barrier()
        if self.stop == "ph2":
            return S, finals
        attnT = self.sb("attnT", [128, 8, NMAIN], BF16, 75 * KB)
        atb = S.buf("attnT")
        atbs = S.bufs(8, "attnTk")
        AB = self.sb("AB", [128, 16, 256], F32, 111 * KB)
        abb = S.buf("AB")
        scS = [self.sb("sc0", [128, 16, 256], F32, 127 * KB), self.sb("sc1", [128, 16, 256], F32, 174 * KB)]
        scbS = [S.bufs(16, "sc0_"), S.bufs(16, "sc1_")]
        pnS = [self.sb("pn0", [128, 16, 256], BF16, 143 * KB), self.sb("pn1", [128, 16, 256], BF16, 190 * KB)]
        pnbS = [S.bufs(16, "pn0_"), S.bufs(16, "pn1_")]
        pTsS = [self.sb("pTs0", [128, 16, 256], BF16, 165 * KB), self.sb("pTs1", [128, 16, 256], BF16, 198 * KB)]
        pTbS = [S.bufs(16, "pT0_"), S.bufs(16, "pT1_")]
        vecS = []
        for i in range(2):
            vo = (173 if i == 0 else 151) * KB
            vecS.append([self.sb("av%d_%d" % (i, t), [128, 16], F32, vo + 64 * t) for t in range(5)])
        vbS = S.bufs(2, "attvec")
        S.dma("sp", AB[:], abias, writes=[abb], sem_buf=abb)
        def att_setup(mb):
            c0 = 128 * mb
            return (c0,) + (scS[mb % 2], scbS[mb % 2], pnS[mb % 2], pnbS[mb % 2], pTsS[mb % 2], pTbS[mb % 2]) + tuple(vecS[mb % 2]) + (vbS[mb % 2],)

        def stageA(mb):
            c0, sc, scb, pn, pnb, pTs, pTb, mx, nmx, rsum, es, rden, vb_ = att_setup(mb)
            for base in (0, 4, 8, 12):
                for par in (0, 1):
                    h0 = base + par
                    g = h0 // 8
                    r0 = 64 * par
                    pb = nb()
                    for i_, h in enumerate((h0, h0 + 2)):
                        S.op("pe", lambda e, pb=pb, h=h, g=g, r0=r0, i_=i_: e.matmul(
                            PS[pb][:, 256 * i_:256 * i_ + 256], lhsT=qT[r0:r0 + 64, h // 2, c0:c0 + 128],
                            rhs=kd[g][r0:r0 + 64, c0 - 128:c0 + 128], start=True, stop=True, skip_group_check=True),
                            reads=[qb, kdb[g]], writes=[PSB[pb]], sig=(i_ == 1))
                    S.op("dve", lambda e, pb=pb, h0=h0: e.scalar_tensor_tensor(
                        out=sc[:, h0:h0 + 3:2, :], in0=PS[pb][:].rearrange("p (h c) -> p h c", h=2), scalar=0.125,
                        in1=AB[:, h0:h0 + 3:2, :], op0=ALU.mult, op1=ALU.add),
                        reads=[PSB[pb], abb], writes=[scb[h0], scb[h0 + 2]])
                    if mb == 2:
                        for h in (h0, h0 + 2):
                            S.op("dve", lambda e, h=h: e.tensor_tensor(out=sc[:, h, 0:128], in0=sc[:, h, 0:128], in1=NM[:], op=ALU.add),
                                 reads=[scb[h], c2b], writes=[scb[h]])
                for h in (base + 1, base + 3):
                    S.op("dve", lambda e, h=h: e.reduce_max(out=mx[:, h - 1:h + 1], in_=sc[:, h - 1:h + 1, :],
                                                             axis=mybir.AxisListType.X),
                         reads=[scb[h - 1], scb[h]], writes=[vb_])
            S.op("dve", lambda e, mx=mx: e.tensor_tensor(out=mx[:], in0=mx[:], in1=sinkc[:], op=ALU.max), reads=[vb_, c2b], writes=[vb_])
            S.op("dve", lambda e, mx=mx, nmx=nmx: e.tensor_scalar_mul(out=nmx[:], in0=mx[:], scalar1=-1.0), reads=[vb_], writes=[vb_])
            S.op("dve", lambda e, es=es, nmx=nmx: e.tensor_tensor(out=es[:], in0=sinkc[:], in1=nmx[:], op=ALU.add), reads=[vb_, c2b], writes=[vb_])
            S.op("act", lambda e, es=es: e.activation(out=es[:], in_=es[:], func=AF.Exp), reads=[vb_], writes=[vb_])
            for h in range(16):
                S.op("act", lambda e, h=h, sc=sc, nmx=nmx, rsum=rsum: e.activation(out=sc[:, h, :], in_=sc[:, h, :], func=AF.Exp, bias=nmx[:, h:h + 1],
                                                          accum_out=rsum[:, h:h + 1]), reads=[scb[h], vb_], writes=[scb[h], vb_])

        def stageB(mb):
            c0, sc, scb, pn, pnb, pTs, pTb, mx, nmx, rsum, es, rden, vb_ = att_setup(mb)
            S.op("dve", lambda e, rden=rden, rsum=rsum, es=es: e.tensor_tensor(out=rden[:], in0=rsum[:], in1=es[:], op=ALU.add), reads=[vb_], writes=[vb_])
            S.op("dve", lambda e, rden=rden: e.reciprocal(out=rden[:], in_=rden[:]), reads=[vb_], writes=[vb_])
            banks = {}

            def b_first(hp):
                pb = nb()
                banks[hp] = pb
                psv = PS[pb][:].bitcast(BF16)
                S.op("dve", lambda e, hp=hp: e.tensor_tensor(
                    out=pn[:, 2 * hp:2 * hp + 2, :], in0=sc[:, 2 * hp:2 * hp + 2, :],
                    in1=rden[:, 2 * hp:2 * hp + 2].unsqueeze(2).to_broadcast([128, 2, 256]), op=ALU.mult),
                    reads=[scb[2 * hp], scb[2 * hp + 1], vb_], writes=[pnb[2 * hp], pnb[2 * hp + 1]])
                for h in (2 * hp, 2 * hp + 1):
                    for hf in range(2):
                        o0 = 256 * (h % 2) + 128 * hf
                        S.op("pe", lambda e, psv=psv, h=h, hf=hf, o0=o0: e.transpose(
                            out=psv[:, o0:o0 + 128], in_=pn[:, h, 128 * hf:128 * hf + 128], identity=identb[:]),
                            reads=[pnb[h], cbuf2], writes=[PSB[pb]], sig=(h % 2 == 1 and hf == 1))
                S.op("act", lambda e, psv=psv, hp=hp: e.activation(
                    out=pTs[:, 2 * hp:2 * hp + 2, :].rearrange("p h c -> p (h c)"), in_=psv[:, 0:512], func=AF.Copy),
                    reads=[PSB[pb]], writes=[pTb[2 * hp]])

            def b_second(hp):
                pb2 = nb()
                for h in (2 * hp, 2 * hp + 1):
                    g = h // 8
                    r0 = 64 * (h % 2)
                    for hf in range(2):
                        S.op("pe", lambda e, pb2=pb2, h=h, hf=hf, g=g, r0=r0: e.matmul(
                            PS[pb2][r0:r0 + 64, 0:128], lhsT=vsb[:, mb - 1 + hf, 64 * g:64 * g + 64],
                            rhs=pTs[:, h, 128 * hf:128 * hf + 128], start=(hf == 0), stop=(hf == 1), skip_group_check=True),
                            reads=[vsbb, pTb[2 * hp]], writes=[PSB[pb2]], sig=(h % 2 == 1 and hf == 1))
                S.op("act", lambda e, pb2=pb2, hp=hp: e.activation(
                    out=attnT[:, hp, c0:c0 + 128], in_=PS[pb2][:, 0:128], func=AF.Copy),
                    reads=[PSB[pb2]], writes=[atbs[hp]])

            for i in range(9):
                if i < 8:
                    b_first(i)
                if i >= 1:
                    b_second(i - 1)

        stageA(1)
        for mb in range(1, 18):
            if mb + 1 < 18:
                stageA(mb + 1)
            stageB(mb)
        if "ATT" in dbg:
            ATT = self.dscr("ATT", [8, 128, NMAIN], BF16)
            S.dma("sp", ATT.rearrange("k p c -> p k c"), attnT[:], reads=[atb] + atbs, writes=[], sem_buf=atb)
            finals.append(atb)
        S.barrier()
        if self.stop == "ph3":
            return S, finals
        ssmT = self.sb("ssmT", [128, 4, NMAIN], BF16, 30 * KB)
        ssb = S.buf("ssmT")
        wg2 = [self.sb("wg2_%d" % i, [128, 4, 256], BF16, (48 + 2 * i) * KB) for i in range(2)]
        wg2b = S.bufs(2, "wg2")
        sgt = [self.sb("sgt%d" % i, [128, 512], F32, (52 + 2 * i) * KB) for i in range(2)]
        sgtb = S.bufs(2, "sgt")
        it = 0
        for c in range(4):
            w = wg2[c % 2]
            wb = wg2b[c % 2]
            S.dma("pool", w[:, :, 0:128], w_glu[:, 128 * c:128 * c + 128].rearrange("(kt p) c -> p kt c", p=128), writes=[wb], sem_buf=wb)
            S.dma("pool", w[:, :, 128:256], w_glu[:, 512 + 128 * c:640 + 128 * c].rearrange("(kt p) c -> p kt c", p=128),
                  writes=[wb], sem_buf=wb)
            for (c0, n) in VT:
                pv, pg = nb(), nb()
                for kt in range(4):
                    S.op("pe", lambda e, pv=pv, w=w, kt=kt, c0=c0, n=n: e.matmul(
                        PS[pv][:, 0:n], lhsT=w[:, kt, 0:128], rhs=yg[:, kt, c0:c0 + n], start=(kt == 0), stop=(kt == 3)),
                        reads=[wb, ygb], writes=[PSB[pv]], sig=(kt == 3))
                for kt in range(4):
                    S.op("pe", lambda e, pg=pg, w=w, kt=kt, c0=c0, n=n: e.matmul(
                        PS[pg][:, 0:n], lhsT=w[:, kt, 128:256], rhs=yg[:, kt, c0:c0 + n], start=(kt == 0), stop=(kt == 3)),
                        reads=[wb, ygb], writes=[PSB[pg]], sig=(kt == 3))
                si = it % 2
                it += 1
                S.op("act", lambda e, pg=pg, c=c, n=n, si=si: e.activation(
                    out=sgt[si][:, 0:n], in_=PS[pg][:, 0:n], func=AF.Sigmoid, bias=bgluc[:, 4 + c:5 + c]),
                    reads=[PSB[pg], cbuf], writes=[sgtb[si]])
                S.op("dve", lambda e, pv=pv, c=c, c0=c0, n=n, si=si: e.scalar_tensor_tensor(
                    out=ssmT[:, c, c0:c0 + n], in0=PS[pv][:, 0:n], scalar=bgluc[:, c:c + 1], in1=sgt[si][:, 0:n],
                    op0=ALU.add, op1=ALU.mult), reads=[PSB[pv], sgtb[si], cbuf], writes=[ssb])
        S.barrier()
        mgT = self.sb("mgT", [128, 16, NMAIN], BF16, 111 * KB)
        mgb = S.buf("mgT")
        wa = [self.sb("wa%d" % i, [128, 8, 128], BF16, (48 + 2 * i) * KB) for i in range(2)]
        wab = S.bufs(2, "wa")
        ws_ = [self.sb("ws%d" % i, [128, 4, 128], BF16, (52 + i) * KB) for i in range(2)]
        wsb = S.bufs(2, "ws")
        sga = [self.sb("sga%d" % i, [128, NMAIN], BF16, 54 * KB + i * 4608) for i in range(2)]
        sgab = S.bufs(2, "sga")
        sgs = [self.sb("sgs%d" % i, [128, NMAIN], BF16, 63 * KB + i * 4608) for i in range(2)]
        sgsb = S.bufs(2, "sgs")
        t1 = [self.sb("t1_%d" % i, [128, 512], F32, (183 + 2 * i) * KB) for i in range(2)]
        t1b = S.bufs(2, "t1")
        t2 = [self.sb("t2_%d" % i, [128, 512], F32, (187 + 2 * i) * KB) for i in range(2)]
        t2b = S.bufs(2, "t2")
        it = 0
        for c in range(16):
            i2 = c % 2
            loadw("pool", wa[i2], wab[i2], w_ba[:, 128 * c:128 * c + 128], 8)
            loadw("pool", ws_[i2], wsb[i2], w_bs[:, 128 * c:128 * c + 128], 4)
            S.dma("sp", sga[i2][:], SG[c], writes=[sgab[i2]], sem_buf=sgab[i2])
            S.dma("sp", sgs[i2][:], SG[16 + c], writes=[sgsb[i2]], sem_buf=sgsb[i2])
            for (c0, n) in VT:
                pa, ps_ = nb(), nb()
                for kt in range(8):
                    S.op("pe", lambda e, pa=pa, i2=i2, kt=kt, c0=c0, n=n: e.matmul(
                        PS[pa][:, 0:n], lhsT=wa[i2][:, kt, :], rhs=attnT[:, kt, c0:c0 + n], start=(kt == 0), stop=(kt == 7)),
                        reads=[wab[i2], atbs[kt]], writes=[PSB[pa]], sig=(kt == 7))
                for kt in range(4):
                    S.op("pe", lambda e, ps_=ps_, i2=i2, kt=kt, c0=c0, n=n: e.matmul(
                        PS[ps_][:, 0:n], lhsT=ws_[i2][:, kt, :], rhs=ssmT[:, kt, c0:c0 + n], start=(kt == 0), stop=(kt == 3)),
                        reads=[wsb[i2], ssb], writes=[PSB[ps_]], sig=(kt == 3))
                ti = it % 2
                it += 1
                S.op("dve", lambda e, pa=pa, i2=i2, c0=c0, n=n, ti=ti: e.tensor_tensor(
                    out=t1[ti][:, 0:n], in0=PS[pa][:, 0:n], in1=sga[i2][:, c0:c0 + n], op=ALU.mult),
                    reads=[PSB[pa], sgab[i2]], writes=[t1b[ti]])
                S.op("dve", lambda e, ps_=ps_, i2=i2, c0=c0, n=n, ti=ti: e.tensor_tensor(
                    out=t2[ti][:, 0:n], in0=PS[ps_][:, 0:n], in1=sgs[i2][:, c0:c0 + n], op=ALU.mult),
                    reads=[PSB[ps_], sgsb[i2]], writes=[t2b[ti]])
                S.op("dve", lambda e, c=c, c0=c0, n=n, ti=ti: e.tensor_tensor(
                    out=mgT[:, c, c0:c0 + n], in0=t1[ti][:, 0:n], in1=t2[ti][:, 0:n], op=ALU.add),
                    reads=[t1b[ti], t2b[ti]], writes=[mgb])
        S.barrier()
        wo = [self.sb("wo%d" % i, [128, 16, 128], BF16, (12 + 4 * i) * KB) for i in range(2)]
        wob = S.bufs(2, "wo")
        xl = [self.sb("xl%d" % i, [128, 512], F32, (20 + 2 * i) * KB) for i in range(2)]
        xlb = S.bufs(2, "xl")
        x1s = [self.sb("x1s%d" % i, [128, 512], F32, (24 + 2 * i) * KB) for i in range(2)]
        x1sb = S.bufs(2, "x1s")
        sqt = [self.sb("sqt%d" % i, [128, 512], F32, (28 + 2 * i) * KB) for i in range(2)]
        sqtb = S.bufs(2, "sqt")
        ssacc = self.sb("ssacc", [128, NMAIN], F32, 32 * KB)
        ssab = S.buf("ssacc")
        rstd = self.sb("rstd", [128, NMAIN], F32, 183 * KB)
        rstb = S.buf("rstd")
        h2T = self.sb("h2T", [128, 16, 2050], BF16, 42 * KB)
        h2b = S.buf("h2T")
        S.op("dve", lambda e: e.memset(ssacc[:], 0.0), writes=[ssab])
        it = 0
        for c in range(16):
            i2 = c % 2
            loadw("pool", wo[i2], wob[i2], w_out[:, 128 * c:128 * c + 128], 16)
            for (c0, n) in VT:
                ti = it % 2
                it += 1
                S.dma("pool", xl[ti][:, 0:n], XT[c][:, c0:c0 + n], writes=[xlb[ti]], sem_buf=xlb[ti])
                pb = nb()
                for kt in range(16):
                    S.op("pe", lambda e, pb=pb, i2=i2, kt=kt, c0=c0, n=n: e.matmul(
                        PS[pb][:, 0:n], lhsT=wo[i2][:, kt, :], rhs=mgT[:, kt, c0:c0 + n], start=(kt == 0), stop=(kt == 15)),
                        reads=[wob[i2], mgb], writes=[PSB[pb]], sig=(kt == 15))
                S.op("dve", lambda e, pb=pb, n=n, ti=ti: e.tensor_tensor(
                    out=x1s[ti][:, 0:n], in0=PS[pb][:, 0:n], in1=xl[ti][:, 0:n], op=ALU.add),
                    reads=[PSB[pb], xlb[ti]], writes=[x1sb[ti]])
                S.dma("sp", X1T[c][:, c0:c0 + n], x1s[ti][:, 0:n], reads=[x1sb[ti]], writes=[], sem_buf=x1sb[ti])
                S.op("act", lambda e, n=n, ti=ti: e.activation(out=sqt[ti][:, 0:n], in_=x1s[ti][:, 0:n], func=AF.Square),
                     reads=[x1sb[ti]], writes=[sqtb[ti]])
                lo_ = max(c0, 254)
                S.op("act", lambda e, c=c, c0=c0, n=n, ti=ti, lo_=lo_: e.activation(
                    out=h2T[:, c, lo_ - 254:c0 + n - 254], in_=x1s[ti][:, lo_ - c0:n], func=AF.Copy, scale=g2c[:, c:c + 1]),
                    reads=[x1sb[ti], cbuf], writes=[h2b])
                S.op("dve", lambda e, c0=c0, n=n, ti=ti: e.tensor_tensor(
                    out=ssacc[:, c0:c0 + n], in0=ssacc[:, c0:c0 + n], in1=sqt[ti][:, 0:n], op=ALU.add),
                    reads=[sqtb[ti], ssab], writes=[ssab])

        def rstd_from(acc, accb, dst, dstb, tiles):
            for (c0, n) in tiles:
                pb = nb()
                S.op("pe", lambda e, pb=pb, c0=c0, n=n: e.matmul(PS[pb][:, 0:n], lhsT=onesf[:], rhs=acc[:, c0:c0 + n],
                                                                   start=True, stop=True),
                     reads=[accb, cbuf], writes=[PSB[pb]], sig=True)
                S.op("dve", lambda e, pb=pb, c0=c0, n=n: e.tensor_scalar(
                    out=dst[:, c0:c0 + n], in0=PS[pb][:, 0:n], scalar1=1.0 / D, scalar2=1e-6, op0=ALU.mult, op1=ALU.add),
                    reads=[PSB[pb]], writes=[dstb])
                S.op("act", lambda e, c0=c0, n=n: e.activation(out=dst[:, c0:c0 + n], in_=dst[:, c0:c0 + n], func=AF.Sqrt),
                     reads=[dstb], writes=[dstb])
                S.op("dve", lambda e, c0=c0, n=n: e.reciprocal(out=dst[:, c0:c0 + n], in_=dst[:, c0:c0 + n]),
                     reads=[dstb], writes=[dstb])

        rstd_from(ssacc, ssab, rstd, rstb, VT)
        for c in range(16):
            S.op("dve", lambda e, c=c: e.tensor_tensor(out=h2T[:, c, :], in0=h2T[:, c, :], in1=rstd[:, 254:2304], op=ALU.mult),
                 reads=[rstb, h2b], writes=[h2b])
        S.barrier()
        wu2 = [self.sb("wu2_%d" % i, [128, 16, 256], BF16, (111 + 8 * i) * KB) for i in range(2)]
        wu2b = S.bufs(2, "wu2")
        gbufs = [self.sb("gbuf%d" % i, [128, 2050], F32, (127 + 9 * i) * KB) for i in range(2)]
        gbbs = S.bufs(2, "gbuf")
        vbufs = [self.sb("vbuf%d" % i, [128, 2048], F32, (145 + 8 * i) * KB) for i in range(2)]
        vbbs = S.bufs(2, "vbuf")
        tcvs = [self.sb("tcv%d" % i, [128, 2048], F32, (161 + 8 * i) * KB) for i in range(2)]
        tcbs = S.bufs(2, "tcv")
        tgs = [self.sb("tg%d" % i, [128, 2048], F32, (177 + 8 * i) * KB) for i in range(2)]
        tgbs = S.bufs(2, "tg")
        ast = [self.sb("ast%d" % i, [128, 2048], BF16, (193 + 4 * i) * KB) for i in range(2)]
        astb = S.bufs(2, "ast")
        mk2 = self.sb("mk2", [128, 2], F32, 201 * KB)
        mk2b = S.buf("mk2")
        S.dma("sp", mk2[:], maskw[MB0 * 128 + 254:MB0 * 128 + 256].partition_broadcast(128), writes=[mk2b], sem_buf=mk2b, nonc=True)
        for j in range(44):
            i2 = j % 2
            w = wu2[i2]
            wb = wu2b[i2]
            gbuf, gbb, vbuf, vbb, tcv, tcb, tg, tgb = gbufs[i2], gbbs[i2], vbufs[i2], vbbs[i2], tcvs[i2], tcbs[i2], tgs[i2], tgbs[i2]
            S.dma("pool", w[:, :, 0:128], w_up[:, 128 * j:128 * j + 128].rearrange("(kt p) c -> p kt c", p=128), writes=[wb], sem_buf=wb)
            S.dma("pool", w[:, :, 128:256], w_up[:, DFF + 128 * j:DFF + 128 * j + 128].rearrange("(kt p) c -> p kt c", p=128),
                  writes=[wb], sem_buf=wb)
            ph = nb()
            for kt in range(16):
                S.op("pe", lambda e, ph=ph, w=w, kt=kt: e.matmul(PS[ph][:, 0:2], lhsT=w[:, kt, 128:256], rhs=h2T[:, kt, 0:2],
                                                                  start=(kt == 0), stop=(kt == 15)),
                     reads=[wb, h2b], writes=[PSB[ph]], sig=(kt == 15))
            S.op("dve", lambda e, ph=ph, gbuf=gbuf: e.tensor_tensor(out=gbuf[:, 0:2], in0=PS[ph][:, 0:2], in1=mk2[:], op=ALU.mult),
                 reads=[PSB[ph], mk2b], writes=[gbb])
            for t in range(4):
                pv, pg = nb(), nb()
                for kt in range(16):
                    S.op("pe", lambda e, pv=pv, w=w, kt=kt, t=t: e.matmul(
                        PS[pv][:], lhsT=w[:, kt, 0:128], rhs=h2T[:, kt, 2 + 512 * t:514 + 512 * t], start=(kt == 0), stop=(kt == 15)),
                        reads=[wb, h2b], writes=[PSB[pv]], sig=(kt == 15))
                for kt in range(16):
                    S.op("pe", lambda e, pg=pg, w=w, kt=kt, t=t: e.matmul(
                        PS[pg][:], lhsT=w[:, kt, 128:256], rhs=h2T[:, kt, 2 + 512 * t:514 + 512 * t], start=(kt == 0), stop=(kt == 15)),
                        reads=[wb, h2b], writes=[PSB[pg]], sig=(kt == 15))
                S.op("act", lambda e, pg=pg, t=t, gbuf=gbuf: e.activation(out=gbuf[:, 2 + 512 * t:514 + 512 * t], in_=PS[pg][:], func=AF.Copy),
                     reads=[PSB[pg]], writes=[gbb])
                S.op("dve", lambda e, pv=pv, t=t, vbuf=vbuf: e.tensor_copy(out=vbuf[:, 512 * t:512 * t + 512], in_=PS[pv][:]),
                     reads=[PSB[pv]], writes=[vbb])
            S.op("dve", lambda e, j=j, tcv=tcv, gbuf=gbuf: e.tensor_scalar(out=tcv[:], in0=gbuf[:, 0:2048], scalar1=cwc[:, 0, j:j + 1], scalar2=cbc[:, j:j + 1],
                                                        op0=ALU.mult, op1=ALU.add), reads=[gbb, cbuf], writes=[tcb])
            S.op("dve", lambda e, j=j, tcv=tcv, gbuf=gbuf: e.scalar_tensor_tensor(out=tcv[:], in0=gbuf[:, 1:2049], scalar=cwc[:, 1, j:j + 1], in1=tcv[:],
                                                               op0=ALU.mult, op1=ALU.add), reads=[gbb, tcb, cbuf], writes=[tcb])
            S.op("dve", lambda e, j=j, tcv=tcv, gbuf=gbuf: e.scalar_tensor_tensor(out=tcv[:], in0=gbuf[:, 2:2050], scalar=cwc[:, 2, j:j + 1], in1=tcv[:],
                                                               op0=ALU.mult, op1=ALU.add), reads=[gbb, tcb, cbuf], writes=[tcb])
            S.op("act", lambda e, tg=tg, tcv=tcv: e.activation(out=tg[:], in_=tcv[:], func=AF.Gelu), reads=[tcb], writes=[tgb])
            S.op("dve", lambda e, i2=i2, vbuf=vbuf, tg=tg: e.tensor_tensor(out=ast[i2][:], in0=vbuf[:], in1=tg[:], op=ALU.mult),
                 reads=[vbb, tgb], writes=[astb[i2]])
            S.dma("sp", ACTd[j], ast[i2][:], reads=[astb[i2]], writes=[], sem_buf=astb[i2])
        S.barrier()
        acth = self.sb("acth", [128, 44, 1024], BF16, 12 * KB)
        achb = S.buf("acth")
        achq = S.bufs(4, "acthq")
        wd = [self.sb("wd%d" % i, [128, 44, 128], BF16, (100 + 11 * i) * KB) for i in range(2)]
        wdb = S.bufs(2, "wd")
        xl9 = [self.sb("xl9_%d" % i, [128, 512], F32, (122 + 2 * i) * KB) for i in range(2)]
        xl9b = S.bufs(2, "xl9")
        x2s = [self.sb("x2s%d" % i, [128, 512], F32, (126 + 2 * i) * KB) for i in range(2)]
        x2sb = S.bufs(2, "x2s")
        sq9 = [self.sb("sq9_%d" % i, [128, 512], F32, (130 + 2 * i) * KB) for i in range(2)]
        sq9b = S.bufs(2, "sq9")
        ssa2 = self.sb("ssa2", [128, 2048], F32, 134 * KB)
        ssa2b = S.buf("ssa2")
        rstd2 = self.sb("rstd2", [128, 2048], F32, 142 * KB)
        rst2b = S.buf("rstd2")
        S.op("dve", lambda e: e.memset(ssa2[:], 0.0), writes=[ssa2b])
        it = 0
        wi = 0
        spare = [self.sb("acts%d" % i, [128, 11, 1024], BF16, (150 + 22 * i) * KB) for i in range(2)]
        spb = S.bufs(2, "acts")
        ACv = ACTd.rearrange("j p c -> p j c")

        def qtile(hf_, qq):
            if hf_ == 0:
                return acth[:, 11 * qq:11 * qq + 11, :], achq[qq]
            if qq < 2:
                return spare[qq][:], spb[qq]
            return acth[:, 11 * (qq - 2):11 * (qq - 2) + 11, :], achq[qq - 2]

        def qload(hf_, qq):
            tl, tb_ = qtile(hf_, qq)
            S.dma("sp", tl, ACv[:, 11 * qq:11 * qq + 11, 1024 * hf_:1024 * hf_ + 1024], writes=[tb_], sem_buf=tb_)

        for qq in range(4):
            qload(0, qq)
        qload(1, 0)
        qload(1, 1)
        for hf in range(2):
            if hf == 1:
                qload(1, 2)
                qload(1, 3)
            for c in range(16):
                i2 = wi % 2
                wi += 1
                loadw("pool", wd[i2], wdb[i2], w_down[:, 128 * c:128 * c + 128], 44)
                for tt in range(2):
                    o0 = 1024 * hf + 512 * tt
                    ti = it % 2
                    it += 1
                    S.dma("pool", xl9[ti][:], X1T[c][:, OWN0 + o0:OWN0 + o0 + 512], writes=[xl9b[ti]], sem_buf=xl9b[ti])
                    pb = nb()
                    for kt in range(44):
                        qt_, qb_ = qtile(hf, kt // 11)
                        S.op("pe", lambda e, pb=pb, i2=i2, kt=kt, tt=tt, qt_=qt_: e.matmul(
                            PS[pb][:], lhsT=wd[i2][:, kt, :], rhs=qt_[:, kt % 11, 512 * tt:512 * tt + 512], start=(kt == 0), stop=(kt == 43)),
                            reads=[wdb[i2], qb_], writes=[PSB[pb]], sig=(kt == 43))
                    S.op("dve", lambda e, pb=pb, ti=ti: e.tensor_tensor(out=x2s[ti][:], in0=PS[pb][:], in1=xl9[ti][:], op=ALU.add),
                         reads=[PSB[pb], xl9b[ti]], writes=[x2sb[ti]])
                    S.dma("sp", X2T[c][:, o0:o0 + 512], x2s[ti][:], reads=[x2sb[ti]], writes=[], sem_buf=x2sb[ti])
                    S.op("act", lambda e, ti=ti: e.activation(out=sq9[ti][:], in_=x2s[ti][:], func=AF.Square),
                         reads=[x2sb[ti]], writes=[sq9b[ti]])
                    S.op("dve", lambda e, o0=o0, ti=ti: e.tensor_tensor(
                        out=ssa2[:, o0:o0 + 512], in0=ssa2[:, o0:o0 + 512], in1=sq9[ti][:], op=ALU.add),
                        reads=[sq9b[ti], ssa2b], writes=[ssa2b])
        rstd_from(ssa2, ssa2b, rstd2, rst2b, [(0, 512), (512, 512), (1024, 512), (1536, 512)])
        S.barrier()
        x2l = [self.sb("x2l%d" % i, [128, 16, 128], F32, (12 + 8 * i) * KB) for i in range(2)]
        x2lb = S.bufs(2, "x2l")
        otmps = [self.sb("otmp%d" % i, [128, 16, 128], F32, (28 + 32 * i) * KB) for i in range(2)]
        otbs = S.bufs(2, "otmp")
        oTs = [self.sb("oT%d" % i, [128, 16, 128], F32, (36 + 32 * i) * KB) for i in range(2)]
        oTbs = S.bufs(2, "oT")
        orow = [self.sb("orow%d" % i, [128, D], F32, (44 + 8 * i) * KB) for i in range(2)]
        orb = S.bufs(2, "orow")
        X2v = X2T.rearrange("k p c -> p k c")
        rcol = [self.sb("rcol%d" % i, [128, 1], F32, 76 * KB + 64 * i) for i in range(2)]
        rcolb = S.bufs(2, "rcol")
        S.dma("sp", x2l[0][:], X2v[:, :, 0:128], writes=[x2lb[0]], sem_buf=x2lb[0])
        for tblk in range(16):
            i2 = tblk % 2
            c0 = 128 * tblk
            otmp, otb, oT, oTb = otmps[i2], otbs[i2], oTs[i2], oTbs[i2]
            if tblk + 1 < 16:
                S.dma("sp", x2l[1 - i2][:], X2v[:, :, c0 + 128:c0 + 256], writes=[x2lb[1 - i2]], sem_buf=x2lb[1 - i2])
            pr = nb()
            S.op("pe", lambda e, pr=pr, c0=c0: e.matmul(PS[pr][:, 0:1], lhsT=rstd2[0:1, c0:c0 + 128], rhs=onesf[0:1, 0:1],
                                                         start=True, stop=True), reads=[rst2b, cbuf], writes=[PSB[pr]], sig=True)
            S.op("act", lambda e, pr=pr, i2=i2: e.activation(out=rcol[i2][:], in_=PS[pr][:, 0:1], func=AF.Copy),
                 reads=[PSB[pr]], writes=[rcolb[i2]])
            S.op("dve", lambda e, i2=i2, oT=oT: e.tensor_tensor(
                out=oT[:], in0=x2l[i2][:], in1=g3c[:].unsqueeze(2).to_broadcast([128, 16, 128]), op=ALU.mult),
                reads=[x2lb[i2], cbuf], writes=[oTb])
            pbs = []
            for j in range(4):
                pb = nb()
                pbs.append(pb)
                for kk in range(4):
                    kt = 4 * j + kk
                    S.op("pe", lambda e, pb=pb, kk=kk, kt=kt, oT=oT: e.transpose(
                        out=PS[pb][:, 128 * kk:128 * kk + 128], in_=oT[:, kt, :], identity=ident[:]),
                        reads=[oTb, cbuf], writes=[PSB[pb]], sig=(kk == 3))
            for j in range(4):
                pb = pbs[j]
                if j % 2 == 0:
                    S.op("act", lambda e, pb=pb, j=j, i2=i2: e.activation(out=orow[i2][:, 512 * j:512 * j + 512], in_=PS[pb][:], func=AF.Copy,
                                                                         scale=rcol[i2][:, 0:1]),
                         reads=[PSB[pb], rcolb[i2]], writes=[orb[i2]])
                else:
                    S.op("dve", lambda e, pb=pb, j=j, i2=i2: e.tensor_scalar_mul(out=orow[i2][:, 512 * j:512 * j + 512], in0=PS[pb][:],
                                                                                scalar1=rcol[i2][:, 0:1]),
                         reads=[PSB[pb], rcolb[i2]], writes=[orb[i2]])
            S.dma("sp", out[c0:c0 + 128, :], orow[i2][:], reads=[orb[i2]], writes=[], sem_buf=orb[i2])
        finals.extend(orb)
        return S, finals

    def finish(self, S, final_bufs):
        toks = []
        for b in final_bufs:
            if b.dsem is not None:
                toks.append([b.dsem, b.dsem.cnt])
        S.emit(toks)
        return self.nc


def _abias_table():
    qi = np.arange(128)[:, None]
    si = np.arange(256)[None, :]
    dist = qi + 128 - si
    band = (dist >= 0) & (dist < 128)
    slopes = 2.0 ** (-8.0 * np.arange(1, 17, dtype=np.float32) / 16)
    t = -slopes[None, :, None] * dist[:, None, :].astype(np.float32)
    t = np.where(band[:, None, :], t, np.float32(-30000.0))
    return np.ascontiguousarray(t.astype(np.float32))


def make_in_maps(inputs):
    x = np.asarray(inputs["x"], dtype=np.float32)
    maps = []
    ident = np.eye(128, dtype=np.float32)
    ab = _abias_table()
    for core in range(NCORES):
        b, c = core // 4, core % 4
        t1 = 2048 * (c + 1)
        xw = np.zeros((WIN, D), np.float32)
        xw[WIN - t1:] = x[b, :t1]
        mask = np.zeros((WIN,), np.float32)
        mask[WIN - t1:] = 1.0
        m = {"xw": xw, "maskw": mask, "identity": ident, "abias": ab}
        for k, v in inputs.items():
            if k == "x":
                continue
            v = np.asarray(v, dtype=np.float32)
            m[k] = np.ascontiguousarray(v[0]) if k != "final_norm_g" else np.ascontiguousarray(v)
        maps.append(m)
    return maps


_NC_CACHE = {}


def _get_nc():
    if "nc" not in _NC_CACHE:
        kb = K()
        S, finals = kb.build()
        _NC_CACHE["nc"] = kb.finish(S, finals)
    return _NC_CACHE["nc"]


def kernel(**inputs):
    nc = _get_nc()
    in_maps = make_in_maps(inputs)
    res = run_bass_kernel_spmd(nc, in_maps, core_ids=list(range(NCORES)))
    outs = [np.asarray(r["out"], dtype=np.float32) for r in res.results]
    full = np.stack(outs, 0).reshape(2, 4 * 2048, D)
    return full
```

```python
import contextlib
import math
import numpy as np
import concourse.bass as bass
import concourse.mybir as mybir
from concourse.bass_utils import run_bass_kernel_spmd

F32 = mybir.dt.float32
BF16 = mybir.dt.bfloat16
AF = mybir.ActivationFunctionType
ALU = mybir.AluOpType

D = 2048
NCORES = 8
WIN = 8192
NBLK = 64
MB0 = 46
NMAIN = 2304
OWN0 = 256
NCH0 = MB0 * 8
NCHM = 144
KVALS = list(range(17)) + [64]
NK = len(KVALS)
DFF = 5632
KB = 1024


class Sem:
    def __init__(self, h):
        self.h = h
        self.cnt = 0


class Buf:
    __slots__ = ("name", "w", "r", "dsem")

    def __init__(self, name):
        self.name = name
        self.w = None
        self.r = {}
        self.dsem = None


class Sched:
    ENG = ("pe", "act", "dve", "pool", "sp")

    def __init__(self, nc, stack):
        self.nc = nc
        self.stack = stack
        self.ops = {e: [] for e in self.ENG}
        self.esem = {e: Sem(stack.enter_context(nc.semaphore("es_" + e))) for e in self.ENG}
        self.future = {e: [self.esem[e], None] for e in self.ENG}
        self.last = {e: None for e in self.ENG}
        self.extra = {e: [] for e in self.ENG}
        self.dsems = []
        self.nbuf = 0

    def buf(self, name=None):
        self.nbuf += 1
        return Buf(name or "b%d" % self.nbuf)

    def bufs(self, n, name="b"):
        return [self.buf("%s%d" % (name, i)) for i in range(n)]

    def _dsem(self, b):
        if b.dsem is None:
            b.dsem = Sem(self.stack.enter_context(self.nc.semaphore("ds_%d" % len(self.dsems))))
            self.dsems.append(b.dsem)
        return b.dsem

    def _collect(self, eng, reads, writes):
        waits = list(self.extra[eng])
        self.extra[eng] = []
        for b in reads:
            if b.w is not None:
                waits.append(b.w)
        for b in writes:
            if b.w is not None:
                waits.append(b.w)
            waits.extend(b.r.values())
        out = []
        for t in waits:
            if eng in ("pe", "act", "pool") and t[0] is self.esem[eng]:
                continue
            if t[0] in self.dsems:
                t = [t[0], t[0].cnt]
            out.append(t)
        return out

    def op(self, eng, fn, reads=(), writes=(), sig=True):
        waits = self._collect(eng, reads, writes)
        tok = self.future[eng]
        inc = None
        if sig:
            s = self.esem[eng]
            s.cnt += 1
            tok[1] = s.cnt
            self.future[eng] = [s, None]
            inc = (s, 1)
            self.last[eng] = tok
        self.ops[eng].append((waits, fn, inc))
        for b in writes:
            b.w = tok
            b.r = {}
        for b in reads:
            b.r[id(tok[0])] = tok
        return tok

    def dma(self, eng, out_ap, in_ap, reads=(), writes=(), sem_buf=None, nonc=False):
        waits = self._collect(eng, reads, writes)
        s = self._dsem(sem_buf)
        s.cnt += 16
        tok = [s, s.cnt]
        nc = self.nc

        def fn(e, out_ap=out_ap, in_ap=in_ap):
            if nonc:
                with nc.allow_non_contiguous_dma(reason="small param gather"):
                    return e.dma_start(out=out_ap, in_=in_ap)
            return e.dma_start(out=out_ap, in_=in_ap)

        self.ops[eng].append((waits, fn, (s, 16)))
        for b in writes:
            b.w = tok
            b.r = {}
        for b in reads:
            b.r[id(s)] = tok
        return tok

    def barrier(self):
        toks = []
        for e in self.ENG:
            if self.last[e] is not None:
                toks.append(self.last[e])
        for s in self.dsems:
            if s.cnt:
                toks.append([s, s.cnt])
        for e in self.ENG:
            self.extra[e].extend(toks)

    def emit(self, final_toks):
        nc = self.nc
        for e in self.ENG:
            assert self.future[e][1] is None
        with nc.Block() as block:
            def replay(eng, h):
                waited = {}
                for waits, fn, inc in self.ops[eng]:
                    for t in waits:
                        assert t[1] is not None, "unresolved token"
                        k = id(t[0])
                        if waited.get(k, 0) < t[1]:
                            h.wait_ge(t[0].h, t[1])
                            waited[k] = t[1]
                    ins = fn(h)
                    if inc is not None:
                        ins.then_inc(inc[0].h, inc[1])
                if eng == "sp":
                    for t in final_toks:
                        h.wait_ge(t[0].h, t[0].cnt)

            @block.tensor
            def _(h):
                replay("pe", h)

            @block.scalar
            def _(h):
                replay("act", h)

            @block.vector
            def _(h):
                replay("dve", h)

            @block.gpsimd
            def _(h):
                replay("pool", h)

            @block.sync
            def _(h):
                replay("sp", h)


class K:
    def __init__(self, debug=()):
        self.debug = set(debug)
        self.stack = contextlib.ExitStack()
        self.nc = bass.Bass("TRN2", target_bir_lowering=False)
        self.S = None
        self.stop = None

    def din(self, name, shape, dt=F32):
        return self.nc.dram_tensor(name, list(shape), dt, kind="ExternalInput").ap()

    def dscr(self, name, shape, dt, out=False):
        kind = "ExternalOutput" if (out or name in self.debug) else "Internal"
        return self.nc.dram_tensor(name, list(shape), dt, kind=kind).ap()

    def sb(self, name, shape, dt, off):
        return self.nc.alloc_sbuf_tensor_at(name, list(shape), dt, offset=off + 16 * KB)

    def build(self):
        nc = self.nc
        st = self.stack
        S = self.S = Sched(nc, st)
        dbg = self.debug
        xw = self.din("xw", [WIN, D])
        maskw = self.din("maskw", [WIN])
        g1 = self.din("attn_norm_g", [D])
        w_in = self.din("w_in", [D, 5888])
        b_in = self.din("b_in", [5888])
        sinks = self.din("attn_sinks", [16])
        a_re = self.din("ssm_a_re", [32, 64])
        a_im = self.din("ssm_a_im", [32, 64])
        log_dt = self.din("ssm_log_dt", [32])
        b_re = self.din("ssm_b_re", [32, 64, 16])
        b_im = self.din("ssm_b_im", [32, 64, 16])
        c_re = self.din("ssm_c_re", [32, 16, 64])
        c_im = self.din("ssm_c_im", [32, 16, 64])
        ssm_d = self.din("ssm_d", [512])
        w_glu = self.din("w_glu", [512, 1024])
        b_glu = self.din("b_glu", [1024])
        w_ba = self.din("w_branch_attn", [1024, D])
        w_bs = self.din("w_branch_ssm", [512, D])
        w_out = self.din("w_out", [D, D])
        g2 = self.din("ffn_norm_g", [D])
        w_up = self.din("w_up", [D, 2 * DFF])
        conv_w = self.din("conv_w", [3, DFF])
        conv_b = self.din("conv_b", [DFF])
        w_down = self.din("w_down", [DFF, D])
        g3 = self.din("final_norm_g", [D])
        abias = self.din("abias", [128, 16, 256])
        out = self.nc.dram_tensor("out", [2048, D], F32, kind="ExternalOutput").ap()
        FFd = self.dscr("FFd", [128, 16 * 2 * 16 * 32], BF16)
        TZd = self.dscr("TZd", [128, 16 * 4 * 128], BF16)
        XT = self.dscr("XT", [16, 128, NMAIN], F32)
        HT = self.dscr("HT", [16, 128, NMAIN], BF16)
        YG = self.dscr("YG", [4, 128, NMAIN], BF16) if "YG" in dbg else None
        UM = self.dscr("UM", [4, 128, NMAIN], BF16) if "UM" in dbg else None
        XSd = self.dscr("XSd", [128, 2 * 16 * NCHM], F32) if "XSd" in dbg else None

        o = 0

        def cal(name, shape, dt):
            nonlocal o
            nbytes = int(np.prod(shape[1:])) * (4 if dt == F32 else 2)
            t = self.sb(name, shape, dt, o)
            o += (nbytes + 31) // 32 * 32
            return t

        ident = cal("ident", [128, 128], F32)
        identb = cal("identb", [128, 128], BF16)
        onesb = cal("onesb", [128, 128], BF16)
        onesf = cal("onesf", [128, 128], F32)
        g1c = cal("g1c", [128, 16], F32)
        g2c = cal("g2c", [128, 16], F32)
        g3c = cal("g3c", [128, 16], F32)
        binc = cal("binc", [128, 46], F32)
        bgluc = cal("bgluc", [128, 8], F32)
        cwc = cal("cwc", [128, 3, 44], F32)
        cbc = cal("cbc", [128, 44], F32)
        dcol = cal("dcol", [128, 4], F32)
        AAt = cal("AAt", [128, 2, 16], F32)
        BBt = cal("BBt", [128, 2, 16], F32)
        X4 = [cal("X4a", [128, 3, 16], F32), cal("X4b", [128, 3, 16], F32)]
        P1 = cal("P1", [128, 2, 16], F32)
        P2 = cal("P2", [128, 2, 16], F32)
        A4re = cal("A4re", [128, 16], F32)
        A4im = cal("A4im", [128, 16], F32)
        A16re = cal("A16re", [128, 16], F32)
        A16im = cal("A16im", [128, 16], F32)
        AA64 = cal("AA64", [128, 2, 16], F32)
        BB64 = cal("BB64", [128, 2, 16], F32)
        assert o <= 8 * KB, o
        cbuf = S.buf("consts")
        identity_src = self.din("identity", [128, 128])
        S.dma("sp", ident[:], identity_src, writes=[cbuf], sem_buf=cbuf)
        cbuf2 = S.buf("consts2")
        S.dma("pool", identb[:], identity_src, writes=[cbuf2], sem_buf=cbuf2)
        S.op("dve", lambda e: e.memset(onesb[:], 1.0), writes=[cbuf])
        S.op("dve", lambda e: e.memset(onesf[:], 1.0), writes=[cbuf])
        for t, src in ((g1c, g1), (g2c, g2), (g3c, g3)):
            S.dma("sp", t[:], src.rearrange("(k p) -> p k", p=128), writes=[cbuf], sem_buf=cbuf, nonc=True)
        S.dma("sp", binc[:], b_in.rearrange("(k p) -> p k", p=128), writes=[cbuf], sem_buf=cbuf, nonc=True)
        S.dma("sp", bgluc[:], b_glu.rearrange("(k p) -> p k", p=128), writes=[cbuf], sem_buf=cbuf, nonc=True)
        S.dma("sp", cwc[:], conv_w.rearrange("w (k p) -> p w k", p=128), writes=[cbuf], sem_buf=cbuf, nonc=True)
        S.dma("sp", cbc[:], conv_b.rearrange("(k p) -> p k", p=128), writes=[cbuf], sem_buf=cbuf, nonc=True)
        S.dma("sp", dcol[:], ssm_d.rearrange("(k p) -> p k", p=128), writes=[cbuf], sem_buf=cbuf, nonc=True)

        PS = [st.enter_context(nc.psum_tensor("ps%d" % i, [128, 512], F32)) for i in range(8)]
        PSB = S.bufs(8, "psb")

        base = 12 * KB
        SW = self.sb("SW", [128, 16, 2, 4, 128], BF16, 30 * KB)
        swb = S.buf("SW")
        po = 62 * KB

        def pal(name, shape, dt):
            nonlocal po
            nbytes = int(np.prod(shape[1:])) * (4 if dt == F32 else 2)
            t = self.sb(name, shape, dt, po)
            po += (nbytes + 31) // 32 * 32
            return t

        CAre = pal("CAre", [128, 17, 16, 32], F32)
        CAim = pal("CAim", [128, 17, 16, 32], F32)
        SWT = pal("SWT", [128, 16, 16, 32], F32)
        TMPB = pal("TMPB", [128, 17, 16, 32], F32)
        assert po <= 200 * KB, po
        po = base
        are = pal("are", [128, 16], F32)
        aim = pal("aim", [128, 16], F32)
        ldt = pal("ldt", [128, 16], F32)
        dtv = pal("dtv", [128, 16], F32)
        adr = pal("adr", [128, 16], F32)
        ang = pal("ang", [128, 16], F32)
        KM = pal("KM", [128, NK, 16], F32)
        PWm = pal("PWm", [128, NK, 16], F32)
        ANG = pal("ANG", [128, NK, 16], F32)
        T17 = pal("T17", [128, NK, 16], F32)
        XS17 = self.sb("XS17", [128, NK, 16], F32, 174 * KB)
        NI = self.sb("NI", [128, NK, 16], mybir.dt.int32, 176 * KB)
        PWre = pal("PWre", [128, NK, 16], F32)
        PWim = pal("PWim", [128, NK, 16], F32)
        t16 = [pal("t16_%d" % i, [128, 16], F32) for i in range(6)]
        zre = pal("zre", [128, 16], F32)
        zim = pal("zim", [128, 16], F32)
        BEre = self.sb("BEre", [128, 16, 32], F32, 170 * KB)
        BEim = self.sb("BEim", [128, 16, 32], F32, 172 * KB)
        Bbre = pal("Bbre", [128, 16, 32], F32)
        Bbim = pal("Bbim", [128, 16, 32], F32)
        CEre = pal("CEre", [128, 16, 32], F32)
        CEim = pal("CEim", [128, 16, 32], F32)
        assert po <= 30 * KB, po
        CN4 = [self.sb("CN4re", [128, 4, 128], F32, 196 * KB), self.sb("CN4im", [128, 4, 128], F32, 198 * KB)]
        BZre = self.sb("BZre", [128, 16, 128], F32, 130 * KB)
        BZim = self.sb("BZim", [128, 16, 128], F32, 138 * KB)
        FFs = self.sb("FFs", [128, 16, 16 * 32], BF16, 130 * KB)
        p0 = S.buf("p0")

        PI = math.pi
        pin = []

        def pin_new():
            pin.append(S.buf("pin%d" % len(pin)))
            return pin[-1]

        for gl in range(2):
            sl = slice(64 * gl, 64 * gl + 64)
            S.dma("sp", are[sl, :], a_re.rearrange("(gp gl) p -> gl p gp", gl=2)[gl], writes=[pin_new()], sem_buf=pin[-1], nonc=True)
            S.dma("sp", aim[sl, :], a_im.rearrange("(gp gl) p -> gl p gp", gl=2)[gl], writes=[pin_new()], sem_buf=pin[-1], nonc=True)
            S.dma("sp", ldt[sl, :], log_dt.rearrange("(gp gl) -> gl gp", gl=2)[gl].partition_broadcast(64),
                  writes=[pin_new()], sem_buf=pin[-1], nonc=True)
        S.op("dve", lambda e: e.memset(BEre[:], 0.0), writes=[p0])
        S.op("dve", lambda e: e.memset(BEim[:], 0.0), writes=[p0])
        S.op("dve", lambda e: e.memset(CN4[0][:], 0.0), writes=[p0])
        S.op("dve", lambda e: e.memset(CN4[1][:], 0.0), writes=[p0])
        for gl in range(2):
            sl = slice(64 * gl, 64 * gl + 64)
            for t, src in ((BEre, b_re), (BEim, b_im)):
                S.dma("sp", t[sl, :, 16 * gl:16 * gl + 16], src.rearrange("(gp gl) p h -> gl p gp h", gl=2)[gl],
                      reads=[p0], writes=[pin_new()], sem_buf=pin[-1], nonc=True)
        for q in range(4):
            for gl in range(2):
                for t, src in ((CN4[0], c_re), (CN4[1], c_im)):
                    S.dma("sp", t[32 * q + 16 * gl:32 * q + 16 * gl + 16, :, 64 * gl:64 * gl + 64],
                          src.rearrange("(k q gl) h p -> q gl h k p", q=4, gl=2)[q, gl],
                          reads=[p0], writes=[pin_new()], sem_buf=pin[-1], nonc=True)
        for ki, kv in enumerate(KVALS):
            S.op("dve", lambda e, ki=ki, kv=kv: e.memset(KM[:, ki, :], float(kv)), writes=[p0])
        S.op("act", lambda e: e.activation(out=dtv[:], in_=ldt[:], func=AF.Exp), reads=[p0] + pin, writes=[p0] + pin)
        S.op("dve", lambda e: e.tensor_tensor(out=adr[:], in0=are[:], in1=dtv[:], op=ALU.mult), reads=[p0], writes=[p0])
        S.op("dve", lambda e: e.tensor_tensor(out=ang[:], in0=aim[:], in1=dtv[:], op=ALU.mult), reads=[p0], writes=[p0])

        def bc17(t):
            return t[:].unsqueeze(1).to_broadcast([128, NK, 16])

        S.op("dve", lambda e: e.tensor_tensor(out=T17[:], in0=KM[:], in1=bc17(adr), op=ALU.mult), reads=[p0], writes=[p0])
        S.op("act", lambda e: e.activation(out=PWm[:], in_=T17[:], func=AF.Exp), reads=[p0], writes=[p0])
        S.op("dve", lambda e: e.tensor_tensor(out=ANG[:], in0=KM[:], in1=bc17(ang), op=ALU.mult), reads=[p0], writes=[p0])
        def sincos(dst, shift):
            S.op("dve", lambda e: e.tensor_scalar(out=T17[:], in0=ANG[:], scalar1=1.0 / (2 * PI), scalar2=0.5 + shift / (2 * PI),
                                                   op0=ALU.mult, op1=ALU.add), reads=[p0], writes=[p0])
            S.op("dve", lambda e: e.tensor_copy(out=NI[:], in_=T17[:]), reads=[p0], writes=[p0])
            S.op("dve", lambda e: e.tensor_copy(out=T17[:], in_=NI[:]), reads=[p0], writes=[p0])
            S.op("dve", lambda e: e.tensor_scalar_add(out=XS17[:], in0=ANG[:], scalar1=shift), reads=[p0], writes=[p0])
            S.op("dve", lambda e: e.scalar_tensor_tensor(out=T17[:], in0=T17[:], scalar=-2 * PI, in1=XS17[:],
                                                          op0=ALU.mult, op1=ALU.add), reads=[p0], writes=[p0])
            S.op("dve", lambda e: e.tensor_scalar(out=XS17[:], in0=T17[:], scalar1=-PI, scalar2=2 * PI,
                                                   op0=ALU.is_lt, op1=ALU.mult), reads=[p0], writes=[p0])
            S.op("dve", lambda e: e.tensor_tensor(out=T17[:], in0=T17[:], in1=XS17[:], op=ALU.add), reads=[p0], writes=[p0])
            S.op("dve", lambda e: e.tensor_scalar(out=T17[:], in0=T17[:], scalar1=-PI, scalar2=PI,
                                                   op0=ALU.max, op1=ALU.min), reads=[p0], writes=[p0])
            S.op("act", lambda e: e.activation(out=dst[:], in_=T17[:], func=AF.Sin), reads=[p0], writes=[p0])

        sincos(PWim, 0.0)
        sincos(PWre, 0.5 * PI)
        S.op("dve", lambda e: e.tensor_tensor(out=PWre[:], in0=PWre[:], in1=PWm[:], op=ALU.mult), reads=[p0], writes=[p0])
        S.op("dve", lambda e: e.tensor_tensor(out=PWim[:], in0=PWim[:], in1=PWm[:], op=ALU.mult), reads=[p0], writes=[p0])
        nr, ni, den, ta, tb, tc = [t[:] for t in t16]

        def dv(fn):
            S.op("dve", fn, reads=[p0], writes=[p0])

        dv(lambda e: e.tensor_scalar_add(out=nr, in0=PWre[:, 1, :], scalar1=-1.0))
        dv(lambda e: e.tensor_copy(out=ni, in_=PWim[:, 1, :]))
        dv(lambda e: e.tensor_tensor(out=den, in0=are[:], in1=are[:], op=ALU.mult))
        dv(lambda e: e.tensor_tensor(out=ta, in0=aim[:], in1=aim[:], op=ALU.mult))
        dv(lambda e: e.tensor_tensor(out=den, in0=den, in1=ta, op=ALU.add))
        dv(lambda e: e.reciprocal(out=den, in_=den))
        dv(lambda e: e.tensor_tensor(out=ta, in0=nr, in1=are[:], op=ALU.mult))
        dv(lambda e: e.tensor_tensor(out=tb, in0=ni, in1=aim[:], op=ALU.mult))
        dv(lambda e: e.tensor_tensor(out=ta, in0=ta, in1=tb, op=ALU.add))
        dv(lambda e: e.tensor_tensor(out=zre[:], in0=ta, in1=den, op=ALU.mult))
        dv(lambda e: e.tensor_tensor(out=ta, in0=ni, in1=are[:], op=ALU.mult))
        dv(lambda e: e.tensor_tensor(out=tb, in0=nr, in1=aim[:], op=ALU.mult))
        dv(lambda e: e.tensor_tensor(out=ta, in0=ta, in1=tb, op=ALU.subtract))
        dv(lambda e: e.tensor_tensor(out=zim[:], in0=ta, in1=den, op=ALU.mult))
        dv(lambda e: e.tensor_copy(out=AAt[:, 0, :], in_=PWre[:, 16, :]))
        dv(lambda e: e.tensor_copy(out=AAt[:, 1, :], in_=PWre[:, 16, :]))
        dv(lambda e: e.tensor_copy(out=BBt[:, 1, :], in_=PWim[:, 16, :]))
        dv(lambda e: e.tensor_scalar_mul(out=BBt[:, 0, :], in0=PWim[:, 16, :], scalar1=-1.0))
        dv(lambda e: e.tensor_copy(out=A4re[:], in_=PWre[:, 4, :]))
        dv(lambda e: e.tensor_copy(out=A4im[:], in_=PWim[:, 4, :]))
        dv(lambda e: e.tensor_copy(out=A16re[:], in_=PWre[:, 16, :]))
        dv(lambda e: e.tensor_copy(out=A16im[:], in_=PWim[:, 16, :]))
        dv(lambda e: e.tensor_copy(out=AA64[:, 0, :], in_=PWre[:, 17, :]))
        dv(lambda e: e.tensor_copy(out=AA64[:, 1, :], in_=PWre[:, 17, :]))
        dv(lambda e: e.tensor_copy(out=BB64[:, 1, :], in_=PWim[:, 17, :]))
        dv(lambda e: e.tensor_scalar_mul(out=BB64[:, 0, :], in0=PWim[:, 17, :], scalar1=-1.0))

        def bc32(t):
            return t[:].unsqueeze(2).to_broadcast([128, 16, 32])

        T32a = TMPB[:, 0, :, :]
        T32b = TMPB[:, 1, :, :]
        dv(lambda e: e.tensor_tensor(out=T32a, in0=BEre[:], in1=bc32(zre), op=ALU.mult))
        dv(lambda e: e.tensor_tensor(out=T32b, in0=BEim[:], in1=bc32(zim), op=ALU.mult))
        dv(lambda e: e.tensor_tensor(out=Bbre[:], in0=T32a, in1=T32b, op=ALU.subtract))
        dv(lambda e: e.tensor_tensor(out=T32a, in0=BEim[:], in1=bc32(zre), op=ALU.mult))
        dv(lambda e: e.tensor_tensor(out=T32b, in0=BEre[:], in1=bc32(zim), op=ALU.mult))
        dv(lambda e: e.tensor_tensor(out=Bbim[:], in0=T32a, in1=T32b, op=ALU.add))
        for i, (cn, ce) in enumerate(((CN4[0], CEre), (CN4[1], CEim))):
            for k in range(4):
                S.op("pe", lambda e, cn=cn, k=k: e.transpose(out=PS[0][:, 128 * k:128 * k + 128], in_=cn[:, k, :], identity=ident[:]),
                     reads=[p0, cbuf], writes=[PSB[0]], sig=(k == 3))
            S.op("act", lambda e, ce=ce: e.activation(out=ce[:].rearrange("p g c -> p (g c)"), in_=PS[0][:], func=AF.Copy),
                 reads=[PSB[0]], writes=[p0])

        def bck(t):
            return t[:].unsqueeze(1).to_broadcast([128, 17, 16, 32])

        def bcc(t):
            return t[:, 0:17, :].unsqueeze(3).to_broadcast([128, 17, 16, 32])

        dv(lambda e: e.tensor_tensor(out=CAre[:], in0=bck(CEre), in1=bcc(PWre), op=ALU.mult))
        dv(lambda e: e.tensor_tensor(out=TMPB[:], in0=bck(CEim), in1=bcc(PWim), op=ALU.mult))
        dv(lambda e: e.tensor_tensor(out=CAre[:], in0=CAre[:], in1=TMPB[:], op=ALU.subtract))
        dv(lambda e: e.tensor_tensor(out=CAim[:], in0=bck(CEre), in1=bcc(PWim), op=ALU.mult))
        dv(lambda e: e.tensor_tensor(out=TMPB[:], in0=bck(CEim), in1=bcc(PWre), op=ALU.mult))
        dv(lambda e: e.tensor_tensor(out=CAim[:], in0=CAim[:], in1=TMPB[:], op=ALU.add))
        FFv = FFd.rearrange("p (t r x) -> p t r x", t=16, r=2)
        ffb = p0
        for ri in range(2):
            src = (CAre if ri == 0 else CAim)[:, 1:17, :, :].rearrange("p t g c -> p t (g c)")
            S.op("act", lambda e, src=src, ri=ri: e.activation(out=FFs[:], in_=src, func=AF.Copy, scale=(1.0 if ri == 0 else -1.0)),
                 reads=[p0], writes=[ffb])
            S.dma("sp", FFv[:, :, ri, :], FFs[:], reads=[ffb], writes=[p0], sem_buf=ffb)
        def bcs(t):
            return t[:].unsqueeze(1).to_broadcast([128, 16, 16, 32])

        def bcp(t):
            return t[:, 0:16, :].unsqueeze(3).to_broadcast([128, 16, 16, 32])

        TM16 = TMPB[:, 0:16, :, :]
        for ri in range(2):
            if ri == 0:
                dv(lambda e: e.tensor_tensor(out=SWT[:], in0=bcs(Bbre), in1=bcp(PWre), op=ALU.mult))
                dv(lambda e: e.tensor_tensor(out=TM16, in0=bcs(Bbim), in1=bcp(PWim), op=ALU.mult))
                dv(lambda e: e.tensor_tensor(out=SWT[:], in0=SWT[:], in1=TM16, op=ALU.subtract))
            else:
                dv(lambda e: e.tensor_tensor(out=SWT[:], in0=bcs(Bbre), in1=bcp(PWim), op=ALU.mult))
                dv(lambda e: e.tensor_tensor(out=TM16, in0=bcs(Bbim), in1=bcp(PWre), op=ALU.mult))
                dv(lambda e: e.tensor_tensor(out=SWT[:], in0=SWT[:], in1=TM16, op=ALU.add))
            for kk in range(16):
                pb = kk % 2 + 1
                for k in range(4):
                    S.op("pe", lambda e, kk=kk, k=k, pb=pb: e.transpose(
                        out=PS[pb][:, 128 * k:128 * k + 128],
                        in_=SWT[:, kk, 4 * k:4 * k + 4, :].rearrange("p g c -> p (g c)"), identity=ident[:]),
                        reads=[p0, cbuf], writes=[PSB[pb]], sig=(k == 3))
                S.op("act", lambda e, kk=kk, ri=ri, pb=pb: e.activation(
                    out=SW[:, kk, ri, :, :].rearrange("p k c -> p (k c)"), in_=PS[pb][:], func=AF.Copy),
                    reads=[PSB[pb]], writes=[swb])
        dv(lambda e: e.memset(BZre[:], 0.0))
        dv(lambda e: e.memset(BZim[:], 0.0))
        for q in range(4):
            dv(lambda e, q=q: e.tensor_copy(
                out=BZre[:].rearrange("p (k q) c -> p k q c", q=4)[:, :, q, 32 * q:32 * q + 32],
                in_=Bbre[:].rearrange("p (k q) c -> p k q c", q=4)[:, :, q, :]))
            dv(lambda e, q=q: e.tensor_scalar_mul(
                out=BZim[:].rearrange("p (k q) c -> p k q c", q=4)[:, :, q, 32 * q:32 * q + 32],
                in0=Bbim[:].rearrange("p (k q) c -> p k q c", q=4)[:, :, q, :], scalar1=-1.0))
        TZs = self.sb("TZs", [128, 16, 512], BF16, 162 * KB)
        tzb = p0
        TZv = TZd.rearrange("p (l x) -> p l x", l=16)
        for lag in range(16):
            pb = 3 + lag % 2
            for k in range(4):
                for q in range(4):
                    gp = 4 * k + q
                    osl = PS[pb][:, 128 * k + 32 * q:128 * k + 32 * q + 32]
                    S.op("pe", lambda e, osl=osl, gp=gp, lag=lag: e.matmul(
                        osl, lhsT=BZre[:, gp, :], rhs=CAre[:, lag, gp, :], start=True, stop=False, skip_group_check=True),
                        reads=[p0], writes=[PSB[pb]], sig=False)
                    S.op("pe", lambda e, osl=osl, gp=gp, lag=lag: e.matmul(
                        osl, lhsT=BZim[:, gp, :], rhs=CAim[:, lag, gp, :], start=False, stop=True, skip_group_check=True),
                        reads=[p0], writes=[PSB[pb]], sig=(k == 3 and q == 3))
            S.op("act", lambda e, lag=lag, pb=pb: e.activation(out=TZs[:, lag, :], in_=PS[pb][:], func=AF.Copy),
                 reads=[PSB[pb], p0], writes=[tzb])
        S.dma("sp", TZv, TZs[:], reads=[tzb], writes=[p0], sem_buf=tzb)
        S.barrier()

        Wu = self.sb("Wu", [128, 16, 512], BF16, 62 * KB)
        wub = S.buf("Wu")
        xt = [self.sb("xt%d" % i, [128, D], F32, (78 + 8 * i) * KB) for i in range(2)]
        xtb = S.bufs(2, "xt")
        xTs2 = [self.sb("xTs0", [128, 16, 128], F32, 94 * KB)] * 2
        xTb2 = [S.buf("xTs")] * 2
        sq2 = [self.sb("sq0", [128, 16, 128], BF16, 102 * KB), self.sb("sq1", [128, 16, 128], BF16, 196 * KB)]
        sqb2 = S.bufs(2, "sq")
        hTs = [self.sb("hTs%d" % i, [128, 16, 512], BF16, (106 + 16 * i) * KB) for i in range(2)]
        hTb = S.bufs(2, "hTs")
        ub = self.sb("ubatch", [128, 4, 1024], BF16, 138 * KB)
        ubb = S.buf("ubatch")
        Ssb2 = [self.sb("Ssb%d" % i, [128, 64, 2, 16], BF16, (146 + 4 * i) * KB) for i in range(2)]
        Ssbb2 = S.bufs(2, "Ssb")
        s4q = [self.sb("s4q%d" % i, [128, 2, 4, 256], BF16, (20 + 4 * i) * KB) for i in range(2)] + \
              [self.sb("s4q%d" % (2 + i), [128, 2, 4, 256], BF16, (12 + 4 * i) * KB) for i in range(2)]
        s4qb = S.bufs(4, "s4q")
        Yh = self.sb("Yh", [128, 2, 4, 64], F32, 28 * KB)
        HT4 = [self.sb("HT4_%d" % i, [128, 256], F32, (200 + i) * KB) for i in range(4)]
        Zhs = [self.sb("Zh0", [128, 2, 16, 16], F32, 204 * KB), self.sb("Zh1", [128, 2, 16, 16], F32, 8 * KB)]
        zhb = S.bufs(2, "Zh")
        hb = S.buf("horner")

        def cma(eng, o_re, o_im, x_re, x_im, a_re, a_im, s_re, s_im, T, rb, wb):
            t1, t2, t3, t4 = T
            f = lambda fn, r=rb, w=wb: S.op(eng, fn, reads=list(r), writes=list(w))
            f(lambda e: e.tensor_tensor(out=t1, in0=a_re, in1=x_re, op=ALU.mult))
            f(lambda e: e.tensor_tensor(out=t2, in0=a_im, in1=x_im, op=ALU.mult))
            f(lambda e: e.tensor_tensor(out=t3, in0=a_re, in1=x_im, op=ALU.mult))
            f(lambda e: e.tensor_tensor(out=t4, in0=a_im, in1=x_re, op=ALU.mult))
            f(lambda e: e.tensor_tensor(out=t1, in0=t1, in1=t2, op=ALU.subtract))
            f(lambda e: e.tensor_tensor(out=t3, in0=t3, in1=t4, op=ALU.add))
            f(lambda e: e.tensor_tensor(out=o_re, in0=t1, in1=s_re, op=ALU.add))
            f(lambda e: e.tensor_tensor(out=o_im, in0=t3, in1=s_im, op=ALU.add))
        um = self.sb("u_main", [128, 4, NMAIN], BF16, 154 * KB)
        umb = S.buf("u_main")
        Xs = self.sb("Xs", [128, 2, 16, NCHM], BF16, 172 * KB)
        Xsb = S.buf("Xs")
        rsA = [self.sb("rsA%d" % i, [128, 128], F32, 181 * KB + 1024 * i) for i in range(2)]
        rsB = [self.sb("rsB%d" % i, [128, 128], F32, 181 * KB + 1024 * i + 512) for i in range(2)]
        rsb2 = S.bufs(2, "rs")
        xtmp = self.sb("xtmp", [128, 16, 128], F32, 184 * KB)
        xtmpb = S.buf("xtmp")
        mk = [self.sb("mk%d" % i, [128, 512], F32, (192 + 2 * i) * KB) for i in range(2)]
        mkb = S.bufs(2, "mk")
        S.dma("pool", Wu[:], w_in[:, 1280:1792].rearrange("(kt p) c -> p kt c", p=128), writes=[wub], sem_buf=wub)
        stb = S.buf("state")
        S.op("pool", lambda e: e.memset(X4[0][:], 0.0), writes=[stb])
        S.op("pool", lambda e: e.memset(X4[1][:], 0.0), writes=[stb])
        cur = 0
        XTv = XT.rearrange("k p c -> p k c")
        HTv = HT.rearrange("k p c -> p k c")
        deferred = {}

        def ph1_load(b_):
            S.dma("sp", xt[b_ % 2][:], xw[b_ * 128:(b_ + 1) * 128, :], writes=[xtb[b_ % 2]], sem_buf=xtb[b_ % 2])

        junk = [self.sb("junk0", [128, D], BF16, 102 * KB)] * 2
        junkb = [S.buf("junk")] * 2
        HT4b = [self.sb("HT4b_%d" % i, [128, 256], F32, (196 + i) * KB) for i in range(4)]
        Yhb = self.sb("Yhb", [128, 2, 4, 64], F32, 10 * KB)
        hb2 = S.buf("horner2")
        xsb = [self.sb("xsb%d" % i, [128, D], BF16, (184 + 4 * i) * KB) for i in range(2)]
        xsbb = S.bufs(2, "xsb")
        ssqs = [self.sb("ssq%d" % i, [128, 1], F32, 183 * KB + 64 * i) for i in range(2)]
        ssqb = S.bufs(2, "ssq")

        def ph1_sq(b_):
            ii = b_ % 2
            S.op("act", lambda e, ii=ii: e.activation(out=junk[ii][:], in_=xt[ii][:], func=AF.Square, accum_out=ssqs[ii][:]),
                 reads=[xtb[ii]], writes=[junkb[ii], ssqb[ii]])

        def ph1_transposes(b_):
            ii = b_ % 2
            for j in range(4):
                for kk in range(4):
                    kt = 4 * j + kk
                    S.op("pe", lambda e, j=j, kk=kk, kt=kt, ii=ii: e.transpose(
                        out=PS[j][:, 128 * kk:128 * kk + 128], in_=xt[ii][:, 128 * kt:128 * kt + 128], identity=ident[:]),
                        reads=[xtb[ii], cbuf], writes=[PSB[j]], sig=(kk == 3))

        for blk in range(NBLK):
            i2 = blk % 2
            sub = (blk // 4) % 2
            tcol = (blk % 4) * 128
            xTs, xTb = xTs2[i2], xTb2[i2]
            rs1, rs2, rsb = rsA[i2], rsB[i2], rsb2[i2]
            if blk == 0:
                ph1_load(0)
                ph1_sq(0)
            if blk % 4 == 0:
                mi = (blk // 4) % 2
                S.dma("sp", mk[mi][:], maskw[blk * 128:blk * 128 + 512].partition_broadcast(128), writes=[mkb[mi]],
                      sem_buf=mkb[mi], nonc=True)
            if blk + 1 < NBLK:
                ph1_load(blk + 1)
            S.op("dve", lambda e, rs1=rs1, i2=i2: e.tensor_scalar(out=rs1[:, 0:1], in0=ssqs[i2][:], scalar1=1.0 / D, scalar2=1e-6,
                                                                   op0=ALU.mult, op1=ALU.add), reads=[ssqb[i2]], writes=[rsb])
            S.op("act", lambda e, rs1=rs1, rs2=rs2: e.activation(out=rs2[:, 0:1], in_=rs1[:, 0:1], func=AF.Sqrt), reads=[rsb], writes=[rsb])
            S.op("dve", lambda e, rs1=rs1, rs2=rs2: e.reciprocal(out=rs1[:, 0:1], in_=rs2[:, 0:1]), reads=[rsb], writes=[rsb])
            S.op("act", lambda e, rs1=rs1, i2=i2: e.activation(out=xsb[i2][:], in_=xt[i2][:], func=AF.Copy, scale=rs1[:, 0:1]),
                 reads=[xtb[i2], rsb], writes=[xsbb[i2]])
            if blk + 1 < NBLK:
                ph1_sq(blk + 1)
            if blk >= MB0:
                for j in range(4):
                    pbk = 2 + j % 2
                    for kk in range(4):
                        kt = 4 * j + kk
                        S.op("pe", lambda e, pbk=pbk, kk=kk, kt=kt, i2=i2: e.transpose(
                            out=PS[pbk][:, 128 * kk:128 * kk + 128], in_=xt[i2][:, 128 * kt:128 * kt + 128], identity=ident[:]),
                            reads=[xtb[i2], cbuf], writes=[PSB[pbk]], sig=(kk == 3))
                    S.op("act", lambda e, j=j, pbk=pbk, xTs=xTs: e.activation(
                        out=xTs[:, 4 * j:4 * j + 4, :].rearrange("p k c -> p (k c)"), in_=PS[pbk][:], func=AF.Copy),
                        reads=[PSB[pbk]], writes=[xTb])
                col = (blk - MB0) * 128
                S.dma("sp", XTv[:, :, col:col + 128], xTs[:], reads=[xTb], writes=[], sem_buf=xTb)
            for j in range(2):
                psv = PS[j][:].bitcast(BF16)
                for kk in range(8):
                    kt = 8 * j + kk
                    S.op("pe", lambda e, psv=psv, kk=kk, kt=kt, i2=i2: e.transpose(
                        out=psv[:, 128 * kk:128 * kk + 128], in_=xsb[i2][:, 128 * kt:128 * kt + 128], identity=identb[:]),
                        reads=[xsbb[i2], cbuf2], writes=[PSB[j]], sig=(kk == 7))
                S.op("dve", lambda e, psv=psv, j=j, sub=sub, tcol=tcol: e.tensor_tensor(
                    out=hTs[sub][:, 8 * j:8 * j + 8, tcol:tcol + 128], in0=psv[:, 0:1024].rearrange("p (k c) -> p k c", k=8),
                    in1=g1c[:, 8 * j:8 * j + 8].unsqueeze(2).to_broadcast([128, 8, 128]), op=ALU.mult),
                    reads=[PSB[j], cbuf], writes=[hTb[sub]])
            for fn_ in deferred.pop(blk, []):
                fn_()
            if blk >= MB0:
                S.dma("sp", HTv[:, :, col:col + 128], hTs[sub][:, :, tcol:tcol + 128], reads=[hTb[sub]], writes=[],
                      sem_buf=hTb[sub])
            if blk % 4 == 3:
                mi = (blk // 4) % 2
                boff = ((blk // 4) % 2) * 512
                for m in range(4):
                    pb = 5
                    for kt in range(16):
                        S.op("pe", lambda e, m=m, kt=kt, sub=sub, pb=pb: e.matmul(
                            PS[pb][:], lhsT=Wu[:, kt, 128 * m:128 * m + 128], rhs=hTs[sub][:, kt, :],
                            start=(kt == 0), stop=(kt == 15)),
                            reads=[wub, hTb[sub]], writes=[PSB[pb]], sig=(kt == 15))
                    S.op("dve", lambda e, m=m, pb=pb, mi=mi, boff=boff: e.scalar_tensor_tensor(
                        out=ub[:, m, boff:boff + 512], in0=PS[pb][:], scalar=binc[:, 10 + m:11 + m], in1=mk[mi][:],
                        op0=ALU.add, op1=ALU.mult), reads=[PSB[pb], mkb[mi], cbuf], writes=[ubb])
                    if blk >= MB0 - 2:
                        b0 = blk - 3
                        lo = max(b0, MB0)
                        n = (blk + 1 - lo) * 128
                        so = boff + (lo - b0) * 128
                        do = (lo - MB0) * 128
                        S.op("act", lambda e, m=m, so=so, do=do, n=n: e.activation(
                            out=um[:, m, do:do + n], in_=ub[:, m, so:so + n], func=AF.Copy), reads=[ubb], writes=[umb])
            if blk % 8 == 7:
                bi = blk // 8
                Ssb, Ssbb = Ssb2[bi % 2], Ssbb2[bi % 2]
                Zh, zb = Zhs[bi % 2], zhb[bi % 2]
                def level01(q, bi=bi, Ssb=Ssb, Ssbb=Ssbb):
                    s4, s4b = s4q[q], s4qb[q]
                    for k in range(4):
                        pq = 6 + k % 2
                        for ri in range(2):
                            for b4 in range(4):
                                S.op("pe", lambda e, q=q, k=k, ri=ri, b4=b4, pq=pq: e.matmul(
                                    PS[pq][:, 256 * ri:256 * ri + 256], lhsT=SW[32 * q:32 * q + 32, 3 - b4, ri, k, :],
                                    rhs=ub[32 * q:32 * q + 32, k, b4:1024:4], start=(b4 == 0), stop=(b4 == 3),
                                    tile_position=(32 * q, 0), skip_group_check=True),
                                    reads=[swb, ubb], writes=[PSB[pq]], sig=(ri == 1 and b4 == 3))
                        S.op("act", lambda e, k=k, pq=pq, s4=s4: e.activation(
                            out=s4[:, :, k, :], in_=PS[pq][:].rearrange("p (r c) -> p r c", r=2), func=AF.Copy),
                            reads=[PSB[pq]], writes=[s4b])
                    s4v = s4[:].rearrange("p r k (c a) -> p r k c a", a=4)
                    a4r = A4re[:].rearrange("p (k q) -> p q k", q=4)[:, q, :].unsqueeze(2).to_broadcast([128, 4, 64])
                    a4i = A4im[:].rearrange("p (k q) -> p q k", q=4)[:, q, :].unsqueeze(2).to_broadcast([128, 4, 64])
                    onp = True
                    en_ = "pool" if onp else "dve"
                    Y_ = Yh if onp else Yhb
                    hb_ = hb if onp else hb2
                    tq = [t[:].rearrange("p (k c) -> p k c", k=4) for t in (HT4 if onp else HT4b)]
                    Sq = Ssb[:].rearrange("p c r (k q) -> p r q k c", q=4)
                    cma(en_, Y_[:, 0], Y_[:, 1], s4v[:, 0, :, :, 0], s4v[:, 1, :, :, 0], a4r, a4i,
                        s4v[:, 0, :, :, 1], s4v[:, 1, :, :, 1], tq, [s4b, hb_, cbuf], [hb_])
                    cma(en_, Y_[:, 0], Y_[:, 1], Y_[:, 0], Y_[:, 1], a4r, a4i,
                        s4v[:, 0, :, :, 2], s4v[:, 1, :, :, 2], tq, [s4b, hb_, cbuf], [hb_])
                    cma(en_, Sq[:, 0, q], Sq[:, 1, q], Y_[:, 0], Y_[:, 1], a4r, a4i,
                        s4v[:, 0, :, :, 3], s4v[:, 1, :, :, 3], tq, [s4b, hb_, cbuf], [hb_, Ssbb])

                def batch_tail(bi=bi, Ssb=Ssb, Ssbb=Ssbb, Zh=Zh, zb=zb):
                    nonlocal cur
                    if bi < 5:
                        Sv = Ssb[:].rearrange("p (m a) r g -> p a r m g", a=4)
                        a16r = A16re[:].unsqueeze(1).to_broadcast([128, 16, 16])
                        a16i = A16im[:].unsqueeze(1).to_broadcast([128, 16, 16])
                        tz = [t[:].rearrange("p (m g) -> p m g", m=16) for t in HT4]
                        cma("pool", Zh[:, 0], Zh[:, 1], Sv[:, 0, 0], Sv[:, 0, 1], a16r, a16i, Sv[:, 1, 0], Sv[:, 1, 1], tz,
                            [Ssbb, hb, cbuf], [hb, zb])
                        cma("pool", Zh[:, 0], Zh[:, 1], Zh[:, 0], Zh[:, 1], a16r, a16i, Sv[:, 2, 0], Sv[:, 2, 1], tz,
                            [Ssbb, hb, cbuf, zb], [hb, zb])
                        cma("pool", Zh[:, 0], Zh[:, 1], Zh[:, 0], Zh[:, 1], a16r, a16i, Sv[:, 3, 0], Sv[:, 3, 1], tz,
                            [Ssbb, hb, cbuf, zb], [hb, zb])
                        for m in range(16):
                            xa, xb = X4[cur], X4[1 - cur]
                            pls = lambda fn, r=(stb, cbuf), w=(stb,): S.op("pool", fn, reads=list(r), writes=list(w))
                            pls(lambda e, xa=xa: e.tensor_tensor(out=P1[:], in0=AA64[:], in1=xa[:, 0:2, :], op=ALU.mult))
                            pls(lambda e, xa=xa: e.tensor_tensor(out=P2[:], in0=BB64[:], in1=xa[:, 1:3, :], op=ALU.mult))
                            pls(lambda e: e.tensor_tensor(out=P1[:], in0=P1[:], in1=P2[:], op=ALU.add))
                            S.op("pool", lambda e, xb=xb, m=m, Zh=Zh: e.tensor_tensor(out=xb[:, 0:2, :], in0=P1[:], in1=Zh[:, :, m, :], op=ALU.add),
                                 reads=[stb, zb], writes=[stb])
                            pls(lambda e, xb=xb: e.tensor_copy(out=xb[:, 2, :], in_=xb[:, 0, :]))
                            cur = 1 - cur
                    else:
                        jb = bi * 64
                        for jj in range(64):
                            j = jb + jj
                            xa, xb = X4[cur], X4[1 - cur]
                            if j >= NCH0:
                                S.op("pool", lambda e, xa=xa, j=j: e.tensor_copy(out=Xs[:, :, :, j - NCH0], in_=xa[:, 0:2, :]),
                                     reads=[stb], writes=[Xsb])
                            pls = lambda fn, r=(stb, cbuf), w=(stb,): S.op("pool", fn, reads=list(r), writes=list(w))
                            pls(lambda e, xa=xa: e.tensor_tensor(out=P1[:], in0=AAt[:], in1=xa[:, 0:2, :], op=ALU.mult))
                            pls(lambda e, xa=xa: e.tensor_tensor(out=P2[:], in0=BBt[:], in1=xa[:, 1:3, :], op=ALU.mult))
                            pls(lambda e: e.tensor_tensor(out=P1[:], in0=P1[:], in1=P2[:], op=ALU.add))
                            S.op("pool", lambda e, xb=xb, jj=jj, Ssb=Ssb: e.tensor_tensor(out=xb[:, 0:2, :], in0=P1[:], in1=Ssb[:, jj, :, :], op=ALU.add),
                                 reads=[stb, Ssbb], writes=[stb])
                            pls(lambda e, xb=xb: e.tensor_copy(out=xb[:, 2, :], in_=xb[:, 0, :]))
                            cur = 1 - cur
                level01(0)
                level01(1)
                if blk + 2 < NBLK:
                    deferred.setdefault(blk + 1, []).append(lambda f=level01: f(2))
                    deferred.setdefault(blk + 2, []).append(lambda f=level01: f(3))
                    deferred.setdefault(blk + 2, []).append(batch_tail)
                else:
                    level01(2)
                    level01(3)
                    batch_tail()
        if UM is not None:
            S.dma("sp", UM.rearrange("k p c -> p k c"), um[:], reads=[umb], writes=[], sem_buf=umb)
        S.barrier()
        finals = []
        if UM is not None:
            finals.append(umb)
        if self.stop == "ph1":
            return S, finals
        FF = self.sb("FF", [128, 16, 2, 16, 32], BF16, 62 * KB)
        TZ = self.sb("TZ", [128, 16, 4, 128], BF16, 94 * KB)
        ffb2 = S.buf("FF")
        tzb2 = S.buf("TZ")
        S.dma("sp", FF[:].rearrange("p t r g c -> p (t r g c)"), FFd, writes=[ffb2], sem_buf=ffb2)
        S.dma("sp", TZ[:].rearrange("p l k c -> p (l k c)"), TZd, writes=[tzb2], sem_buf=tzb2)
        yg = self.sb("yg", [128, 4, NMAIN], BF16, 12 * KB)
        ygb = S.buf("yg")
        ytmp = [self.sb("ytmp%d" % i, [128, NCHM], F32, 110 * KB + i * 1024) for i in range(2)]
        ytb = S.bufs(2, "ytmp")
        it = 0
        for k in range(4):
            for tau in range(16):
                pb = it % 4
                yi = it % 2
                it += 1
                for s in range(tau + 1):
                    S.op("pe", lambda e, pb=pb, tau=tau, s=s, k=k: e.matmul(
                        PS[pb][:, 0:NCHM], lhsT=TZ[:, tau - s, k, :], rhs=um[:, k, s:NMAIN:16],
                        start=(s == 0), stop=False, skip_group_check=True),
                        reads=[tzb2, umb], writes=[PSB[pb]], sig=False)
                for q in range(4):
                    for ri in range(2):
                        last = (q == 3 and ri == 1)
                        S.op("pe", lambda e, pb=pb, tau=tau, q=q, ri=ri, k=k, last=last: e.matmul(
                            PS[pb][32 * q:32 * q + 32, 0:NCHM], lhsT=FF[:, tau, ri, 4 * k + q, :], rhs=Xs[:, ri, 4 * k + q, :],
                            start=False, stop=last, tile_position=(0, 32 * q), skip_group_check=True),
                            reads=[ffb2, Xsb], writes=[PSB[pb]], sig=last)
                S.op("dve", lambda e, pb=pb, tau=tau, k=k, yi=yi: e.scalar_tensor_tensor(
                    out=ytmp[yi][:], in0=um[:, k, tau:NMAIN:16], scalar=dcol[:, k:k + 1], in1=PS[pb][:, 0:NCHM],
                    op0=ALU.mult, op1=ALU.add), reads=[PSB[pb], umb, cbuf], writes=[ytb[yi]])
                S.op("act", lambda e, tau=tau, k=k, yi=yi: e.activation(
                    out=yg[:, k, tau:NMAIN:16], in_=ytmp[yi][:], func=AF.Gelu), reads=[ytb[yi]], writes=[ygb])
        if YG is not None:
            S.dma("sp", YG.rearrange("k p c -> p k c"), yg[:], reads=[ygb], writes=[], sem_buf=ygb)
            finals.append(ygb)
        S.barrier()
        if self.stop == "ph1b":
            return S, finals
        self.pbn = 0

        def nb():
            self.pbn = (self.pbn + 1) % 8
            return self.pbn

        MT = [(0, 512), (512, 512), (1024, 512), (1536, 512), (2048, 256)]
        VT = [(128, 512), (640, 512), (1152, 512), (1664, 512), (2176, 128)]

        def loadw(eng, dst, dbuf, src, nkt):
            S.dma(eng, dst[:, 0:nkt, :], src.rearrange("(kt p) c -> p kt c", p=128), writes=[dbuf], sem_buf=dbuf)

        SG = self.dscr("SG", [32, 128, NMAIN], BF16)
        X1T = self.dscr("X1T", [16, 128, NMAIN], F32)
        ACTd = self.dscr("ACTd", [44, 128, 2048], BF16)
        X2T = self.dscr("X2T", [16, 128, 2048], F32)
        qT = self.sb("qT", [128, 8, NMAIN], BF16, 30 * KB)
        qb = S.buf("qT")
        kd = [self.sb("kd%d" % g, [128, NMAIN], BF16, 66 * KB + g * 4608) for g in range(2)]
        kdb = S.bufs(2, "kd")
        hTm = self.sb("hTm", [128, 16, NMAIN], BF16, 75 * KB)
        hTmb = S.buf("hTm")
        wt = [self.sb("wt%d" % i, [128, 16, 128], BF16, (147 + 4 * i) * KB) for i in range(2)]
        wtb = S.bufs(2, "wt")
        vsb = self.sb("vsb", [128, 18, 128], BF16, 155 * KB)
        vsbb = S.buf("vsb")
        gst = [self.sb("gst%d" % i, [128, 512], BF16, (160 + i) * KB) for i in range(2)]
        gstb = S.bufs(2, "gst")
        bvb = self.sb("bvb", [128, 128], F32, 162 * KB)
        bkd = self.sb("bkd", [128, 2], F32, 163 * KB)
        sinkc = self.sb("sinkc", [128, 16], F32, 163 * KB + 64)
        NM = self.sb("NM", [128, 128], F32, 164 * KB)
        c2b = S.buf("c2")
        S.dma("sp", hTm[:], HT.rearrange("k p c -> p k c"), writes=[hTmb], sem_buf=hTmb)
        S.dma("sp", bvb[:], b_in[1152:1280].partition_broadcast(128), writes=[c2b], sem_buf=c2b, nonc=True)
        for g in range(2):
            for hh in range(2):
                S.dma("sp", bkd[64 * hh:64 * hh + 64, g:g + 1], b_in[1024 + 64 * g:1088 + 64 * g].rearrange("(p o) -> p o", o=1),
                      writes=[c2b], sem_buf=c2b, nonc=True)
        S.dma("sp", sinkc[:], sinks.partition_broadcast(128), writes=[c2b], sem_buf=c2b, nonc=True)
        S.dma("sp", NM[:], maskw[MB0 * 128 + 128:MB0 * 128 + 256].partition_broadcast(128), writes=[c2b], sem_buf=c2b, nonc=True)
        S.op("dve", lambda e: e.tensor_scalar(out=NM[:], in0=NM[:], scalar1=-1.0, scalar2=30000.0, op0=ALU.add, op1=ALU.mult),
             reads=[c2b], writes=[c2b])
        wi = 0
        jobs = [("q", cb) for cb in range(8)] + [("k", g) for g in range(2)] + [("g", cb) for cb in range(14, 46)]
        gi = 0
        for kind, cb in jobs:
            w = wt[wi % 2]
            wb = wtb[wi % 2]
            wi += 1
            if kind == "k":
                for hh in range(2):
                    S.dma("pool", w[:, :, 64 * hh:64 * hh + 64],
                          w_in[:, 1024 + 64 * cb:1088 + 64 * cb].rearrange("(kt p) c -> p kt c", p=128), writes=[wb], sem_buf=wb)
            else:
                loadw("pool", w, wb, w_in[:, 128 * cb:128 * cb + 128], 16)
            for (c0, n) in (MT if kind == "k" else VT):
                pb = nb()
                for kt in range(16):
                    S.op("pe", lambda e, pb=pb, w=w, kt=kt, c0=c0, n=n: e.matmul(
                        PS[pb][:, 0:n], lhsT=w[:, kt, :], rhs=hTm[:, kt, c0:c0 + n], start=(kt == 0), stop=(kt == 15)),
                        reads=[wb, hTmb], writes=[PSB[pb]], sig=(kt == 15))
                if kind == "q":
                    S.op("act", lambda e, pb=pb, cb=cb, c0=c0, n=n: e.activation(
                        out=qT[:, cb, c0:c0 + n], in_=PS[pb][:, 0:n], func=AF.Identity, bias=binc[:, cb:cb + 1]),
                        reads=[PSB[pb], cbuf], writes=[qb])
                elif kind == "k":
                    S.op("act", lambda e, pb=pb, cb=cb, c0=c0, n=n: e.activation(
                        out=kd[cb][:, c0:c0 + n], in_=PS[pb][:, 0:n], func=AF.Identity, bias=bkd[:, cb:cb + 1]),
                        reads=[PSB[pb], c2b], writes=[kdb[cb]])
                else:
                    gs_ = gst[gi % 2]
                    gsb = gstb[gi % 2]
                    gi += 1
                    S.op("act", lambda e, pb=pb, cb=cb, n=n, gs_=gs_: e.activation(
                        out=gs_[:, 0:n], in_=PS[pb][:, 0:n], func=AF.Sigmoid, bias=binc[:, cb:cb + 1]),
                        reads=[PSB[pb], cbuf], writes=[gsb])
                    S.dma("sp", SG[cb - 14][:, c0:c0 + n], gs_[:, 0:n], reads=[gsb], writes=[], sem_buf=gsb)
        w = wt[wi % 2]
        wb = wtb[wi % 2]
        wi += 1
        loadw("pool", w, wb, w_in[:, 1152:1280], 16)
        for mb in range(18):
            pb = nb()
            for kt in range(16):
                S.op("pe", lambda e, pb=pb, w=w, kt=kt, mb=mb: e.matmul(
                    PS[pb][:, 0:128], lhsT=hTm[:, kt, 128 * mb:128 * mb + 128], rhs=w[:, kt, :], start=(kt == 0), stop=(kt == 15)),
                    reads=[wb, hTmb], writes=[PSB[pb]], sig=(kt == 15))
            S.op("dve", lambda e, pb=pb, mb=mb: e.tensor_tensor(out=vsb[:, mb, :], in0=PS[pb][:, 0:128], in1=bvb[:], op=ALU.add),
                 reads=[PSB[pb], c2b], writes=[vsbb])
        S.barrier()
        if self.stop == "ph2":
            return S, finals
        attnT = self.sb("attnT", [128, 8, NMAIN], BF16, 75 * KB)
        atb = S.buf("attnT")
        atbs = S.bufs(8, "attnTk")
        AB = self.sb("AB", [128, 16, 256], F32, 111 * KB)
        abb = S.buf("AB")
        scS = [self.sb("sc0", [128, 16, 256], F32, 127 * KB), self.sb("sc1", [128, 16, 256], F32, 174 * KB)]
        scbS = [S.bufs(16, "sc0_"), S.bufs(16, "sc1_")]
        pnS = [self.sb("pn0", [128, 16, 256], BF16, 143 * KB), self.sb("pn1", [128, 16, 256], BF16, 190 * KB)]
        pnbS = [S.bufs(16, "pn0_"), S.bufs(16, "pn1_")]
        pTsS = [self.sb("pTs0", [128, 16, 256], BF16, 165 * KB), self.sb("pTs1", [128, 16, 256], BF16, 198 * KB)]
        pTbS = [S.bufs(16, "pT0_"), S.bufs(16, "pT1_")]
        vecS = []
        for i in range(2):
            vo = (173 if i == 0 else 151) * KB
            vecS.append([self.sb("av%d_%d" % (i, t), [128, 16], F32, vo + 64 * t) for t in range(5)])
        vbS = S.bufs(2, "attvec")
        S.dma("sp", AB[:], abias, writes=[abb], sem_buf=abb)
        def att_setup(mb):
            c0 = 128 * mb
            return (c0,) + (scS[mb % 2], scbS[mb % 2], pnS[mb % 2], pnbS[mb % 2], pTsS[mb % 2], pTbS[mb % 2]) + tuple(vecS[mb % 2]) + (vbS[mb % 2],)

        def stageA(mb):
            c0, sc, scb, pn, pnb, pTs, pTb, mx, nmx, rsum, es, rden, vb_ = att_setup(mb)
            for base in (0, 4, 8, 12):
                for par in (0, 1):
                    h0 = base + par
                    g = h0 // 8
                    r0 = 64 * par
                    pb = nb()
                    for i_, h in enumerate((h0, h0 + 2)):
                        S.op("pe", lambda e, pb=pb, h=h, g=g, r0=r0, i_=i_: e.matmul(
                            PS[pb][:, 256 * i_:256 * i_ + 256], lhsT=qT[r0:r0 + 64, h // 2, c0:c0 + 128],
                            rhs=kd[g][r0:r0 + 64, c0 - 128:c0 + 128], start=True, stop=True, skip_group_check=True),
                            reads=[qb, kdb[g]], writes=[PSB[pb]], sig=(i_ == 1))
                    S.op("dve", lambda e, pb=pb, h0=h0: e.scalar_tensor_tensor(
                        out=sc[:, h0:h0 + 3:2, :], in0=PS[pb][:].rearrange("p (h c) -> p h c", h=2), scalar=0.125,
                        in1=AB[:, h0:h0 + 3:2, :], op0=ALU.mult, op1=ALU.add),
                        reads=[PSB[pb], abb], writes=[scb[h0], scb[h0 + 2]])
                    if mb == 2:
                        for h in (h0, h0 + 2):
                            S.op("dve", lambda e, h=h: e.tensor_tensor(out=sc[:, h, 0:128], in0=sc[:, h, 0:128], in1=NM[:], op=ALU.add),
                                 reads=[scb[h], c2b], writes=[scb[h]])
                for h in (base + 1, base + 3):
                    S.op("dve", lambda e, h=h: e.reduce_max(out=mx[:, h - 1:h + 1], in_=sc[:, h - 1:h + 1, :],
                                                             axis=mybir.AxisListType.X),
                         reads=[scb[h - 1], scb[h]], writes=[vb_])
            S.op("dve", lambda e, mx=mx: e.tensor_tensor(out=mx[:], in0=mx[:], in1=sinkc[:], op=ALU.max), reads=[vb_, c2b], writes=[vb_])
            S.op("dve", lambda e, mx=mx, nmx=nmx: e.tensor_scalar_mul(out=nmx[:], in0=mx[:], scalar1=-1.0), reads=[vb_], writes=[vb_])
            S.op("dve", lambda e, es=es, nmx=nmx: e.tensor_tensor(out=es[:], in0=sinkc[:], in1=nmx[:], op=ALU.add), reads=[vb_, c2b], writes=[vb_])
            S.op("act", lambda e, es=es: e.activation(out=es[:], in_=es[:], func=AF.Exp), reads=[vb_], writes=[vb_])
            for h in range(16):
                S.op("act", lambda e, h=h, sc=sc, nmx=nmx, rsum=rsum: e.activation(out=sc[:, h, :], in_=sc[:, h, :], func=AF.Exp, bias=nmx[:, h:h + 1],
                                                          accum_out=rsum[:, h:h + 1]), reads=[scb[h], vb_], writes=[scb[h], vb_])

        def stageB(mb):
            c0, sc, scb, pn, pnb, pTs, pTb, mx, nmx, rsum, es, rden, vb_ = att_setup(mb)
            S.op("dve", lambda e, rden=rden, rsum=rsum, es=es: e.tensor_tensor(out=rden[:], in0=rsum[:], in1=es[:], op=ALU.add), reads=[vb_], writes=[vb_])
            S.op("dve", lambda e, rden=rden: e.reciprocal(out=rden[:], in_=rden[:]), reads=[vb_], writes=[vb_])
            banks = {}

            def b_first(hp):
                pb = nb()
                banks[hp] = pb
                psv = PS[pb][:].bitcast(BF16)
                S.op("dve", lambda e, hp=hp: e.tensor_tensor(
                    out=pn[:, 2 * hp:2 * hp + 2, :], in0=sc[:, 2 * hp:2 * hp + 2, :],
                    in1=rden[:, 2 * hp:2 * hp + 2].unsqueeze(2).to_broadcast([128, 2, 256]), op=ALU.mult),
                    reads=[scb[2 * hp], scb[2 * hp + 1], vb_], writes=[pnb[2 * hp], pnb[2 * hp + 1]])
                for h in (2 * hp, 2 * hp + 1):
                    for hf in range(2):
                        o0 = 256 * (h % 2) + 128 * hf
                        S.op("pe", lambda e, psv=psv, h=h, hf=hf, o0=o0: e.transpose(
                            out=psv[:, o0:o0 + 128], in_=pn[:, h, 128 * hf:128 * hf + 128], identity=identb[:]),
                            reads=[pnb[h], cbuf2], writes=[PSB[pb]], sig=(h % 2 == 1 and hf == 1))
                S.op("act", lambda e, psv=psv, hp=hp: e.activation(
                    out=pTs[:, 2 * hp:2 * hp + 2, :].rearrange("p h c -> p (h c)"), in_=psv[:, 0:512], func=AF.Copy),
                    reads=[PSB[pb]], writes=[pTb[2 * hp]])

            def b_second(hp):
                pb2 = nb()
                for h in (2 * hp, 2 * hp + 1):
                    g = h // 8
                    r0 = 64 * (h % 2)
                    for hf in range(2):
                        S.op("pe", lambda e, pb2=pb2, h=h, hf=hf, g=g, r0=r0: e.matmul(
                            PS[pb2][r0:r0 + 64, 0:128], lhsT=vsb[:, mb - 1 + hf, 64 * g:64 * g + 64],
                            rhs=pTs[:, h, 128 * hf:128 * hf + 128], start=(hf == 0), stop=(hf == 1), skip_group_check=True),
                            reads=[vsbb, pTb[2 * hp]], writes=[PSB[pb2]], sig=(h % 2 == 1 and hf == 1))
                S.op("act", lambda e, pb2=pb2, hp=hp: e.activation(
                    out=attnT[:, hp, c0:c0 + 128], in_=PS[pb2][:, 0:128], func=AF.Copy),
                    reads=[PSB[pb2]], writes=[atbs[hp]])

            for i in range(9):
                if i < 8:
                    b_first(i)
                if i >= 1:
                    b_second(i - 1)

        stageA(1)
        for mb in range(1, 18):
            if mb + 1 < 18:
                stageA(mb + 1)
            stageB(mb)
        if "ATT" in dbg:
            ATT = self.dscr("ATT", [8, 128, NMAIN], BF16)
            S.dma("sp", ATT.rearrange("k p c -> p k c"), attnT[:], reads=[atb] + atbs, writes=[], sem_buf=atb)
            finals.append(atb)
        S.barrier()
        if self.stop == "ph3":
            return S, finals
        ssmT = self.sb("ssmT", [128, 4, NMAIN], BF16, 30 * KB)
        ssb = S.buf("ssmT")
        wg2 = [self.sb("wg2_%d" % i, [128, 4, 256], BF16, (48 + 2 * i) * KB) for i in range(2)]
        wg2b = S.bufs(2, "wg2")
        sgt = [self.sb("sgt%d" % i, [128, 512], F32, (52 + 2 * i) * KB) for i in range(2)]
        sgtb = S.bufs(2, "sgt")
        it = 0
        for c in range(4):
            w = wg2[c % 2]
            wb = wg2b[c % 2]
            S.dma("pool", w[:, :, 0:128], w_glu[:, 128 * c:128 * c + 128].rearrange("(kt p) c -> p kt c", p=128), writes=[wb], sem_buf=wb)
            S.dma("pool", w[:, :, 128:256], w_glu[:, 512 + 128 * c:640 + 128 * c].rearrange("(kt p) c -> p kt c", p=128),
                  writes=[wb], sem_buf=wb)
            for (c0, n) in VT:
                pv, pg = nb(), nb()
                for kt in range(4):
                    S.op("pe", lambda e, pv=pv, w=w, kt=kt, c0=c0, n=n: e.matmul(
                        PS[pv][:, 0:n], lhsT=w[:, kt, 0:128], rhs=yg[:, kt, c0:c0 + n], start=(kt == 0), stop=(kt == 3)),
                        reads=[wb, ygb], writes=[PSB[pv]], sig=(kt == 3))
                for kt in range(4):
                    S.op("pe", lambda e, pg=pg, w=w, kt=kt, c0=c0, n=n: e.matmul(
                        PS[pg][:, 0:n], lhsT=w[:, kt, 128:256], rhs=yg[:, kt, c0:c0 + n], start=(kt == 0), stop=(kt == 3)),
                        reads=[wb, ygb], writes=[PSB[pg]], sig=(kt == 3))
                si = it % 2
                it += 1
                S.op("act", lambda e, pg=pg, c=c, n=n, si=si: e.activation(
                    out=sgt[si][:, 0:n], in_=PS[pg][:, 0:n], func=AF.Sigmoid, bias=bgluc[:, 4 + c:5 + c]),
                    reads=[PSB[pg], cbuf], writes=[sgtb[si]])
                S.op("dve", lambda e, pv=pv, c=c, c0=c0, n=n, si=si: e.scalar_tensor_tensor(
                    out=ssmT[:, c, c0:c0 + n], in0=PS[pv][:, 0:n], scalar=bgluc[:, c:c + 1], in1=sgt[si][:, 0:n],
                    op0=ALU.add, op1=ALU.mult), reads=[PSB[pv], sgtb[si], cbuf], writes=[ssb])
        S.barrier()
        mgT = self.sb("mgT", [128, 16, NMAIN], BF16, 111 * KB)
        mgb = S.buf("mgT")
        wa = [self.sb("wa%d" % i, [128, 8, 128], BF16, (48 + 2 * i) * KB) for i in range(2)]
        wab = S.bufs(2, "wa")
        ws_ = [self.sb("ws%d" % i, [128, 4, 128], BF16, (52 + i) * KB) for i in range(2)]
        wsb = S.bufs(2, "ws")
        sga = [self.sb("sga%d" % i, [128, NMAIN], BF16, 54 * KB + i * 4608) for i in range(2)]
        sgab = S.bufs(2, "sga")
        sgs = [self.sb("sgs%d" % i, [128, NMAIN], BF16, 63 * KB + i * 4608) for i in range(2)]
        sgsb = S.bufs(2, "sgs")
        t1 = [self.sb("t1_%d" % i, [128, 512], F32, (183 + 2 * i) * KB) for i in range(2)]
        t1b = S.bufs(2, "t1")
        t2 = [self.sb("t2_%d" % i, [128, 512], F32, (187 + 2 * i) * KB) for i in range(2)]
        t2b = S.bufs(2, "t2")
        it = 0
        for c in range(16):
            i2 = c % 2
            loadw("pool", wa[i2], wab[i2], w_ba[:, 128 * c:128 * c + 128], 8)
            loadw("pool", ws_[i2], wsb[i2], w_bs[:, 128 * c:128 * c + 128], 4)
            S.dma("sp", sga[i2][:], SG[c], writes=[sgab[i2]], sem_buf=sgab[i2])
            S.dma("sp", sgs[i2][:], SG[16 + c], writes=[sgsb[i2]], sem_buf=sgsb[i2])
            for (c0, n) in VT:
                pa, ps_ = nb(), nb()
                for kt in range(8):
                    S.op("pe", lambda e, pa=pa, i2=i2, kt=kt, c0=c0, n=n: e.matmul(
                        PS[pa][:, 0:n], lhsT=wa[i2][:, kt, :], rhs=attnT[:, kt, c0:c0 + n], start=(kt == 0), stop=(kt == 7)),
                        reads=[wab[i2], atbs[kt]], writes=[PSB[pa]], sig=(kt == 7))
                for kt in range(4):
                    S.op("pe", lambda e, ps_=ps_, i2=i2, kt=kt, c0=c0, n=n: e.matmul(
                        PS[ps_][:, 0:n], lhsT=ws_[i2][:, kt, :], rhs=ssmT[:, kt, c0:c0 + n], start=(kt == 0), stop=(kt == 3)),
                        reads=[wsb[i2], ssb], writes=[PSB[ps_]], sig=(kt == 3))
                ti = it % 2
                it += 1
                S.op("dve", lambda e, pa=pa, i2=i2, c0=c0, n=n, ti=ti: e.tensor_tensor(
                    out=t1[ti][:, 0:n], in0=PS[pa][:, 0:n], in1=sga[i2][:, c0:c0 + n], op=ALU.mult),
                    reads=[PSB[pa], sgab[i2]], writes=[t1b[ti]])
                S.op("dve", lambda e, ps_=ps_, i2=i2, c0=c0, n=n, ti=ti: e.tensor_tensor(
                    out=t2[ti][:, 0:n], in0=PS[ps_][:, 0:n], in1=sgs[i2][:, c0:c0 + n], op=ALU.mult),
                    reads=[PSB[ps_], sgsb[i2]], writes=[t2b[ti]])
                S.op("dve", lambda e, c=c, c0=c0, n=n, ti=ti: e.tensor_tensor(
                    out=mgT[:, c, c0:c0 + n], in0=t1[ti][:, 0:n], in1=t2[ti][:, 0:n], op=ALU.add),
                    reads=[t1b[ti], t2b[ti]], writes=[mgb])
        S.barrier()
        wo = [self.sb("wo%d" % i, [128, 16, 128], BF16, (12 + 4 * i) * KB) for i in range(2)]
        wob = S.bufs(2, "wo")
        xl = [self.sb("xl%d" % i, [128, 512], F32, (20 + 2 * i) * KB) for i in range(2)]
        xlb = S.bufs(2, "xl")
        x1s = [self.sb("x1s%d" % i, [128, 512], F32, (24 + 2 * i) * KB) for i in range(2)]
        x1sb = S.bufs(2, "x1s")
        sqt = [self.sb("sqt%d" % i, [128, 512], F32, (28 + 2 * i) * KB) for i in range(2)]
        sqtb = S.bufs(2, "sqt")
        ssacc = self.sb("ssacc", [128, NMAIN], F32, 32 * KB)
        ssab = S.buf("ssacc")
        rstd = self.sb("rstd", [128, NMAIN], F32, 183 * KB)
        rstb = S.buf("rstd")
        h2T = self.sb("h2T", [128, 16, 2050], BF16, 42 * KB)
        h2b = S.buf("h2T")
        S.op("dve", lambda e: e.memset(ssacc[:], 0.0), writes=[ssab])
        it = 0
        for c in range(16):
            i2 = c % 2
            loadw("pool", wo[i2], wob[i2], w_out[:, 128 * c:128 * c + 128], 16)
            for (c0, n) in VT:
                ti = it % 2
                it += 1
                S.dma("pool", xl[ti][:, 0:n], XT[c][:, c0:c0 + n], writes=[xlb[ti]], sem_buf=xlb[ti])
                pb = nb()
                for kt in range(16):
                    S.op("pe", lambda e, pb=pb, i2=i2, kt=kt, c0=c0, n=n: e.matmul(
                        PS[pb][:, 0:n], lhsT=wo[i2][:, kt, :], rhs=mgT[:, kt, c0:c0 + n], start=(kt == 0), stop=(kt == 15)),
                        reads=[wob[i2], mgb], writes=[PSB[pb]], sig=(kt == 15))
                S.op("dve", lambda e, pb=pb, n=n, ti=ti: e.tensor_tensor(
                    out=x1s[ti][:, 0:n], in0=PS[pb][:, 0:n], in1=xl[ti][:, 0:n], op=ALU.add),
                    reads=[PSB[pb], xlb[ti]], writes=[x1sb[ti]])
                S.dma("sp", X1T[c][:, c0:c0 + n], x1s[ti][:, 0:n], reads=[x1sb[ti]], writes=[], sem_buf=x1sb[ti])
                S.op("act", lambda e, n=n, ti=ti: e.activation(out=sqt[ti][:, 0:n], in_=x1s[ti][:, 0:n], func=AF.Square),
                     reads=[x1sb[ti]], writes=[sqtb[ti]])
                lo_ = max(c0, 254)
                S.op("act", lambda e, c=c, c0=c0, n=n, ti=ti, lo_=lo_: e.activation(
                    out=h2T[:, c, lo_ - 254:c0 + n - 254], in_=x1s[ti][:, lo_ - c0:n], func=AF.Copy, scale=g2c[:, c:c + 1]),
                    reads=[x1sb[ti], cbuf], writes=[h2b])
                S.op("dve", lambda e, c0=c0, n=n, ti=ti: e.tensor_tensor(
                    out=ssacc[:, c0:c0 + n], in0=ssacc[:, c0:c0 + n], in1=sqt[ti][:, 0:n], op=ALU.add),
                    reads=[sqtb[ti], ssab], writes=[ssab])

        def rstd_from(acc, accb, dst, dstb, tiles):
            for (c0, n) in tiles:
                pb = nb()
                S.op("pe", lambda e, pb=pb, c0=c0, n=n: e.matmul(PS[pb][:, 0:n], lhsT=onesf[:], rhs=acc[:, c0:c0 + n],
                                                                   start=True, stop=True),
                     reads=[accb, cbuf], writes=[PSB[pb]], sig=True)
                S.op("dve", lambda e, pb=pb, c0=c0, n=n: e.tensor_scalar(
                    out=dst[:, c0:c0 + n], in0=PS[pb][:, 0:n], scalar1=1.0 / D, scalar2=1e-6, op0=ALU.mult, op1=ALU.add),
                    reads=[PSB[pb]], writes=[dstb])
                S.op("act", lambda e, c0=c0, n=n: e.activation(out=dst[:, c0:c0 + n], in_=dst[:, c0:c0 + n], func=AF.Sqrt),
                     reads=[dstb], writes=[dstb])
                S.op("dve", lambda e, c0=c0, n=n: e.reciprocal(out=dst[:, c0:c0 + n], in_=dst[:, c0:c0 + n]),
                     reads=[dstb], writes=[dstb])

        rstd_from(ssacc, ssab, rstd, rstb, VT)
        for c in range(16):
            S.op("dve", lambda e, c=c: e.tensor_tensor(out=h2T[:, c, :], in0=h2T[:, c, :], in1=rstd[:, 254:2304], op=ALU.mult),
                 reads=[rstb, h2b], writes=[h2b])
        S.barrier()
        wu2 = [self.sb("wu2_%d" % i, [128, 16, 256], BF16, (111 + 8 * i) * KB) for i in range(2)]
        wu2b = S.bufs(2, "wu2")
        gbufs = [self.sb("gbuf%d" % i, [128, 2050], F32, (127 + 9 * i) * KB) for i in range(2)]
        gbbs = S.bufs(2, "gbuf")
        vbufs = [self.sb("vbuf%d" % i, [128, 2048], F32, (145 + 8 * i) * KB) for i in range(2)]
        vbbs = S.bufs(2, "vbuf")
        tcvs = [self.sb("tcv%d" % i, [128, 2048], F32, (161 + 8 * i) * KB) for i in range(2)]
        tcbs = S.bufs(2, "tcv")
        tgs = [self.sb("tg%d" % i, [128, 2048], F32, (177 + 8 * i) * KB) for i in range(2)]
        tgbs = S.bufs(2, "tg")
        ast = [self.sb("ast%d" % i, [128, 2048], BF16, (193 + 4 * i) * KB) for i in range(2)]
        astb = S.bufs(2, "ast")
        mk2 = self.sb("mk2", [128, 2], F32, 201 * KB)
        mk2b = S.buf("mk2")
        S.dma("sp", mk2[:], maskw[MB0 * 128 + 254:MB0 * 128 + 256].partition_broadcast(128), writes=[mk2b], sem_buf=mk2b, nonc=True)
        for j in range(44):
            i2 = j % 2
            w = wu2[i2]
            wb = wu2b[i2]
            gbuf, gbb, vbuf, vbb, tcv, tcb, tg, tgb = gbufs[i2], gbbs[i2], vbufs[i2], vbbs[i2], tcvs[i2], tcbs[i2], tgs[i2], tgbs[i2]
            S.dma("pool", w[:, :, 0:128], w_up[:, 128 * j:128 * j + 128].rearrange("(kt p) c -> p kt c", p=128), writes=[wb], sem_buf=wb)
            S.dma("pool", w[:, :, 128:256], w_up[:, DFF + 128 * j:DFF + 128 * j + 128].rearrange("(kt p) c -> p kt c", p=128),
                  writes=[wb], sem_buf=wb)
            ph = nb()
            for kt in range(16):
                S.op("pe", lambda e, ph=ph, w=w, kt=kt: e.matmul(PS[ph][:, 0:2], lhsT=w[:, kt, 128:256], rhs=h2T[:, kt, 0:2],
                                                                  start=(kt == 0), stop=(kt == 15)),
                     reads=[wb, h2b], writes=[PSB[ph]], sig=(kt == 15))
            S.op("dve", lambda e, ph=ph, gbuf=gbuf: e.tensor_tensor(out=gbuf[:, 0:2], in0=PS[ph][:, 0:2], in1=mk2[:], op=ALU.mult),
                 reads=[PSB[ph], mk2b], writes=[gbb])
            for t in range(4):
                pv, pg = nb(), nb()
                for kt in range(16):
                    S.op("pe", lambda e, pv=pv, w=w, kt=kt, t=t: e.matmul(
                        PS[pv][:], lhsT=w[:, kt, 0:128], rhs=h2T[:, kt, 2 + 512 * t:514 + 512 * t], start=(kt == 0), stop=(kt == 15)),
                        reads=[wb, h2b], writes=[PSB[pv]], sig=(kt == 15))
                for kt in range(16):
                    S.op("pe", lambda e, pg=pg, w=w, kt=kt, t=t: e.matmul(
                        PS[pg][:], lhsT=w[:, kt, 128:256], rhs=h2T[:, kt, 2 + 512 * t:514 + 512 * t], start=(kt == 0), stop=(kt == 15)),
                        reads=[wb, h2b], writes=[PSB[pg]], sig=(kt == 15))
                S.op("act", lambda e, pg=pg, t=t, gbuf=gbuf: e.activation(out=gbuf[:, 2 + 512 * t:514 + 512 * t], in_=PS[pg][:], func=AF.Copy),
                     reads=[PSB[pg]], writes=[gbb])
                S.op("dve", lambda e, pv=pv, t=t, vbuf=vbuf: e.tensor_copy(out=vbuf[:, 512 * t:512 * t + 512], in_=PS[pv][:]),
                     reads=[PSB[pv]], writes=[vbb])
            S.op("dve", lambda e, j=j, tcv=tcv, gbuf=gbuf: e.tensor_scalar(out=tcv[:], in0=gbuf[:, 0:2048], scalar1=cwc[:, 0, j:j + 1], scalar2=cbc[:, j:j + 1],
                                                        op0=ALU.mult, op1=ALU.add), reads=[gbb, cbuf], writes=[tcb])
            S.op("dve", lambda e, j=j, tcv=tcv, gbuf=gbuf: e.scalar_tensor_tensor(out=tcv[:], in0=gbuf[:, 1:2049], scalar=cwc[:, 1, j:j + 1], in1=tcv[:],
                                                               op0=ALU.mult, op1=ALU.add), reads=[gbb, tcb, cbuf], writes=[tcb])
            S.op("dve", lambda e, j=j, tcv=tcv, gbuf=gbuf: e.scalar_tensor_tensor(out=tcv[:], in0=gbuf[:, 2:2050], scalar=cwc[:, 2, j:j + 1], in1=tcv[:],
                                                               op0=ALU.mult, op1=ALU.add), reads=[gbb, tcb, cbuf], writes=[tcb])
            S.op("act", lambda e, tg=tg, tcv=tcv: e.activation(out=tg[:], in_=tcv[:], func=AF.Gelu), reads=[tcb], writes=[tgb])
            S.op("dve", lambda e, i2=i2, vbuf=vbuf, tg=tg: e.tensor_tensor(out=ast[i2][:], in0=vbuf[:], in1=tg[:], op=ALU.mult),
                 reads=[vbb, tgb], writes=[astb[i2]])
            S.dma("sp", ACTd[j], ast[i2][:], reads=[astb[i2]], writes=[], sem_buf=astb[i2])
        S.barrier()
        acth = self.sb("acth", [128, 44, 1024], BF16, 12 * KB)
        achb = S.buf("acth")
        achq = S.bufs(4, "acthq")
        wd = [self.sb("wd%d" % i, [128, 44, 128], BF16, (100 + 11 * i) * KB) for i in range(2)]
        wdb = S.bufs(2, "wd")
        xl9 = [self.sb("xl9_%d" % i, [128, 512], F32, (122 + 2 * i) * KB) for i in range(2)]
        xl9b = S.bufs(2, "xl9")
        x2s = [self.sb("x2s%d" % i, [128, 512], F32, (126 + 2 * i) * KB) for i in range(2)]
        x2sb = S.bufs(2, "x2s")
        sq9 = [self.sb("sq9_%d" % i, [128, 512], F32, (130 + 2 * i) * KB) for i in range(2)]
        sq9b = S.bufs(2, "sq9")
        ssa2 = self.sb("ssa2", [128, 2048], F32, 134 * KB)
        ssa2b = S.buf("ssa2")
        rstd2 = self.sb("rstd2", [128, 2048], F32, 142 * KB)
        rst2b = S.buf("rstd2")
        S.op("dve", lambda e: e.memset(ssa2[:], 0.0), writes=[ssa2b])
        it = 0
        wi = 0
        for hf in range(2):
            for qq in range(4):
                S.dma("sp", acth[:, 11 * qq:11 * qq + 11, :], ACTd.rearrange("j p c -> p j c")[:, 11 * qq:11 * qq + 11, 1024 * hf:1024 * hf + 1024],
                      writes=[achq[qq]], sem_buf=achq[qq])
            for c in range(16):
                i2 = wi % 2
                wi += 1
                loadw("pool", wd[i2], wdb[i2], w_down[:, 128 * c:128 * c + 128], 44)
                for tt in range(2):
                    o0 = 1024 * hf + 512 * tt
                    ti = it % 2
                    it += 1
                    S.dma("pool", xl9[ti][:], X1T[c][:, OWN0 + o0:OWN0 + o0 + 512], writes=[xl9b[ti]], sem_buf=xl9b[ti])
                    pb = nb()
                    for kt in range(44):
                        S.op("pe", lambda e, pb=pb, i2=i2, kt=kt, tt=tt: e.matmul(
                            PS[pb][:], lhsT=wd[i2][:, kt, :], rhs=acth[:, kt, 512 * tt:512 * tt + 512], start=(kt == 0), stop=(kt == 43)),
                            reads=[wdb[i2], achq[kt // 11]], writes=[PSB[pb]], sig=(kt == 43))
                    S.op("dve", lambda e, pb=pb, ti=ti: e.tensor_tensor(out=x2s[ti][:], in0=PS[pb][:], in1=xl9[ti][:], op=ALU.add),
                         reads=[PSB[pb], xl9b[ti]], writes=[x2sb[ti]])
                    S.dma("sp", X2T[c][:, o0:o0 + 512], x2s[ti][:], reads=[x2sb[ti]], writes=[], sem_buf=x2sb[ti])
                    S.op("act", lambda e, ti=ti: e.activation(out=sq9[ti][:], in_=x2s[ti][:], func=AF.Square),
                         reads=[x2sb[ti]], writes=[sq9b[ti]])
                    S.op("dve", lambda e, o0=o0, ti=ti: e.tensor_tensor(
                        out=ssa2[:, o0:o0 + 512], in0=ssa2[:, o0:o0 + 512], in1=sq9[ti][:], op=ALU.add),
                        reads=[sq9b[ti], ssa2b], writes=[ssa2b])
        rstd_from(ssa2, ssa2b, rstd2, rst2b, [(0, 512), (512, 512), (1024, 512), (1536, 512)])
        S.barrier()
        x2l = [self.sb("x2l%d" % i, [128, 16, 128], F32, (12 + 8 * i) * KB) for i in range(2)]
        x2lb = S.bufs(2, "x2l")
        otmps = [self.sb("otmp%d" % i, [128, 16, 128], F32, (28 + 32 * i) * KB) for i in range(2)]
        otbs = S.bufs(2, "otmp")
        oTs = [self.sb("oT%d" % i, [128, 16, 128], F32, (36 + 32 * i) * KB) for i in range(2)]
        oTbs = S.bufs(2, "oT")
        orow = [self.sb("orow%d" % i, [128, D], F32, (44 + 8 * i) * KB) for i in range(2)]
        orb = S.bufs(2, "orow")
        X2v = X2T.rearrange("k p c -> p k c")
        rcol = [self.sb("rcol%d" % i, [128, 1], F32, 76 * KB + 64 * i) for i in range(2)]
        rcolb = S.bufs(2, "rcol")
        S.dma("sp", x2l[0][:], X2v[:, :, 0:128], writes=[x2lb[0]], sem_buf=x2lb[0])
        for tblk in range(16):
            i2 = tblk % 2
            c0 = 128 * tblk
            otmp, otb, oT, oTb = otmps[i2], otbs[i2], oTs[i2], oTbs[i2]
            if tblk + 1 < 16:
                S.dma("sp", x2l[1 - i2][:], X2v[:, :, c0 + 128:c0 + 256], writes=[x2lb[1 - i2]], sem_buf=x2lb[1 - i2])
            pr = nb()
            S.op("pe", lambda e, pr=pr, c0=c0: e.matmul(PS[pr][:, 0:1], lhsT=rstd2[0:1, c0:c0 + 128], rhs=onesf[0:1, 0:1],
                                                         start=True, stop=True), reads=[rst2b, cbuf], writes=[PSB[pr]], sig=True)
            S.op("act", lambda e, pr=pr, i2=i2: e.activation(out=rcol[i2][:], in_=PS[pr][:, 0:1], func=AF.Copy),
                 reads=[PSB[pr]], writes=[rcolb[i2]])
            S.op("dve", lambda e, i2=i2, oT=oT: e.tensor_tensor(
                out=oT[:], in0=x2l[i2][:], in1=g3c[:].unsqueeze(2).to_broadcast([128, 16, 128]), op=ALU.mult),
                reads=[x2lb[i2], cbuf], writes=[oTb])
            pbs = []
            for j in range(4):
                pb = nb()
                pbs.append(pb)
                for kk in range(4):
                    kt = 4 * j + kk
                    S.op("pe", lambda e, pb=pb, kk=kk, kt=kt, oT=oT: e.transpose(
                        out=PS[pb][:, 128 * kk:128 * kk + 128], in_=oT[:, kt, :], identity=ident[:]),
                        reads=[oTb, cbuf], writes=[PSB[pb]], sig=(kk == 3))
            for j in range(4):
                pb = pbs[j]
                if j % 2 == 0:
                    S.op("act", lambda e, pb=pb, j=j, i2=i2: e.activation(out=orow[i2][:, 512 * j:512 * j + 512], in_=PS[pb][:], func=AF.Copy,
                                                                         scale=rcol[i2][:, 0:1]),
                         reads=[PSB[pb], rcolb[i2]], writes=[orb[i2]])
                else:
                    S.op("dve", lambda e, pb=pb, j=j, i2=i2: e.tensor_scalar_mul(out=orow[i2][:, 512 * j:512 * j + 512], in0=PS[pb][:],
                                                                                scalar1=rcol[i2][:, 0:1]),
                         reads=[PSB[pb], rcolb[i2]], writes=[orb[i2]])
            S.dma("sp", out[c0:c0 + 128, :], orow[i2][:], reads=[orb[i2]], writes=[], sem_buf=orb[i2])
        finals.extend(orb)
        return S, finals

    def finish(self, S, final_bufs):
        toks = []
        for b in final_bufs:
            if b.dsem is not None:
                toks.append([b.dsem, b.dsem.cnt])
        S.emit(toks)
        return self.nc


def _abias_table():
    qi = np.arange(128)[:, None]
    si = np.arange(256)[None, :]
    dist = qi + 128 - si
    band = (dist >= 0) & (dist < 128)
    slopes = 2.0 ** (-8.0 * np.arange(1, 17, dtype=np.float32) / 16)
    t = -slopes[None, :, None] * dist[:, None, :].astype(np.float32)
    t = np.where(band[:, None, :], t, np.float32(-30000.0))
    return np.ascontiguousarray(t.astype(np.float32))


def make_in_maps(inputs):
    x = np.asarray(inputs["x"], dtype=np.float32)
    maps = []
    ident = np.eye(128, dtype=np.float32)
    ab = _abias_table()
    for core in range(NCORES):
        b, c = core // 4, core % 4
        t1 = 2048 * (c + 1)
        xw = np.zeros((WIN, D), np.float32)
        xw[WIN - t1:] = x[b, :t1]
        mask = np.zeros((WIN,), np.float32)
        mask[WIN - t1:] = 1.0
        m = {"xw": xw, "maskw": mask, "identity": ident, "abias": ab}
        for k, v in inputs.items():
            if k == "x":
                continue
            v = np.asarray(v, dtype=np.float32)
            m[k] = np.ascontiguousarray(v[0]) if k != "final_norm_g" else np.ascontiguousarray(v)
        maps.append(m)
    return maps


_NC_CACHE = {}


def _get_nc():
    if "nc" not in _NC_CACHE:
        kb = K()
        S, finals = kb.build()
        _NC_CACHE["nc"] = kb.finish(S, finals)
    return _NC_CACHE["nc"]


def kernel(**inputs):
    nc = _get_nc()
    in_maps = make_in_maps(inputs)
    res = run_bass_kernel_spmd(nc, in_maps, core_ids=list(range(NCORES)))
    outs = [np.asarray(r["out"], dtype=np.float32) for r in res.results]
    full = np.stack(outs, 0).reshape(2, 4 * 2048, D)
    return full
```

```python
import contextlib
import math
import numpy as np
import concourse.bass as bass
import concourse.mybir as mybir
from concourse.bass_utils import run_bass_kernel_spmd

F32 = mybir.dt.float32
BF16 = mybir.dt.bfloat16
AF = mybir.ActivationFunctionType
ALU = mybir.AluOpType

D = 2048
NCORES = 8
WIN = 8192
NBLK = 64
MB0 = 46
NMAIN = 2304
OWN0 = 256
NCH0 = MB0 * 8
NCHM = 144
KVALS = list(range(17)) + [64]
NK = len(KVALS)
DFF = 5632
KB = 1024


class Sem:
    def __init__(self, h):
        self.h = h
        self.cnt = 0


class Buf:
    __slots__ = ("name", "w", "r", "dsem")

    def __init__(self, name):
        self.name = name
        self.w = None
        self.r = {}
        self.dsem = None


class Sched:
    ENG = ("pe", "act", "dve", "pool", "sp")

    def __init__(self, nc, stack):
        self.nc = nc
        self.stack = stack
        self.ops = {e: [] for e in self.ENG}
        self.esem = {e: Sem(stack.enter_context(nc.semaphore("es_" + e))) for e in self.ENG}
        self.future = {e: [self.esem[e], None] for e in self.ENG}
        self.last = {e: None for e in self.ENG}
        self.extra = {e: [] for e in self.ENG}
        self.dsems = []
        self.nbuf = 0

    def buf(self, name=None):
        self.nbuf += 1
        return Buf(name or "b%d" % self.nbuf)

    def bufs(self, n, name="b"):
        return [self.buf("%s%d" % (name, i)) for i in range(n)]

    def _dsem(self, b):
        if b.dsem is None:
            b.dsem = Sem(self.stack.enter_context(self.nc.semaphore("ds_%d" % len(self.dsems))))
            self.dsems.append(b.dsem)
        return b.dsem

    def _collect(self, eng, reads, writes):
        waits = list(self.extra[eng])
        self.extra[eng] = []
        for b in reads:
            if b.w is not None:
                waits.append(b.w)
        for b in writes:
            if b.w is not None:
                waits.append(b.w)
            waits.extend(b.r.values())
        out = []
        for t in waits:
            if eng in ("pe", "act", "pool") and t[0] is self.esem[eng]:
                continue
            if t[0] in self.dsems:
                t = [t[0], t[0].cnt]
            out.append(t)
        return out

    def op(self, eng, fn, reads=(), writes=(), sig=True):
        waits = self._collect(eng, reads, writes)
        tok = self.future[eng]
        inc = None
        if sig:
            s = self.esem[eng]
            s.cnt += 1
            tok[1] = s.cnt
            self.future[eng] = [s, None]
            inc = (s, 1)
            self.last[eng] = tok
        self.ops[eng].append((waits, fn, inc))
        for b in writes:
            b.w = tok
            b.r = {}
        for b in reads:
            b.r[id(tok[0])] = tok
        return tok

    def dma(self, eng, out_ap, in_ap, reads=(), writes=(), sem_buf=None, nonc=False):
        waits = self._collect(eng, reads, writes)
        s = self._dsem(sem_buf)
        s.cnt += 16
        tok = [s, s.cnt]
        nc = self.nc

        def fn(e, out_ap=out_ap, in_ap=in_ap):
            if nonc:
                with nc.allow_non_contiguous_dma(reason="small param gather"):
                    return e.dma_start(out=out_ap, in_=in_ap)
            return e.dma_start(out=out_ap, in_=in_ap)

        self.ops[eng].append((waits, fn, (s, 16)))
        for b in writes:
            b.w = tok
            b.r = {}
        for b in reads:
            b.r[id(s)] = tok
        return tok

    def barrier(self):
        toks = []
        for e in self.ENG:
            if self.last[e] is not None:
                toks.append(self.last[e])
        for s in self.dsems:
            if s.cnt:
                toks.append([s, s.cnt])
        for e in self.ENG:
            self.extra[e].extend(toks)

    def emit(self, final_toks):
        nc = self.nc
        for e in self.ENG:
            assert self.future[e][1] is None
        with nc.Block() as block:
            def replay(eng, h):
                waited = {}
                for waits, fn, inc in self.ops[eng]:
                    for t in waits:
                        assert t[1] is not None, "unresolved token"
                        k = id(t[0])
                        if waited.get(k, 0) < t[1]:
                            h.wait_ge(t[0].h, t[1])
                            waited[k] = t[1]
                    ins = fn(h)
                    if inc is not None:
                        ins.then_inc(inc[0].h, inc[1])
                if eng == "sp":
                    for t in final_toks:
                        h.wait_ge(t[0].h, t[0].cnt)

            @block.tensor
            def _(h):
                replay("pe", h)

            @block.scalar
            def _(h):
                replay("act", h)

            @block.vector
            def _(h):
                replay("dve", h)

            @block.gpsimd
            def _(h):
                replay("pool", h)

            @block.sync
            def _(h):
                replay("sp", h)


class K:
    def __init__(self, debug=()):
        self.debug = set(debug)
        self.stack = contextlib.ExitStack()
        self.nc = bass.Bass("TRN2", target_bir_lowering=False)
        self.S = None
        self.stop = None

    def din(self, name, shape, dt=F32):
        return self.nc.dram_tensor(name, list(shape), dt, kind="ExternalInput").ap()

    def dscr(self, name, shape, dt, out=False):
        kind = "ExternalOutput" if (out or name in self.debug) else "Internal"
        return self.nc.dram_tensor(name, list(shape), dt, kind=kind).ap()

    def sb(self, name, shape, dt, off):
        return self.nc.alloc_sbuf_tensor_at(name, list(shape), dt, offset=off + 16 * KB)

    def build(self):
        nc = self.nc
        st = self.stack
        S = self.S = Sched(nc, st)
        dbg = self.debug
        xw = self.din("xw", [WIN, D])
        maskw = self.din("maskw", [WIN])
        g1 = self.din("attn_norm_g", [D])
        w_in = self.din("w_in", [D, 5888])
        b_in = self.din("b_in", [5888])
        sinks = self.din("attn_sinks", [16])
        a_re = self.din("ssm_a_re", [32, 64])
        a_im = self.din("ssm_a_im", [32, 64])
        log_dt = self.din("ssm_log_dt", [32])
        b_re = self.din("ssm_b_re", [32, 64, 16])
        b_im = self.din("ssm_b_im", [32, 64, 16])
        c_re = self.din("ssm_c_re", [32, 16, 64])
        c_im = self.din("ssm_c_im", [32, 16, 64])
        ssm_d = self.din("ssm_d", [512])
        w_glu = self.din("w_glu", [512, 1024])
        b_glu = self.din("b_glu", [1024])
        w_ba = self.din("w_branch_attn", [1024, D])
        w_bs = self.din("w_branch_ssm", [512, D])
        w_out = self.din("w_out", [D, D])
        g2 = self.din("ffn_norm_g", [D])
        w_up = self.din("w_up", [D, 2 * DFF])
        conv_w = self.din("conv_w", [3, DFF])
        conv_b = self.din("conv_b", [DFF])
        w_down = self.din("w_down", [DFF, D])
        g3 = self.din("final_norm_g", [D])
        abias = self.din("abias", [128, 16, 256])
        out = self.nc.dram_tensor("out", [2048, D], F32, kind="ExternalOutput").ap()
        FFd = self.dscr("FFd", [128, 16 * 2 * 16 * 32], BF16)
        TZd = self.dscr("TZd", [128, 16 * 4 * 128], BF16)
        XT = self.dscr("XT", [16, 128, NMAIN], F32)
        HT = self.dscr("HT", [16, 128, NMAIN], BF16)
        YG = self.dscr("YG", [4, 128, NMAIN], BF16) if "YG" in dbg else None
        UM = self.dscr("UM", [4, 128, NMAIN], BF16) if "UM" in dbg else None
        XSd = self.dscr("XSd", [128, 2 * 16 * NCHM], F32) if "XSd" in dbg else None

        o = 0

        def cal(name, shape, dt):
            nonlocal o
            nbytes = int(np.prod(shape[1:])) * (4 if dt == F32 else 2)
            t = self.sb(name, shape, dt, o)
            o += (nbytes + 31) // 32 * 32
            return t

        ident = cal("ident", [128, 128], F32)
        identb = cal("identb", [128, 128], BF16)
        onesb = cal("onesb", [128, 128], BF16)
        onesf = cal("onesf", [128, 128], F32)
        g1c = cal("g1c", [128, 16], F32)
        g2c = cal("g2c", [128, 16], F32)
        g3c = cal("g3c", [128, 16], F32)
        binc = cal("binc", [128, 46], F32)
        bgluc = cal("bgluc", [128, 8], F32)
        cwc = cal("cwc", [128, 3, 44], F32)
        cbc = cal("cbc", [128, 44], F32)
        dcol = cal("dcol", [128, 4], F32)
        AAt = cal("AAt", [128, 2, 16], F32)
        BBt = cal("BBt", [128, 2, 16], F32)
        X4 = [cal("X4a", [128, 3, 16], F32), cal("X4b", [128, 3, 16], F32)]
        P1 = cal("P1", [128, 2, 16], F32)
        P2 = cal("P2", [128, 2, 16], F32)
        A4re = cal("A4re", [128, 16], F32)
        A4im = cal("A4im", [128, 16], F32)
        A16re = cal("A16re", [128, 16], F32)
        A16im = cal("A16im", [128, 16], F32)
        AA64 = cal("AA64", [128, 2, 16], F32)
        BB64 = cal("BB64", [128, 2, 16], F32)
        assert o <= 8 * KB, o
        cbuf = S.buf("consts")
        identity_src = self.din("identity", [128, 128])
        S.dma("sp", ident[:], identity_src, writes=[cbuf], sem_buf=cbuf)
        cbuf2 = S.buf("consts2")
        S.dma("pool", identb[:], identity_src, writes=[cbuf2], sem_buf=cbuf2)
        S.op("dve", lambda e: e.memset(onesb[:], 1.0), writes=[cbuf])
        S.op("dve", lambda e: e.memset(onesf[:], 1.0), writes=[cbuf])
        for t, src in ((g1c, g1), (g2c, g2), (g3c, g3)):
            S.dma("sp", t[:], src.rearrange("(k p) -> p k", p=128), writes=[cbuf], sem_buf=cbuf, nonc=True)
        S.dma("sp", binc[:], b_in.rearrange("(k p) -> p k", p=128), writes=[cbuf], sem_buf=cbuf, nonc=True)
        S.dma("sp", bgluc[:], b_glu.rearrange("(k p) -> p k", p=128), writes=[cbuf], sem_buf=cbuf, nonc=True)
        S.dma("sp", cwc[:], conv_w.rearrange("w (k p) -> p w k", p=128), writes=[cbuf], sem_buf=cbuf, nonc=True)
        S.dma("sp", cbc[:], conv_b.rearrange("(k p) -> p k", p=128), writes=[cbuf], sem_buf=cbuf, nonc=True)
        S.dma("sp", dcol[:], ssm_d.rearrange("(k p) -> p k", p=128), writes=[cbuf], sem_buf=cbuf, nonc=True)

        PS = [st.enter_context(nc.psum_tensor("ps%d" % i, [128, 512], F32)) for i in range(8)]
        PSB = S.bufs(8, "psb")

        base = 12 * KB
        SW = self.sb("SW", [128, 16, 2, 4, 128], BF16, 30 * KB)
        swb = S.buf("SW")
        po = 62 * KB

        def pal(name, shape, dt):
            nonlocal po
            nbytes = int(np.prod(shape[1:])) * (4 if dt == F32 else 2)
            t = self.sb(name, shape, dt, po)
            po += (nbytes + 31) // 32 * 32
            return t

        CAre = pal("CAre", [128, 17, 16, 32], F32)
        CAim = pal("CAim", [128, 17, 16, 32], F32)
        SWT = pal("SWT", [128, 16, 16, 32], F32)
        TMPB = pal("TMPB", [128, 17, 16, 32], F32)
        assert po <= 200 * KB, po
        po = base
        are = pal("are", [128, 16], F32)
        aim = pal("aim", [128, 16], F32)
        ldt = pal("ldt", [128, 16], F32)
        dtv = pal("dtv", [128, 16], F32)
        adr = pal("adr", [128, 16], F32)
        ang = pal("ang", [128, 16], F32)
        KM = pal("KM", [128, NK, 16], F32)
        PWm = pal("PWm", [128, NK, 16], F32)
        ANG = pal("ANG", [128, NK, 16], F32)
        T17 = pal("T17", [128, NK, 16], F32)
        XS17 = self.sb("XS17", [128, NK, 16], F32, 174 * KB)
        NI = self.sb("NI", [128, NK, 16], mybir.dt.int32, 176 * KB)
        PWre = pal("PWre", [128, NK, 16], F32)
        PWim = pal("PWim", [128, NK, 16], F32)
        t16 = [pal("t16_%d" % i, [128, 16], F32) for i in range(6)]
        zre = pal("zre", [128, 16], F32)
        zim = pal("zim", [128, 16], F32)
        BEre = self.sb("BEre", [128, 16, 32], F32, 170 * KB)
        BEim = self.sb("BEim", [128, 16, 32], F32, 172 * KB)
        Bbre = pal("Bbre", [128, 16, 32], F32)
        Bbim = pal("Bbim", [128, 16, 32], F32)
        CEre = pal("CEre", [128, 16, 32], F32)
        CEim = pal("CEim", [128, 16, 32], F32)
        assert po <= 30 * KB, po
        CN4 = [self.sb("CN4re", [128, 4, 128], F32, 196 * KB), self.sb("CN4im", [128, 4, 128], F32, 198 * KB)]
        BZre = self.sb("BZre", [128, 16, 128], F32, 130 * KB)
        BZim = self.sb("BZim", [128, 16, 128], F32, 138 * KB)
        FFs = self.sb("FFs", [128, 16, 16 * 32], BF16, 130 * KB)
        p0 = S.buf("p0")

        PI = math.pi
        pin = []

        def pin_new():
            pin.append(S.buf("pin%d" % len(pin)))
            return pin[-1]

        for gl in range(2):
            sl = slice(64 * gl, 64 * gl + 64)
            S.dma("sp", are[sl, :], a_re.rearrange("(gp gl) p -> gl p gp", gl=2)[gl], writes=[pin_new()], sem_buf=pin[-1], nonc=True)
            S.dma("sp", aim[sl, :], a_im.rearrange("(gp gl) p -> gl p gp", gl=2)[gl], writes=[pin_new()], sem_buf=pin[-1], nonc=True)
            S.dma("sp", ldt[sl, :], log_dt.rearrange("(gp gl) -> gl gp", gl=2)[gl].partition_broadcast(64),
                  writes=[pin_new()], sem_buf=pin[-1], nonc=True)
        S.op("dve", lambda e: e.memset(BEre[:], 0.0), writes=[p0])
        S.op("dve", lambda e: e.memset(BEim[:], 0.0), writes=[p0])
        S.op("dve", lambda e: e.memset(CN4[0][:], 0.0), writes=[p0])
        S.op("dve", lambda e: e.memset(CN4[1][:], 0.0), writes=[p0])
        for gl in range(2):
            sl = slice(64 * gl, 64 * gl + 64)
            for t, src in ((BEre, b_re), (BEim, b_im)):
                S.dma("sp", t[sl, :, 16 * gl:16 * gl + 16], src.rearrange("(gp gl) p h -> gl p gp h", gl=2)[gl],
                      reads=[p0], writes=[pin_new()], sem_buf=pin[-1], nonc=True)
        for q in range(4):
            for gl in range(2):
                for t, src in ((CN4[0], c_re), (CN4[1], c_im)):
                    S.dma("sp", t[32 * q + 16 * gl:32 * q + 16 * gl + 16, :, 64 * gl:64 * gl + 64],
                          src.rearrange("(k q gl) h p -> q gl h k p", q=4, gl=2)[q, gl],
                          reads=[p0], writes=[pin_new()], sem_buf=pin[-1], nonc=True)
        for ki, kv in enumerate(KVALS):
            S.op("dve", lambda e, ki=ki, kv=kv: e.memset(KM[:, ki, :], float(kv)), writes=[p0])
        S.op("act", lambda e: e.activation(out=dtv[:], in_=ldt[:], func=AF.Exp), reads=[p0] + pin, writes=[p0] + pin)
        S.op("dve", lambda e: e.tensor_tensor(out=adr[:], in0=are[:], in1=dtv[:], op=ALU.mult), reads=[p0], writes=[p0])
        S.op("dve", lambda e: e.tensor_tensor(out=ang[:], in0=aim[:], in1=dtv[:], op=ALU.mult), reads=[p0], writes=[p0])

        def bc17(t):
            return t[:].unsqueeze(1).to_broadcast([128, NK, 16])

        S.op("dve", lambda e: e.tensor_tensor(out=T17[:], in0=KM[:], in1=bc17(adr), op=ALU.mult), reads=[p0], writes=[p0])
        S.op("act", lambda e: e.activation(out=PWm[:], in_=T17[:], func=AF.Exp), reads=[p0], writes=[p0])
        S.op("dve", lambda e: e.tensor_tensor(out=ANG[:], in0=KM[:], in1=bc17(ang), op=ALU.mult), reads=[p0], writes=[p0])
        def sincos(dst, shift):
            S.op("dve", lambda e: e.tensor_scalar(out=T17[:], in0=ANG[:], scalar1=1.0 / (2 * PI), scalar2=0.5 + shift / (2 * PI),
                                                   op0=ALU.mult, op1=ALU.add), reads=[p0], writes=[p0])
            S.op("dve", lambda e: e.tensor_copy(out=NI[:], in_=T17[:]), reads=[p0], writes=[p0])
            S.op("dve", lambda e: e.tensor_copy(out=T17[:], in_=NI[:]), reads=[p0], writes=[p0])
            S.op("dve", lambda e: e.tensor_scalar_add(out=XS17[:], in0=ANG[:], scalar1=shift), reads=[p0], writes=[p0])
            S.op("dve", lambda e: e.scalar_tensor_tensor(out=T17[:], in0=T17[:], scalar=-2 * PI, in1=XS17[:],
                                                          op0=ALU.mult, op1=ALU.add), reads=[p0], writes=[p0])
            S.op("dve", lambda e: e.tensor_scalar(out=XS17[:], in0=T17[:], scalar1=-PI, scalar2=2 * PI,
                                                   op0=ALU.is_lt, op1=ALU.mult), reads=[p0], writes=[p0])
            S.op("dve", lambda e: e.tensor_tensor(out=T17[:], in0=T17[:], in1=XS17[:], op=ALU.add), reads=[p0], writes=[p0])
            S.op("dve", lambda e: e.tensor_scalar(out=T17[:], in0=T17[:], scalar1=-PI, scalar2=PI,
                                                   op0=ALU.max, op1=ALU.min), reads=[p0], writes=[p0])
            S.op("act", lambda e: e.activation(out=dst[:], in_=T17[:], func=AF.Sin), reads=[p0], writes=[p0])

        sincos(PWim, 0.0)
        sincos(PWre, 0.5 * PI)
        S.op("dve", lambda e: e.tensor_tensor(out=PWre[:], in0=PWre[:], in1=PWm[:], op=ALU.mult), reads=[p0], writes=[p0])
        S.op("dve", lambda e: e.tensor_tensor(out=PWim[:], in0=PWim[:], in1=PWm[:], op=ALU.mult), reads=[p0], writes=[p0])
        nr, ni, den, ta, tb, tc = [t[:] for t in t16]

        def dv(fn):
            S.op("dve", fn, reads=[p0], writes=[p0])

        dv(lambda e: e.tensor_scalar_add(out=nr, in0=PWre[:, 1, :], scalar1=-1.0))
        dv(lambda e: e.tensor_copy(out=ni, in_=PWim[:, 1, :]))
        dv(lambda e: e.tensor_tensor(out=den, in0=are[:], in1=are[:], op=ALU.mult))
        dv(lambda e: e.tensor_tensor(out=ta, in0=aim[:], in1=aim[:], op=ALU.mult))
        dv(lambda e: e.tensor_tensor(out=den, in0=den, in1=ta, op=ALU.add))
        dv(lambda e: e.reciprocal(out=den, in_=den))
        dv(lambda e: e.tensor_tensor(out=ta, in0=nr, in1=are[:], op=ALU.mult))
        dv(lambda e: e.tensor_tensor(out=tb, in0=ni, in1=aim[:], op=ALU.mult))
        dv(lambda e: e.tensor_tensor(out=ta, in0=ta, in1=tb, op=ALU.add))
        dv(lambda e: e.tensor_tensor(out=zre[:], in0=ta, in1=den, op=ALU.mult))
        dv(lambda e: e.tensor_tensor(out=ta, in0=ni, in1=are[:], op=ALU.mult))
        dv(lambda e: e.tensor_tensor(out=tb, in0=nr, in1=aim[:], op=ALU.mult))
        dv(lambda e: e.tensor_tensor(out=ta, in0=ta, in1=tb, op=ALU.subtract))
        dv(lambda e: e.tensor_tensor(out=zim[:], in0=ta, in1=den, op=ALU.mult))
        dv(lambda e: e.tensor_copy(out=AAt[:, 0, :], in_=PWre[:, 16, :]))
        dv(lambda e: e.tensor_copy(out=AAt[:, 1, :], in_=PWre[:, 16, :]))
        dv(lambda e: e.tensor_copy(out=BBt[:, 1, :], in_=PWim[:, 16, :]))
        dv(lambda e: e.tensor_scalar_mul(out=BBt[:, 0, :], in0=PWim[:, 16, :], scalar1=-1.0))
        dv(lambda e: e.tensor_copy(out=A4re[:], in_=PWre[:, 4, :]))
        dv(lambda e: e.tensor_copy(out=A4im[:], in_=PWim[:, 4, :]))
        dv(lambda e: e.tensor_copy(out=A16re[:], in_=PWre[:, 16, :]))
        dv(lambda e: e.tensor_copy(out=A16im[:], in_=PWim[:, 16, :]))
        dv(lambda e: e.tensor_copy(out=AA64[:, 0, :], in_=PWre[:, 17, :]))
        dv(lambda e: e.tensor_copy(out=AA64[:, 1, :], in_=PWre[:, 17, :]))
        dv(lambda e: e.tensor_copy(out=BB64[:, 1, :], in_=PWim[:, 17, :]))
        dv(lambda e: e.tensor_scalar_mul(out=BB64[:, 0, :], in0=PWim[:, 17, :], scalar1=-1.0))

        def bc32(t):
            return t[:].unsqueeze(2).to_broadcast([128, 16, 32])

        T32a = TMPB[:, 0, :, :]
        T32b = TMPB[:, 1, :, :]
        dv(lambda e: e.tensor_tensor(out=T32a, in0=BEre[:], in1=bc32(zre), op=ALU.mult))
        dv(lambda e: e.tensor_tensor(out=T32b, in0=BEim[:], in1=bc32(zim), op=ALU.mult))
        dv(lambda e: e.tensor_tensor(out=Bbre[:], in0=T32a, in1=T32b, op=ALU.subtract))
        dv(lambda e: e.tensor_tensor(out=T32a, in0=BEim[:], in1=bc32(zre), op=ALU.mult))
        dv(lambda e: e.tensor_tensor(out=T32b, in0=BEre[:], in1=bc32(zim), op=ALU.mult))
        dv(lambda e: e.tensor_tensor(out=Bbim[:], in0=T32a, in1=T32b, op=ALU.add))
        for i, (cn, ce) in enumerate(((CN4[0], CEre), (CN4[1], CEim))):
            for k in range(4):
                S.op("pe", lambda e, cn=cn, k=k: e.transpose(out=PS[0][:, 128 * k:128 * k + 128], in_=cn[:, k, :], identity=ident[:]),
                     reads=[p0, cbuf], writes=[PSB[0]], sig=(k == 3))
            S.op("act", lambda e, ce=ce: e.activation(out=ce[:].rearrange("p g c -> p (g c)"), in_=PS[0][:], func=AF.Copy),
                 reads=[PSB[0]], writes=[p0])

        def bck(t):
            return t[:].unsqueeze(1).to_broadcast([128, 17, 16, 32])

        def bcc(t):
            return t[:, 0:17, :].unsqueeze(3).to_broadcast([128, 17, 16, 32])

        dv(lambda e: e.tensor_tensor(out=CAre[:], in0=bck(CEre), in1=bcc(PWre), op=ALU.mult))
        dv(lambda e: e.tensor_tensor(out=TMPB[:], in0=bck(CEim), in1=bcc(PWim), op=ALU.mult))
        dv(lambda e: e.tensor_tensor(out=CAre[:], in0=CAre[:], in1=TMPB[:], op=ALU.subtract))
        dv(lambda e: e.tensor_tensor(out=CAim[:], in0=bck(CEre), in1=bcc(PWim), op=ALU.mult))
        dv(lambda e: e.tensor_tensor(out=TMPB[:], in0=bck(CEim), in1=bcc(PWre), op=ALU.mult))
        dv(lambda e: e.tensor_tensor(out=CAim[:], in0=CAim[:], in1=TMPB[:], op=ALU.add))
        FFv = FFd.rearrange("p (t r x) -> p t r x", t=16, r=2)
        ffb = p0
        for ri in range(2):
            src = (CAre if ri == 0 else CAim)[:, 1:17, :, :].rearrange("p t g c -> p t (g c)")
            S.op("act", lambda e, src=src, ri=ri: e.activation(out=FFs[:], in_=src, func=AF.Copy, scale=(1.0 if ri == 0 else -1.0)),
                 reads=[p0], writes=[ffb])
            S.dma("sp", FFv[:, :, ri, :], FFs[:], reads=[ffb], writes=[p0], sem_buf=ffb)
        def bcs(t):
            return t[:].unsqueeze(1).to_broadcast([128, 16, 16, 32])

        def bcp(t):
            return t[:, 0:16, :].unsqueeze(3).to_broadcast([128, 16, 16, 32])

        TM16 = TMPB[:, 0:16, :, :]
        for ri in range(2):
            if ri == 0:
                dv(lambda e: e.tensor_tensor(out=SWT[:], in0=bcs(Bbre), in1=bcp(PWre), op=ALU.mult))
                dv(lambda e: e.tensor_tensor(out=TM16, in0=bcs(Bbim), in1=bcp(PWim), op=ALU.mult))
                dv(lambda e: e.tensor_tensor(out=SWT[:], in0=SWT[:], in1=TM16, op=ALU.subtract))
            else:
                dv(lambda e: e.tensor_tensor(out=SWT[:], in0=bcs(Bbre), in1=bcp(PWim), op=ALU.mult))
                dv(lambda e: e.tensor_tensor(out=TM16, in0=bcs(Bbim), in1=bcp(PWre), op=ALU.mult))
                dv(lambda e: e.tensor_tensor(out=SWT[:], in0=SWT[:], in1=TM16, op=ALU.add))
            for kk in range(16):
                pb = kk % 2 + 1
                for k in range(4):
                    S.op("pe", lambda e, kk=kk, k=k, pb=pb: e.transpose(
                        out=PS[pb][:, 128 * k:128 * k + 128],
                        in_=SWT[:, kk, 4 * k:4 * k + 4, :].rearrange("p g c -> p (g c)"), identity=ident[:]),
                        reads=[p0, cbuf], writes=[PSB[pb]], sig=(k == 3))
                S.op("act", lambda e, kk=kk, ri=ri, pb=pb: e.activation(
                    out=SW[:, kk, ri, :, :].rearrange("p k c -> p (k c)"), in_=PS[pb][:], func=AF.Copy),
                    reads=[PSB[pb]], writes=[swb])
        dv(lambda e: e.memset(BZre[:], 0.0))
        dv(lambda e: e.memset(BZim[:], 0.0))
        for q in range(4):
            dv(lambda e, q=q: e.tensor_copy(
                out=BZre[:].rearrange("p (k q) c -> p k q c", q=4)[:, :, q, 32 * q:32 * q + 32],
                in_=Bbre[:].rearrange("p (k q) c -> p k q c", q=4)[:, :, q, :]))
            dv(lambda e, q=q: e.tensor_scalar_mul(
                out=BZim[:].rearrange("p (k q) c -> p k q c", q=4)[:, :, q, 32 * q:32 * q + 32],
                in0=Bbim[:].rearrange("p (k q) c -> p k q c", q=4)[:, :, q, :], scalar1=-1.0))
        TZs = self.sb("TZs", [128, 16, 512], BF16, 162 * KB)
        tzb = p0
        TZv = TZd.rearrange("p (l x) -> p l x", l=16)
        for lag in range(16):
            pb = 3 + lag % 2
            for k in range(4):
                for q in range(4):
                    gp = 4 * k + q
                    osl = PS[pb][:, 128 * k + 32 * q:128 * k + 32 * q + 32]
                    S.op("pe", lambda e, osl=osl, gp=gp, lag=lag: e.matmul(
                        osl, lhsT=BZre[:, gp, :], rhs=CAre[:, lag, gp, :], start=True, stop=False, skip_group_check=True),
                        reads=[p0], writes=[PSB[pb]], sig=False)
                    S.op("pe", lambda e, osl=osl, gp=gp, lag=lag: e.matmul(
                        osl, lhsT=BZim[:, gp, :], rhs=CAim[:, lag, gp, :], start=False, stop=True, skip_group_check=True),
                        reads=[p0], writes=[PSB[pb]], sig=(k == 3 and q == 3))
            S.op("act", lambda e, lag=lag, pb=pb: e.activation(out=TZs[:, lag, :], in_=PS[pb][:], func=AF.Copy),
                 reads=[PSB[pb], p0], writes=[tzb])
        S.dma("sp", TZv, TZs[:], reads=[tzb], writes=[p0], sem_buf=tzb)
        S.barrier()

        Wu = self.sb("Wu", [128, 16, 512], BF16, 62 * KB)
        wub = S.buf("Wu")
        xt = [self.sb("xt%d" % i, [128, D], F32, (78 + 8 * i) * KB) for i in range(2)]
        xtb = S.bufs(2, "xt")
        xTs2 = [self.sb("xTs0", [128, 16, 128], F32, 94 * KB)] * 2
        xTb2 = [S.buf("xTs")] * 2
        sq2 = [self.sb("sq0", [128, 16, 128], BF16, 102 * KB), self.sb("sq1", [128, 16, 128], BF16, 196 * KB)]
        sqb2 = S.bufs(2, "sq")
        hTs = [self.sb("hTs%d" % i, [128, 16, 512], BF16, (106 + 16 * i) * KB) for i in range(2)]
        hTb = S.bufs(2, "hTs")
        ub = self.sb("ubatch", [128, 4, 1024], BF16, 138 * KB)
        ubb = S.buf("ubatch")
        Ssb2 = [self.sb("Ssb%d" % i, [128, 64, 2, 16], BF16, (146 + 4 * i) * KB) for i in range(2)]
        Ssbb2 = S.bufs(2, "Ssb")
        s4q = [self.sb("s4q%d" % i, [128, 2, 4, 256], BF16, (20 + 4 * i) * KB) for i in range(2)] + \
              [self.sb("s4q%d" % (2 + i), [128, 2, 4, 256], BF16, (12 + 4 * i) * KB) for i in range(2)]
        s4qb = S.bufs(4, "s4q")
        Yh = self.sb("Yh", [128, 2, 4, 64], F32, 28 * KB)
        HT4 = [self.sb("HT4_%d" % i, [128, 256], F32, (200 + i) * KB) for i in range(4)]
        Zhs = [self.sb("Zh0", [128, 2, 16, 16], F32, 204 * KB), self.sb("Zh1", [128, 2, 16, 16], F32, 8 * KB)]
        zhb = S.bufs(2, "Zh")
        hb = S.buf("horner")

        def cma(eng, o_re, o_im, x_re, x_im, a_re, a_im, s_re, s_im, T, rb, wb):
            t1, t2, t3, t4 = T
            f = lambda fn, r=rb, w=wb: S.op(eng, fn, reads=list(r), writes=list(w))
            f(lambda e: e.tensor_tensor(out=t1, in0=a_re, in1=x_re, op=ALU.mult))
            f(lambda e: e.tensor_tensor(out=t2, in0=a_im, in1=x_im, op=ALU.mult))
            f(lambda e: e.tensor_tensor(out=t3, in0=a_re, in1=x_im, op=ALU.mult))
            f(lambda e: e.tensor_tensor(out=t4, in0=a_im, in1=x_re, op=ALU.mult))
            f(lambda e: e.tensor_tensor(out=t1, in0=t1, in1=t2, op=ALU.subtract))
            f(lambda e: e.tensor_tensor(out=t3, in0=t3, in1=t4, op=ALU.add))
            f(lambda e: e.tensor_tensor(out=o_re, in0=t1, in1=s_re, op=ALU.add))
            f(lambda e: e.tensor_tensor(out=o_im, in0=t3, in1=s_im, op=ALU.add))
        um = self.sb("u_main", [128, 4, NMAIN], BF16, 154 * KB)
        umb = S.buf("u_main")
        Xs = self.sb("Xs", [128, 2, 16, NCHM], BF16, 172 * KB)
        Xsb = S.buf("Xs")
        rsA = [self.sb("rsA%d" % i, [128, 128], F32, 181 * KB + 1024 * i) for i in range(2)]
        rsB = [self.sb("rsB%d" % i, [128, 128], F32, 181 * KB + 1024 * i + 512) for i in range(2)]
        rsb2 = S.bufs(2, "rs")
        xtmp = self.sb("xtmp", [128, 16, 128], F32, 184 * KB)
        xtmpb = S.buf("xtmp")
        mk = [self.sb("mk%d" % i, [128, 512], F32, (192 + 2 * i) * KB) for i in range(2)]
        mkb = S.bufs(2, "mk")
        S.dma("pool", Wu[:], w_in[:, 1280:1792].rearrange("(kt p) c -> p kt c", p=128), writes=[wub], sem_buf=wub)
        stb = S.buf("state")
        S.op("pool", lambda e: e.memset(X4[0][:], 0.0), writes=[stb])
        S.op("pool", lambda e: e.memset(X4[1][:], 0.0), writes=[stb])
        cur = 0
        XTv = XT.rearrange("k p c -> p k c")
        HTv = HT.rearrange("k p c -> p k c")
        deferred = {}

        def ph1_load(b_):
            S.dma("sp", xt[b_ % 2][:], xw[b_ * 128:(b_ + 1) * 128, :], writes=[xtb[b_ % 2]], sem_buf=xtb[b_ % 2])

        junk = [self.sb("junk0", [128, D], BF16, 102 * KB)] * 2
        junkb = [S.buf("junk")] * 2
        HT4b = [self.sb("HT4b_%d" % i, [128, 256], F32, (196 + i) * KB) for i in range(4)]
        Yhb = self.sb("Yhb", [128, 2, 4, 64], F32, 10 * KB)
        hb2 = S.buf("horner2")
        xsb = [self.sb("xsb%d" % i, [128, D], BF16, (184 + 4 * i) * KB) for i in range(2)]
        xsbb = S.bufs(2, "xsb")
        ssqs = [self.sb("ssq%d" % i, [128, 1], F32, 183 * KB + 64 * i) for i in range(2)]
        ssqb = S.bufs(2, "ssq")

        def ph1_sq(b_):
            ii = b_ % 2
            S.op("act", lambda e, ii=ii: e.activation(out=junk[ii][:], in_=xt[ii][:], func=AF.Square, accum_out=ssqs[ii][:]),
                 reads=[xtb[ii]], writes=[junkb[ii], ssqb[ii]])

        def ph1_transposes(b_):
            ii = b_ % 2
            for j in range(4):
                for kk in range(4):
                    kt = 4 * j + kk
                    S.op("pe", lambda e, j=j, kk=kk, kt=kt, ii=ii: e.transpose(
                        out=PS[j][:, 128 * kk:128 * kk + 128], in_=xt[ii][:, 128 * kt:128 * kt + 128], identity=ident[:]),
                        reads=[xtb[ii], cbuf], writes=[PSB[j]], sig=(kk == 3))

        for blk in range(NBLK):
            i2 = blk % 2
            sub = (blk // 4) % 2
            tcol = (blk % 4) * 128
            xTs, xTb = xTs2[i2], xTb2[i2]
            rs1, rs2, rsb = rsA[i2], rsB[i2], rsb2[i2]
            if blk == 0:
                ph1_load(0)
                ph1_sq(0)
            if blk % 4 == 0:
                mi = (blk // 4) % 2
                S.dma("sp", mk[mi][:], maskw[blk * 128:blk * 128 + 512].partition_broadcast(128), writes=[mkb[mi]],
                      sem_buf=mkb[mi], nonc=True)
            if blk + 1 < NBLK:
                ph1_load(blk + 1)
            S.op("dve", lambda e, rs1=rs1, i2=i2: e.tensor_scalar(out=rs1[:, 0:1], in0=ssqs[i2][:], scalar1=1.0 / D, scalar2=1e-6,
                                                                   op0=ALU.mult, op1=ALU.add), reads=[ssqb[i2]], writes=[rsb])
            S.op("act", lambda e, rs1=rs1, rs2=rs2: e.activation(out=rs2[:, 0:1], in_=rs1[:, 0:1], func=AF.Sqrt), reads=[rsb], writes=[rsb])
            S.op("dve", lambda e, rs1=rs1, rs2=rs2: e.reciprocal(out=rs1[:, 0:1], in_=rs2[:, 0:1]), reads=[rsb], writes=[rsb])
            S.op("act", lambda e, rs1=rs1, i2=i2: e.activation(out=xsb[i2][:], in_=xt[i2][:], func=AF.Copy, scale=rs1[:, 0:1]),
                 reads=[xtb[i2], rsb], writes=[xsbb[i2]])
            if blk + 1 < NBLK:
                ph1_sq(blk + 1)
            if blk >= MB0:
                for j in range(4):
                    pbk = 2 + j % 2
                    for kk in range(4):
                        kt = 4 * j + kk
                        S.op("pe", lambda e, pbk=pbk, kk=kk, kt=kt, i2=i2: e.transpose(
                            out=PS[pbk][:, 128 * kk:128 * kk + 128], in_=xt[i2][:, 128 * kt:128 * kt + 128], identity=ident[:]),
                            reads=[xtb[i2], cbuf], writes=[PSB[pbk]], sig=(kk == 3))
                    S.op("act", lambda e, j=j, pbk=pbk, xTs=xTs: e.activation(
                        out=xTs[:, 4 * j:4 * j + 4, :].rearrange("p k c -> p (k c)"), in_=PS[pbk][:], func=AF.Copy),
                        reads=[PSB[pbk]], writes=[xTb])
                col = (blk - MB0) * 128
                S.dma("sp", XTv[:, :, col:col + 128], xTs[:], reads=[xTb], writes=[], sem_buf=xTb)
            for j in range(2):
                psv = PS[j][:].bitcast(BF16)
                for kk in range(8):
                    kt = 8 * j + kk
                    S.op("pe", lambda e, psv=psv, kk=kk, kt=kt, i2=i2: e.transpose(
                        out=psv[:, 128 * kk:128 * kk + 128], in_=xsb[i2][:, 128 * kt:128 * kt + 128], identity=identb[:]),
                        reads=[xsbb[i2], cbuf2], writes=[PSB[j]], sig=(kk == 7))
                S.op("dve", lambda e, psv=psv, j=j, sub=sub, tcol=tcol: e.tensor_tensor(
                    out=hTs[sub][:, 8 * j:8 * j + 8, tcol:tcol + 128], in0=psv[:, 0:1024].rearrange("p (k c) -> p k c", k=8),
                    in1=g1c[:, 8 * j:8 * j + 8].unsqueeze(2).to_broadcast([128, 8, 128]), op=ALU.mult),
                    reads=[PSB[j], cbuf], writes=[hTb[sub]])
            for fn_ in deferred.pop(blk, []):
                fn_()
            if blk >= MB0:
                S.dma("sp", HTv[:, :, col:col + 128], hTs[sub][:, :, tcol:tcol + 128], reads=[hTb[sub]], writes=[],
                      sem_buf=hTb[sub])
            if blk % 4 == 3:
                mi = (blk // 4) % 2
                boff = ((blk // 4) % 2) * 512
                for m in range(4):
                    pb = 4 + m % 2
                    for kt in range(16):
                        S.op("pe", lambda e, m=m, kt=kt, sub=sub, pb=pb: e.matmul(
                            PS[pb][:], lhsT=Wu[:, kt, 128 * m:128 * m + 128], rhs=hTs[sub][:, kt, :],
                            start=(kt == 0), stop=(kt == 15)),
                            reads=[wub, hTb[sub]], writes=[PSB[pb]], sig=(kt == 15))
                    S.op("dve", lambda e, m=m, pb=pb, mi=mi, boff=boff: e.scalar_tensor_tensor(
                        out=ub[:, m, boff:boff + 512], in0=PS[pb][:], scalar=binc[:, 10 + m:11 + m], in1=mk[mi][:],
                        op0=ALU.add, op1=ALU.mult), reads=[PSB[pb], mkb[mi], cbuf], writes=[ubb])
                    if blk >= MB0 - 2:
                        b0 = blk - 3
                        lo = max(b0, MB0)
                        n = (blk + 1 - lo) * 128
                        so = boff + (lo - b0) * 128
                        do = (lo - MB0) * 128
                        S.op("act", lambda e, m=m, so=so, do=do, n=n: e.activation(
                            out=um[:, m, do:do + n], in_=ub[:, m, so:so + n], func=AF.Copy), reads=[ubb], writes=[umb])
            if blk % 8 == 7:
                bi = blk // 8
                Ssb, Ssbb = Ssb2[bi % 2], Ssbb2[bi % 2]
                Zh, zb = Zhs[bi % 2], zhb[bi % 2]
                def level01(q, bi=bi, Ssb=Ssb, Ssbb=Ssbb):
                    s4, s4b = s4q[q], s4qb[q]
                    for k in range(4):
                        pq = 6 + k % 2
                        for ri in range(2):
                            for b4 in range(4):
                                S.op("pe", lambda e, q=q, k=k, ri=ri, b4=b4, pq=pq: e.matmul(
                                    PS[pq][:, 256 * ri:256 * ri + 256], lhsT=SW[32 * q:32 * q + 32, 3 - b4, ri, k, :],
                                    rhs=ub[32 * q:32 * q + 32, k, b4:1024:4], start=(b4 == 0), stop=(b4 == 3),
                                    tile_position=(32 * q, 0), skip_group_check=True),
                                    reads=[swb, ubb], writes=[PSB[pq]], sig=(ri == 1 and b4 == 3))
                        S.op("act", lambda e, k=k, pq=pq, s4=s4: e.activation(
                            out=s4[:, :, k, :], in_=PS[pq][:].rearrange("p (r c) -> p r c", r=2), func=AF.Copy),
                            reads=[PSB[pq]], writes=[s4b])
                    s4v = s4[:].rearrange("p r k (c a) -> p r k c a", a=4)
                    a4r = A4re[:].rearrange("p (k q) -> p q k", q=4)[:, q, :].unsqueeze(2).to_broadcast([128, 4, 64])
                    a4i = A4im[:].rearrange("p (k q) -> p q k", q=4)[:, q, :].unsqueeze(2).to_broadcast([128, 4, 64])
                    onp = True
                    en_ = "pool" if onp else "dve"
                    Y_ = Yh if onp else Yhb
                    hb_ = hb if onp else hb2
                    tq = [t[:].rearrange("p (k c) -> p k c", k=4) for t in (HT4 if onp else HT4b)]
                    Sq = Ssb[:].rearrange("p c r (k q) -> p r q k c", q=4)
                    cma(en_, Y_[:, 0], Y_[:, 1], s4v[:, 0, :, :, 0], s4v[:, 1, :, :, 0], a4r, a4i,
                        s4v[:, 0, :, :, 1], s4v[:, 1, :, :, 1], tq, [s4b, hb_, cbuf], [hb_])
                    cma(en_, Y_[:, 0], Y_[:, 1], Y_[:, 0], Y_[:, 1], a4r, a4i,
                        s4v[:, 0, :, :, 2], s4v[:, 1, :, :, 2], tq, [s4b, hb_, cbuf], [hb_])
                    cma(en_, Sq[:, 0, q], Sq[:, 1, q], Y_[:, 0], Y_[:, 1], a4r, a4i,
                        s4v[:, 0, :, :, 3], s4v[:, 1, :, :, 3], tq, [s4b, hb_, cbuf], [hb_, Ssbb])

                def batch_tail(bi=bi, Ssb=Ssb, Ssbb=Ssbb, Zh=Zh, zb=zb):
                    nonlocal cur
                    if bi < 5:
                        Sv = Ssb[:].rearrange("p (m a) r g -> p a r m g", a=4)
                        a16r = A16re[:].unsqueeze(1).to_broadcast([128, 16, 16])
                        a16i = A16im[:].unsqueeze(1).to_broadcast([128, 16, 16])
                        tz = [t[:].rearrange("p (m g) -> p m g", m=16) for t in HT4]
                        cma("pool", Zh[:, 0], Zh[:, 1], Sv[:, 0, 0], Sv[:, 0, 1], a16r, a16i, Sv[:, 1, 0], Sv[:, 1, 1], tz,
                            [Ssbb, hb, cbuf], [hb, zb])
                        cma("pool", Zh[:, 0], Zh[:, 1], Zh[:, 0], Zh[:, 1], a16r, a16i, Sv[:, 2, 0], Sv[:, 2, 1], tz,
                            [Ssbb, hb, cbuf, zb], [hb, zb])
                        cma("pool", Zh[:, 0], Zh[:, 1], Zh[:, 0], Zh[:, 1], a16r, a16i, Sv[:, 3, 0], Sv[:, 3, 1], tz,
                            [Ssbb, hb, cbuf, zb], [hb, zb])
                        for m in range(16):
                            xa, xb = X4[cur], X4[1 - cur]
                            pls = lambda fn, r=(stb, cbuf), w=(stb,): S.op("pool", fn, reads=list(r), writes=list(w))
                            pls(lambda e, xa=xa: e.tensor_tensor(out=P1[:], in0=AA64[:], in1=xa[:, 0:2, :], op=ALU.mult))
                            pls(lambda e, xa=xa: e.tensor_tensor(out=P2[:], in0=BB64[:], in1=xa[:, 1:3, :], op=ALU.mult))
                            pls(lambda e: e.tensor_tensor(out=P1[:], in0=P1[:], in1=P2[:], op=ALU.add))
                            S.op("pool", lambda e, xb=xb, m=m, Zh=Zh: e.tensor_tensor(out=xb[:, 0:2, :], in0=P1[:], in1=Zh[:, :, m, :], op=ALU.add),
                                 reads=[stb, zb], writes=[stb])
                            pls(lambda e, xb=xb: e.tensor_copy(out=xb[:, 2, :], in_=xb[:, 0, :]))
                            cur = 1 - cur
                    else:
                        jb = bi * 64
                        for jj in range(64):
                            j = jb + jj
                            xa, xb = X4[cur], X4[1 - cur]
                            if j >= NCH0:
                                S.op("pool", lambda e, xa=xa, j=j: e.tensor_copy(out=Xs[:, :, :, j - NCH0], in_=xa[:, 0:2, :]),
                                     reads=[stb], writes=[Xsb])
                            pls = lambda fn, r=(stb, cbuf), w=(stb,): S.op("pool", fn, reads=list(r), writes=list(w))
                            pls(lambda e, xa=xa: e.tensor_tensor(out=P1[:], in0=AAt[:], in1=xa[:, 0:2, :], op=ALU.mult))
                            pls(lambda e, xa=xa: e.tensor_tensor(out=P2[:], in0=BBt[:], in1=xa[:, 1:3, :], op=ALU.mult))
                            pls(lambda e: e.tensor_tensor(out=P1[:], in0=P1[:], in1=P2[:], op=ALU.add))
                            S.op("pool", lambda e, xb=xb, jj=jj, Ssb=Ssb: e.tensor_tensor(out=xb[:, 0:2, :], in0=P1[:], in1=Ssb[:, jj, :, :], op=ALU.add),
                                 reads=[stb, Ssbb], writes=[stb])
                            pls(lambda e, xb=xb: e.tensor_copy(out=xb[:, 2, :], in_=xb[:, 0, :]))
                            cur = 1 - cur
                level01(0)
                level01(1)
                if blk + 2 < NBLK:
                    deferred.setdefault(blk + 1, []).append(lambda f=level01: f(2))
                    deferred.setdefault(blk + 2, []).append(lambda f=level01: f(3))
                    deferred.setdefault(blk + 2, []).append(batch_tail)
                else:
                    level01(2)
                    level01(3)
                    batch_tail()
        if UM is not None:
            S.dma("sp", UM.rearrange("k p c -> p k c"), um[:], reads=[umb], writes=[], sem_buf=umb)
        S.barrier()
        finals = []
        if UM is not None:
            finals.append(umb)
        if self.stop == "ph1":
            return S, finals
        FF = self.sb("FF", [128, 16, 2, 16, 32], BF16, 62 * KB)
        TZ = self.sb("TZ", [128, 16, 4, 128], BF16, 94 * KB)
        ffb2 = S.buf("FF")
        tzb2 = S.buf("TZ")
        S.dma("sp", FF[:].rearrange("p t r g c -> p (t r g c)"), FFd, writes=[ffb2], sem_buf=ffb2)
        S.dma("sp", TZ[:].rearrange("p l k c -> p (l k c)"), TZd, writes=[tzb2], sem_buf=tzb2)
        yg = self.sb("yg", [128, 4, NMAIN], BF16, 12 * KB)
        ygb = S.buf("yg")
        ytmp = [self.sb("ytmp%d" % i, [128, NCHM], F32, 110 * KB + i * 1024) for i in range(2)]
        ytb = S.bufs(2, "ytmp")
        it = 0
        for k in range(4):
            for tau in range(16):
                pb = it % 4
                yi = it % 2
                it += 1
                for s in range(tau + 1):
                    S.op("pe", lambda e, pb=pb, tau=tau, s=s, k=k: e.matmul(
                        PS[pb][:, 0:NCHM], lhsT=TZ[:, tau - s, k, :], rhs=um[:, k, s:NMAIN:16],
                        start=(s == 0), stop=False, skip_group_check=True),
                        reads=[tzb2, umb], writes=[PSB[pb]], sig=False)
                for q in range(4):
                    for ri in range(2):
                        last = (q == 3 and ri == 1)
                        S.op("pe", lambda e, pb=pb, tau=tau, q=q, ri=ri, k=k, last=last: e.matmul(
                            PS[pb][32 * q:32 * q + 32, 0:NCHM], lhsT=FF[:, tau, ri, 4 * k + q, :], rhs=Xs[:, ri, 4 * k + q, :],
                            start=False, stop=last, tile_position=(0, 32 * q), skip_group_check=True),
                            reads=[ffb2, Xsb], writes=[PSB[pb]], sig=last)
                S.op("dve", lambda e, pb=pb, tau=tau, k=k, yi=yi: e.scalar_tensor_tensor(
                    out=ytmp[yi][:], in0=um[:, k, tau:NMAIN:16], scalar=dcol[:, k:k + 1], in1=PS[pb][:, 0:NCHM],
                    op0=ALU.mult, op1=ALU.add), reads=[PSB[pb], umb, cbuf], writes=[ytb[yi]])
                S.op("act", lambda e, tau=tau, k=k, yi=yi: e.activation(
                    out=yg[:, k, tau:NMAIN:16], in_=ytmp[yi][:], func=AF.Gelu), reads=[ytb[yi]], writes=[ygb])
        if YG is not None:
            S.dma("sp", YG.rearrange("k p c -> p k c"), yg[:], reads=[ygb], writes=[], sem_buf=ygb)
            finals.append(ygb)
        S.barrier()
        if self.stop == "ph1b":
            return S, finals
        self.pbn = 0

        def nb():
            self.pbn = (self.pbn + 1) % 8
            return self.pbn

        MT = [(0, 512), (512, 512), (1024, 512), (1536, 512), (2048, 256)]
        VT = [(128, 512), (640, 512), (1152, 512), (1664, 512), (2176, 128)]

        def loadw(eng, dst, dbuf, src, nkt):
            S.dma(eng, dst[:, 0:nkt, :], src.rearrange("(kt p) c -> p kt c", p=128), writes=[dbuf], sem_buf=dbuf)

        SG = self.dscr("SG", [32, 128, NMAIN], BF16)
        X1T = self.dscr("X1T", [16, 128, NMAIN], F32)
        ACTd = self.dscr("ACTd", [44, 128, 2048], BF16)
        X2T = self.dscr("X2T", [16, 128, 2048], F32)
        qT = self.sb("qT", [128, 8, NMAIN], BF16, 30 * KB)
        qb = S.buf("qT")
        kd = [self.sb("kd%d" % g, [128, NMAIN], BF16, 66 * KB + g * 4608) for g in range(2)]
        kdb = S.bufs(2, "kd")
        hTm = self.sb("hTm", [128, 16, NMAIN], BF16, 75 * KB)
        hTmb = S.buf("hTm")
        wt = [self.sb("wt%d" % i, [128, 16, 128], BF16, (147 + 4 * i) * KB) for i in range(2)]
        wtb = S.bufs(2, "wt")
        vsb = self.sb("vsb", [128, 18, 128], BF16, 155 * KB)
        vsbb = S.buf("vsb")
        gst = [self.sb("gst%d" % i, [128, 512], BF16, (160 + i) * KB) for i in range(2)]
        gstb = S.bufs(2, "gst")
        bvb = self.sb("bvb", [128, 128], F32, 162 * KB)
        bkd = self.sb("bkd", [128, 2], F32, 163 * KB)
        sinkc = self.sb("sinkc", [128, 16], F32, 163 * KB + 64)
        NM = self.sb("NM", [128, 128], F32, 164 * KB)
        c2b = S.buf("c2")
        S.dma("sp", hTm[:], HT.rearrange("k p c -> p k c"), writes=[hTmb], sem_buf=hTmb)
        S.dma("sp", bvb[:], b_in[1152:1280].partition_broadcast(128), writes=[c2b], sem_buf=c2b, nonc=True)
        for g in range(2):
            for hh in range(2):
                S.dma("sp", bkd[64 * hh:64 * hh + 64, g:g + 1], b_in[1024 + 64 * g:1088 + 64 * g].rearrange("(p o) -> p o", o=1),
                      writes=[c2b], sem_buf=c2b, nonc=True)
        S.dma("sp", sinkc[:], sinks.partition_broadcast(128), writes=[c2b], sem_buf=c2b, nonc=True)
        S.dma("sp", NM[:], maskw[MB0 * 128 + 128:MB0 * 128 + 256].partition_broadcast(128), writes=[c2b], sem_buf=c2b, nonc=True)
        S.op("dve", lambda e: e.tensor_scalar(out=NM[:], in0=NM[:], scalar1=-1.0, scalar2=30000.0, op0=ALU.add, op1=ALU.mult),
             reads=[c2b], writes=[c2b])
        wi = 0
        jobs = [("q", cb) for cb in range(8)] + [("k", g) for g in range(2)] + [("g", cb) for cb in range(14, 46)]
        gi = 0
        for kind, cb in jobs:
            w = wt[wi % 2]
            wb = wtb[wi % 2]
            wi += 1
            if kind == "k":
                for hh in range(2):
                    S.dma("pool", w[:, :, 64 * hh:64 * hh + 64],
                          w_in[:, 1024 + 64 * cb:1088 + 64 * cb].rearrange("(kt p) c -> p kt c", p=128), writes=[wb], sem_buf=wb)
            else:
                loadw("pool", w, wb, w_in[:, 128 * cb:128 * cb + 128], 16)
            for (c0, n) in (MT if kind == "k" else VT):
                pb = nb()
                for kt in range(16):
                    S.op("pe", lambda e, pb=pb, w=w, kt=kt, c0=c0, n=n: e.matmul(
                        PS[pb][:, 0:n], lhsT=w[:, kt, :], rhs=hTm[:, kt, c0:c0 + n], start=(kt == 0), stop=(kt == 15)),
                        reads=[wb, hTmb], writes=[PSB[pb]], sig=(kt == 15))
                if kind == "q":
                    S.op("act", lambda e, pb=pb, cb=cb, c0=c0, n=n: e.activation(
                        out=qT[:, cb, c0:c0 + n], in_=PS[pb][:, 0:n], func=AF.Identity, bias=binc[:, cb:cb + 1]),
                        reads=[PSB[pb], cbuf], writes=[qb])
                elif kind == "k":
                    S.op("act", lambda e, pb=pb, cb=cb, c0=c0, n=n: e.activation(
                        out=kd[cb][:, c0:c0 + n], in_=PS[pb][:, 0:n], func=AF.Identity, bias=bkd[:, cb:cb + 1]),
                        reads=[PSB[pb], c2b], writes=[kdb[cb]])
                else:
                    gs_ = gst[gi % 2]
                    gsb = gstb[gi % 2]
                    gi += 1
                    S.op("act", lambda e, pb=pb, cb=cb, n=n, gs_=gs_: e.activation(
                        out=gs_[:, 0:n], in_=PS[pb][:, 0:n], func=AF.Sigmoid, bias=binc[:, cb:cb + 1]),
                        reads=[PSB[pb], cbuf], writes=[gsb])
                    S.dma("sp", SG[cb - 14][:, c0:c0 + n], gs_[:, 0:n], reads=[gsb], writes=[], sem_buf=gsb)
        w = wt[wi % 2]
        wb = wtb[wi % 2]
        wi += 1
        loadw("pool", w, wb, w_in[:, 1152:1280], 16)
        for mb in range(18):
            pb = nb()
            for kt in range(16):
                S.op("pe", lambda e, pb=pb, w=w, kt=kt, mb=mb: e.matmul(
                    PS[pb][:, 0:128], lhsT=hTm[:, kt, 128 * mb:128 * mb + 128], rhs=w[:, kt, :], start=(kt == 0), stop=(kt == 15)),
                    reads=[wb, hTmb], writes=[PSB[pb]], sig=(kt == 15))
            S.op("dve", lambda e, pb=pb, mb=mb: e.tensor_tensor(out=vsb[:, mb, :], in0=PS[pb][:, 0:128], in1=bvb[:], op=ALU.add),
                 reads=[PSB[pb], c2b], writes=[vsbb])
        S.barrier()
        if self.stop == "ph2":
            return S, finals
        attnT = self.sb("attnT", [128, 8, NMAIN], BF16, 75 * KB)
        atb = S.buf("attnT")
        atbs = S.bufs(8, "attnTk")
        AB = self.sb("AB", [128, 16, 256], F32, 111 * KB)
        abb = S.buf("AB")
        scS = [self.sb("sc0", [128, 16, 256], F32, 127 * KB), self.sb("sc1", [128, 16, 256], F32, 174 * KB)]
        scbS = [S.bufs(16, "sc0_"), S.bufs(16, "sc1_")]
        pnS = [self.sb("pn0", [128, 16, 256], BF16, 143 * KB), self.sb("pn1", [128, 16, 256], BF16, 190 * KB)]
        pnbS = [S.bufs(16, "pn0_"), S.bufs(16, "pn1_")]
        pTsS = [self.sb("pTs0", [128, 16, 256], BF16, 165 * KB), self.sb("pTs1", [128, 16, 256], BF16, 198 * KB)]
        pTbS = [S.bufs(16, "pT0_"), S.bufs(16, "pT1_")]
        vecS = []
        for i in range(2):
            vo = (173 if i == 0 else 151) * KB
            vecS.append([self.sb("av%d_%d" % (i, t), [128, 16], F32, vo + 64 * t) for t in range(5)])
        vbS = S.bufs(2, "attvec")
        S.dma("sp", AB[:], abias, writes=[abb], sem_buf=abb)
        def att_setup(mb):
            c0 = 128 * mb
            return (c0,) + (scS[mb % 2], scbS[mb % 2], pnS[mb % 2], pnbS[mb % 2], pTsS[mb % 2], pTbS[mb % 2]) + tuple(vecS[mb % 2]) + (vbS[mb % 2],)

        def stageA(mb):
            c0, sc, scb, pn, pnb, pTs, pTb, mx, nmx, rsum, es, rden, vb_ = att_setup(mb)
            for base in (0, 4, 8, 12):
                for par in (0, 1):
                    h0 = base + par
                    g = h0 // 8
                    r0 = 64 * par
                    pb = nb()
                    for i_, h in enumerate((h0, h0 + 2)):
                        S.op("pe", lambda e, pb=pb, h=h, g=g, r0=r0, i_=i_: e.matmul(
                            PS[pb][:, 256 * i_:256 * i_ + 256], lhsT=qT[r0:r0 + 64, h // 2, c0:c0 + 128],
                            rhs=kd[g][r0:r0 + 64, c0 - 128:c0 + 128], start=True, stop=True, skip_group_check=True),
                            reads=[qb, kdb[g]], writes=[PSB[pb]], sig=(i_ == 1))
                    S.op("dve", lambda e, pb=pb, h0=h0: e.scalar_tensor_tensor(
                        out=sc[:, h0:h0 + 3:2, :], in0=PS[pb][:].rearrange("p (h c) -> p h c", h=2), scalar=0.125,
                        in1=AB[:, h0:h0 + 3:2, :], op0=ALU.mult, op1=ALU.add),
                        reads=[PSB[pb], abb], writes=[scb[h0], scb[h0 + 2]])
                    if mb == 2:
                        for h in (h0, h0 + 2):
                            S.op("dve", lambda e, h=h: e.tensor_tensor(out=sc[:, h, 0:128], in0=sc[:, h, 0:128], in1=NM[:], op=ALU.add),
                                 reads=[scb[h], c2b], writes=[scb[h]])
                for h in (base + 1, base + 3):
                    S.op("dve", lambda e, h=h: e.reduce_max(out=mx[:, h - 1:h + 1], in_=sc[:, h - 1:h + 1, :],
                                                             axis=mybir.AxisListType.X),
                         reads=[scb[h - 1], scb[h]], writes=[vb_])
            S.op("dve", lambda e, mx=mx: e.tensor_tensor(out=mx[:], in0=mx[:], in1=sinkc[:], op=ALU.max), reads=[vb_, c2b], writes=[vb_])
            S.op("dve", lambda e, mx=mx, nmx=nmx: e.tensor_scalar_mul(out=nmx[:], in0=mx[:], scalar1=-1.0), reads=[vb_], writes=[vb_])
            S.op("dve", lambda e, es=es, nmx=nmx: e.tensor_tensor(out=es[:], in0=sinkc[:], in1=nmx[:], op=ALU.add), reads=[vb_, c2b], writes=[vb_])
            S.op("act", lambda e, es=es: e.activation(out=es[:], in_=es[:], func=AF.Exp), reads=[vb_], writes=[vb_])
            for h in range(16):
                S.op("act", lambda e, h=h, sc=sc, nmx=nmx, rsum=rsum: e.activation(out=sc[:, h, :], in_=sc[:, h, :], func=AF.Exp, bias=nmx[:, h:h + 1],
                                                          accum_out=rsum[:, h:h + 1]), reads=[scb[h], vb_], writes=[scb[h], vb_])

        def stageB(mb):
            c0, sc, scb, pn, pnb, pTs, pTb, mx, nmx, rsum, es, rden, vb_ = att_setup(mb)
            S.op("dve", lambda e, rden=rden, rsum=rsum, es=es: e.tensor_tensor(out=rden[:], in0=rsum[:], in1=es[:], op=ALU.add), reads=[vb_], writes=[vb_])
            S.op("dve", lambda e, rden=rden: e.reciprocal(out=rden[:], in_=rden[:]), reads=[vb_], writes=[vb_])
            banks = {}

            def b_first(hp):
                pb = nb()
                banks[hp] = pb
                psv = PS[pb][:].bitcast(BF16)
                S.op("dve", lambda e, hp=hp: e.tensor_tensor(
                    out=pn[:, 2 * hp:2 * hp + 2, :], in0=sc[:, 2 * hp:2 * hp + 2, :],
                    in1=rden[:, 2 * hp:2 * hp + 2].unsqueeze(2).to_broadcast([128, 2, 256]), op=ALU.mult),
                    reads=[scb[2 * hp], scb[2 * hp + 1], vb_], writes=[pnb[2 * hp], pnb[2 * hp + 1]])
                for h in (2 * hp, 2 * hp + 1):
                    for hf in range(2):
                        o0 = 256 * (h % 2) + 128 * hf
                        S.op("pe", lambda e, psv=psv, h=h, hf=hf, o0=o0: e.transpose(
                            out=psv[:, o0:o0 + 128], in_=pn[:, h, 128 * hf:128 * hf + 128], identity=identb[:]),
                            reads=[pnb[h], cbuf2], writes=[PSB[pb]], sig=(h % 2 == 1 and hf == 1))
                S.op("act", lambda e, psv=psv, hp=hp: e.activation(
                    out=pTs[:, 2 * hp:2 * hp + 2, :].rearrange("p h c -> p (h c)"), in_=psv[:, 0:512], func=AF.Copy),
                    reads=[PSB[pb]], writes=[pTb[2 * hp]])

            def b_second(hp):
                pb2 = nb()
                for h in (2 * hp, 2 * hp + 1):
                    g = h // 8
                    r0 = 64 * (h % 2)
                    for hf in range(2):
                        S.op("pe", lambda e, pb2=pb2, h=h, hf=hf, g=g, r0=r0: e.matmul(
                            PS[pb2][r0:r0 + 64, 0:128], lhsT=vsb[:, mb - 1 + hf, 64 * g:64 * g + 64],
                            rhs=pTs[:, h, 128 * hf:128 * hf + 128], start=(hf == 0), stop=(hf == 1), skip_group_check=True),
                            reads=[vsbb, pTb[2 * hp]], writes=[PSB[pb2]], sig=(h % 2 == 1 and hf == 1))
                S.op("act", lambda e, pb2=pb2, hp=hp: e.activation(
                    out=attnT[:, hp, c0:c0 + 128], in_=PS[pb2][:, 0:128], func=AF.Copy),
                    reads=[PSB[pb2]], writes=[atbs[hp]])

            for i in range(9):
                if i < 8:
                    b_first(i)
                if i >= 1:
                    b_second(i - 1)

        stageA(1)
        for mb in range(1, 18):
            if mb + 1 < 18:
                stageA(mb + 1)
            stageB(mb)
        if "ATT" in dbg:
            ATT = self.dscr("ATT", [8, 128, NMAIN], BF16)
            S.dma("sp", ATT.rearrange("k p c -> p k c"), attnT[:], reads=[atb] + atbs, writes=[], sem_buf=atb)
            finals.append(atb)
        S.barrier()
        if self.stop == "ph3":
            return S, finals
        ssmT = self.sb("ssmT", [128, 4, NMAIN], BF16, 30 * KB)
        ssb = S.buf("ssmT")
        wg2 = [self.sb("wg2_%d" % i, [128, 4, 256], BF16, (48 + 2 * i) * KB) for i in range(2)]
        wg2b = S.bufs(2, "wg2")
        sgt = [self.sb("sgt%d" % i, [128, 512], F32, (52 + 2 * i) * KB) for i in range(2)]
        sgtb = S.bufs(2, "sgt")
        it = 0
        for c in range(4):
            w = wg2[c % 2]
            wb = wg2b[c % 2]
            S.dma("pool", w[:, :, 0:128], w_glu[:, 128 * c:128 * c + 128].rearrange("(kt p) c -> p kt c", p=128), writes=[wb], sem_buf=wb)
            S.dma("pool", w[:, :, 128:256], w_glu[:, 512 + 128 * c:640 + 128 * c].rearrange("(kt p) c -> p kt c", p=128),
                  writes=[wb], sem_buf=wb)
            for (c0, n) in VT:
                pv, pg = nb(), nb()
                for kt in range(4):
                    S.op("pe", lambda e, pv=pv, w=w, kt=kt, c0=c0, n=n: e.matmul(
                        PS[pv][:, 0:n], lhsT=w[:, kt, 0:128], rhs=yg[:, kt, c0:c0 + n], start=(kt == 0), stop=(kt == 3)),
                        reads=[wb, ygb], writes=[PSB[pv]], sig=(kt == 3))
                for kt in range(4):
                    S.op("pe", lambda e, pg=pg, w=w, kt=kt, c0=c0, n=n: e.matmul(
                        PS[pg][:, 0:n], lhsT=w[:, kt, 128:256], rhs=yg[:, kt, c0:c0 + n], start=(kt == 0), stop=(kt == 3)),
                        reads=[wb, ygb], writes=[PSB[pg]], sig=(kt == 3))
                si = it % 2
                it += 1
                S.op("act", lambda e, pg=pg, c=c, n=n, si=si: e.activation(
                    out=sgt[si][:, 0:n], in_=PS[pg][:, 0:n], func=AF.Sigmoid, bias=bgluc[:, 4 + c:5 + c]),
                    reads=[PSB[pg], cbuf], writes=[sgtb[si]])
                S.op("dve", lambda e, pv=pv, c=c, c0=c0, n=n, si=si: e.scalar_tensor_tensor(
                    out=ssmT[:, c, c0:c0 + n], in0=PS[pv][:, 0:n], scalar=bgluc[:, c:c + 1], in1=sgt[si][:, 0:n],
                    op0=ALU.add, op1=ALU.mult), reads=[PSB[pv], sgtb[si], cbuf], writes=[ssb])
        S.barrier()
        mgT = self.sb("mgT", [128, 16, NMAIN], BF16, 111 * KB)
        mgb = S.buf("mgT")
        wa = [self.sb("wa%d" % i, [128, 8, 128], BF16, (48 + 2 * i) * KB) for i in range(2)]
        wab = S.bufs(2, "wa")
        ws_ = [self.sb("ws%d" % i, [128, 4, 128], BF16, (52 + i) * KB) for i in range(2)]
        wsb = S.bufs(2, "ws")
        sga = [self.sb("sga%d" % i, [128, NMAIN], BF16, 54 * KB + i * 4608) for i in range(2)]
        sgab = S.bufs(2, "sga")
        sgs = [self.sb("sgs%d" % i, [128, NMAIN], BF16, 63 * KB + i * 4608) for i in range(2)]
        sgsb = S.bufs(2, "sgs")
        t1 = [self.sb("t1_%d" % i, [128, 512], F32, (183 + 2 * i) * KB) for i in range(2)]
        t1b = S.bufs(2, "t1")
        t2 = [self.sb("t2_%d" % i, [128, 512], F32, (187 + 2 * i) * KB) for i in range(2)]
        t2b = S.bufs(2, "t2")
        it = 0
        for c in range(16):
            i2 = c % 2
            loadw("pool", wa[i2], wab[i2], w_ba[:, 128 * c:128 * c + 128], 8)
            loadw("pool", ws_[i2], wsb[i2], w_bs[:, 128 * c:128 * c + 128], 4)
            S.dma("sp", sga[i2][:], SG[c], writes=[sgab[i2]], sem_buf=sgab[i2])
            S.dma("sp", sgs[i2][:], SG[16 + c], writes=[sgsb[i2]], sem_buf=sgsb[i2])
            for (c0, n) in VT:
                pa, ps_ = nb(), nb()
                for kt in range(8):
                    S.op("pe", lambda e, pa=pa, i2=i2, kt=kt, c0=c0, n=n: e.matmul(
                        PS[pa][:, 0:n], lhsT=wa[i2][:, kt, :], rhs=attnT[:, kt, c0:c0 + n], start=(kt == 0), stop=(kt == 7)),
                        reads=[wab[i2], atbs[kt]], writes=[PSB[pa]], sig=(kt == 7))
                for kt in range(4):
                    S.op("pe", lambda e, ps_=ps_, i2=i2, kt=kt, c0=c0, n=n: e.matmul(
                        PS[ps_][:, 0:n], lhsT=ws_[i2][:, kt, :], rhs=ssmT[:, kt, c0:c0 + n], start=(kt == 0), stop=(kt == 3)),
                        reads=[wsb[i2], ssb], writes=[PSB[ps_]], sig=(kt == 3))
                ti = it % 2
                it += 1
                S.op("dve", lambda e, pa=pa, i2=i2, c0=c0, n=n, ti=ti: e.tensor_tensor(
                    out=t1[ti][:, 0:n], in0=PS[pa][:, 0:n], in1=sga[i2][:, c0:c0 + n], op=ALU.mult),
                    reads=[PSB[pa], sgab[i2]], writes=[t1b[ti]])
                S.op("dve", lambda e, ps_=ps_, i2=i2, c0=c0, n=n, ti=ti: e.tensor_tensor(
                    out=t2[ti][:, 0:n], in0=PS[ps_][:, 0:n], in1=sgs[i2][:, c0:c0 + n], op=ALU.mult),
                    reads=[PSB[ps_], sgsb[i2]], writes=[t2b[ti]])
                S.op("dve", lambda e, c=c, c0=c0, n=n, ti=ti: e.tensor_tensor(
                    out=mgT[:, c, c0:c0 + n], in0=t1[ti][:, 0:n], in1=t2[ti][:, 0:n], op=ALU.add),
                    reads=[t1b[ti], t2b[ti]], writes=[mgb])
        S.barrier()
        wo = [self.sb("wo%d" % i, [128, 16, 128], BF16, (12 + 4 * i) * KB) for i in range(2)]
        wob = S.bufs(2, "wo")
        xl = [self.sb("xl%d" % i, [128, 512], F32, (20 + 2 * i) * KB) for i in range(2)]
        xlb = S.bufs(2, "xl")
        x1s = [self.sb("x1s%d" % i, [128, 512], F32, (24 + 2 * i) * KB) for i in range(2)]
        x1sb = S.bufs(2, "x1s")
        sqt = [self.sb("sqt%d" % i, [128, 512], F32, (28 + 2 * i) * KB) for i in range(2)]
        sqtb = S.bufs(2, "sqt")
        ssacc = self.sb("ssacc", [128, NMAIN], F32, 32 * KB)
        ssab = S.buf("ssacc")
        rstd = self.sb("rstd", [128, NMAIN], F32, 183 * KB)
        rstb = S.buf("rstd")
        h2T = self.sb("h2T", [128, 16, 2050], BF16, 42 * KB)
        h2b = S.buf("h2T")
        S.op("dve", lambda e: e.memset(ssacc[:], 0.0), writes=[ssab])
        it = 0
        for c in range(16):
            i2 = c % 2
            loadw("pool", wo[i2], wob[i2], w_out[:, 128 * c:128 * c + 128], 16)
            for (c0, n) in VT:
                ti = it % 2
                it += 1
                S.dma("pool", xl[ti][:, 0:n], XT[c][:, c0:c0 + n], writes=[xlb[ti]], sem_buf=xlb[ti])
                pb = nb()
                for kt in range(16):
                    S.op("pe", lambda e, pb=pb, i2=i2, kt=kt, c0=c0, n=n: e.matmul(
                        PS[pb][:, 0:n], lhsT=wo[i2][:, kt, :], rhs=mgT[:, kt, c0:c0 + n], start=(kt == 0), stop=(kt == 15)),
                        reads=[wob[i2], mgb], writes=[PSB[pb]], sig=(kt == 15))
                S.op("dve", lambda e, pb=pb, n=n, ti=ti: e.tensor_tensor(
                    out=x1s[ti][:, 0:n], in0=PS[pb][:, 0:n], in1=xl[ti][:, 0:n], op=ALU.add),
                    reads=[PSB[pb], xlb[ti]], writes=[x1sb[ti]])
                S.dma("sp", X1T[c][:, c0:c0 + n], x1s[ti][:, 0:n], reads=[x1sb[ti]], writes=[], sem_buf=x1sb[ti])
                S.op("act", lambda e, n=n, ti=ti: e.activation(out=sqt[ti][:, 0:n], in_=x1s[ti][:, 0:n], func=AF.Square),
                     reads=[x1sb[ti]], writes=[sqtb[ti]])
                lo_ = max(c0, 254)
                S.op("act", lambda e, c=c, c0=c0, n=n, ti=ti, lo_=lo_: e.activation(
                    out=h2T[:, c, lo_ - 254:c0 + n - 254], in_=x1s[ti][:, lo_ - c0:n], func=AF.Copy, scale=g2c[:, c:c + 1]),
                    reads=[x1sb[ti], cbuf], writes=[h2b])
                S.op("dve", lambda e, c0=c0, n=n, ti=ti: e.tensor_tensor(
                    out=ssacc[:, c0:c0 + n], in0=ssacc[:, c0:c0 + n], in1=sqt[ti][:, 0:n], op=ALU.add),
                    reads=[sqtb[ti], ssab], writes=[ssab])

        def rstd_from(acc, accb, dst, dstb, tiles):
            for (c0, n) in tiles:
                pb = nb()
                S.op("pe", lambda e, pb=pb, c0=c0, n=n: e.matmul(PS[pb][:, 0:n], lhsT=onesf[:], rhs=acc[:, c0:c0 + n],
                                                                   start=True, stop=True),
                     reads=[accb, cbuf], writes=[PSB[pb]], sig=True)
                S.op("dve", lambda e, pb=pb, c0=c0, n=n: e.tensor_scalar(
                    out=dst[:, c0:c0 + n], in0=PS[pb][:, 0:n], scalar1=1.0 / D, scalar2=1e-6, op0=ALU.mult, op1=ALU.add),
                    reads=[PSB[pb]], writes=[dstb])
                S.op("act", lambda e, c0=c0, n=n: e.activation(out=dst[:, c0:c0 + n], in_=dst[:, c0:c0 + n], func=AF.Sqrt),
                     reads=[dstb], writes=[dstb])
                S.op("dve", lambda e, c0=c0, n=n: e.reciprocal(out=dst[:, c0:c0 + n], in_=dst[:, c0:c0 + n]),
                     reads=[dstb], writes=[dstb])

        rstd_from(ssacc, ssab, rstd, rstb, VT)
        for c in range(16):
            S.op("dve", lambda e, c=c: e.tensor_tensor(out=h2T[:, c, :], in0=h2T[:, c, :], in1=rstd[:, 254:2304], op=ALU.mult),
                 reads=[rstb, h2b], writes=[h2b])
        S.barrier()
        wu2 = [self.sb("wu2_%d" % i, [128, 16, 256], BF16, (111 + 8 * i) * KB) for i in range(2)]
        wu2b = S.bufs(2, "wu2")
        gbufs = [self.sb("gbuf%d" % i, [128, 2050], F32, (127 + 9 * i) * KB) for i in range(2)]
        gbbs = S.bufs(2, "gbuf")
        vbufs = [self.sb("vbuf%d" % i, [128, 2048], F32, (145 + 8 * i) * KB) for i in range(2)]
        vbbs = S.bufs(2, "vbuf")
        tcvs = [self.sb("tcv%d" % i, [128, 2048], F32, (161 + 8 * i) * KB) for i in range(2)]
        tcbs = S.bufs(2, "tcv")
        tgs = [self.sb("tg%d" % i, [128, 2048], F32, (177 + 8 * i) * KB) for i in range(2)]
        tgbs = S.bufs(2, "tg")
        ast = [self.sb("ast%d" % i, [128, 2048], BF16, (193 + 4 * i) * KB) for i in range(2)]
        astb = S.bufs(2, "ast")
        mk2 = self.sb("mk2", [128, 2], F32, 201 * KB)
        mk2b = S.buf("mk2")
        S.dma("sp", mk2[:], maskw[MB0 * 128 + 254:MB0 * 128 + 256].partition_broadcast(128), writes=[mk2b], sem_buf=mk2b, nonc=True)
        for j in range(44):
            i2 = j % 2
            w = wu2[i2]
            wb = wu2b[i2]
            gbuf, gbb, vbuf, vbb, tcv, tcb, tg, tgb = gbufs[i2], gbbs[i2], vbufs[i2], vbbs[i2], tcvs[i2], tcbs[i2], tgs[i2], tgbs[i2]
            S.dma("pool", w[:, :, 0:128], w_up[:, 128 * j:128 * j + 128].rearrange("(kt p) c -> p kt c", p=128), writes=[wb], sem_buf=wb)
            S.dma("pool", w[:, :, 128:256], w_up[:, DFF + 128 * j:DFF + 128 * j + 128].rearrange("(kt p) c -> p kt c", p=128),
                  writes=[wb], sem_buf=wb)
            ph = nb()
            for kt in range(16):
                S.op("pe", lambda e, ph=ph, w=w, kt=kt: e.matmul(PS[ph][:, 0:2], lhsT=w[:, kt, 128:256], rhs=h2T[:, kt, 0:2],
                                                                  start=(kt == 0), stop=(kt == 15)),
                     reads=[wb, h2b], writes=[PSB[ph]], sig=(kt == 15))
            S.op("dve", lambda e, ph=ph, gbuf=gbuf: e.tensor_tensor(out=gbuf[:, 0:2], in0=PS[ph][:, 0:2], in1=mk2[:], op=ALU.mult),
                 reads=[PSB[ph], mk2b], writes=[gbb])
            for t in range(4):
                pv, pg = nb(), nb()
                for kt in range(16):
                    S.op("pe", lambda e, pv=pv, w=w, kt=kt, t=t: e.matmul(
                        PS[pv][:], lhsT=w[:, kt, 0:128], rhs=h2T[:, kt, 2 + 512 * t:514 + 512 * t], start=(kt == 0), stop=(kt == 15)),
                        reads=[wb, h2b], writes=[PSB[pv]], sig=(kt == 15))
                for kt in range(16):
                    S.op("pe", lambda e, pg=pg, w=w, kt=kt, t=t: e.matmul(
                        PS[pg][:], lhsT=w[:, kt, 128:256], rhs=h2T[:, kt, 2 + 512 * t:514 + 512 * t], start=(kt == 0), stop=(kt == 15)),
                        reads=[wb, h2b], writes=[PSB[pg]], sig=(kt == 15))
                S.op("act", lambda e, pg=pg, t=t, gbuf=gbuf: e.activation(out=gbuf[:, 2 + 512 * t:514 + 512 * t], in_=PS[pg][:], func=AF.Copy),
                     reads=[PSB[pg]], writes=[gbb])
                S.op("dve", lambda e, pv=pv, t=t, vbuf=vbuf: e.tensor_copy(out=vbuf[:, 512 * t:512 * t + 512], in_=PS[pv][:]),
                     reads=[PSB[pv]], writes=[vbb])
            S.op("dve", lambda e, j=j, tcv=tcv, gbuf=gbuf: e.tensor_scalar(out=tcv[:], in0=gbuf[:, 0:2048], scalar1=cwc[:, 0, j:j + 1], scalar2=cbc[:, j:j + 1],
                                                        op0=ALU.mult, op1=ALU.add), reads=[gbb, cbuf], writes=[tcb])
            S.op("dve", lambda e, j=j, tcv=tcv, gbuf=gbuf: e.scalar_tensor_tensor(out=tcv[:], in0=gbuf[:, 1:2049], scalar=cwc[:, 1, j:j + 1], in1=tcv[:],
                                                               op0=ALU.mult, op1=ALU.add), reads=[gbb, tcb, cbuf], writes=[tcb])
            S.op("dve", lambda e, j=j, tcv=tcv, gbuf=gbuf: e.scalar_tensor_tensor(out=tcv[:], in0=gbuf[:, 2:2050], scalar=cwc[:, 2, j:j + 1], in1=tcv[:],
                                                               op0=ALU.mult, op1=ALU.add), reads=[gbb, tcb, cbuf], writes=[tcb])
            S.op("act", lambda e, tg=tg, tcv=tcv: e.activation(out=tg[:], in_=tcv[:], func=AF.Gelu), reads=[tcb], writes=[tgb])
            S.op("dve", lambda e, i2=i2, vbuf=vbuf, tg=tg: e.tensor_tensor(out=ast[i2][:], in0=vbuf[:], in1=tg[:], op=ALU.mult),
                 reads=[vbb, tgb], writes=[astb[i2]])
            S.dma("sp", ACTd[j], ast[i2][:], reads=[astb[i2]], writes=[], sem_buf=astb[i2])
        S.barrier()
        acth = self.sb("acth", [128, 44, 1024], BF16, 12 * KB)
        achb = S.buf("acth")
        achq = S.bufs(4, "acthq")
        wd = [self.sb("wd%d" % i, [128, 44, 128], BF16, (100 + 11 * i) * KB) for i in range(2)]
        wdb = S.bufs(2, "wd")
        xl9 = [self.sb("xl9_%d" % i, [128, 512], F32, (122 + 2 * i) * KB) for i in range(2)]
        xl9b = S.bufs(2, "xl9")
        x2s = [self.sb("x2s%d" % i, [128, 512], F32, (126 + 2 * i) * KB) for i in range(2)]
        x2sb = S.bufs(2, "x2s")
        sq9 = [self.sb("sq9_%d" % i, [128, 512], F32, (130 + 2 * i) * KB) for i in range(2)]
        sq9b = S.bufs(2, "sq9")
        ssa2 = self.sb("ssa2", [128, 2048], F32, 134 * KB)
        ssa2b = S.buf("ssa2")
        rstd2 = self.sb("rstd2", [128, 2048], F32, 142 * KB)
        rst2b = S.buf("rstd2")
        S.op("dve", lambda e: e.memset(ssa2[:], 0.0), writes=[ssa2b])
        it = 0
        wi = 0
        for hf in range(2):
            for qq in range(4):
                S.dma("sp", acth[:, 11 * qq:11 * qq + 11, :], ACTd.rearrange("j p c -> p j c")[:, 11 * qq:11 * qq + 11, 1024 * hf:1024 * hf + 1024],
                      writes=[achq[qq]], sem_buf=achq[qq])
            for c in range(16):
                i2 = wi % 2
                wi += 1
                loadw("pool", wd[i2], wdb[i2], w_down[:, 128 * c:128 * c + 128], 44)
                for tt in range(2):
                    o0 = 1024 * hf + 512 * tt
                    ti = it % 2
                    it += 1
                    S.dma("pool", xl9[ti][:], X1T[c][:, OWN0 + o0:OWN0 + o0 + 512], writes=[xl9b[ti]], sem_buf=xl9b[ti])
                    pb = nb()
                    for kt in range(44):
                        S.op("pe", lambda e, pb=pb, i2=i2, kt=kt, tt=tt: e.matmul(
                            PS[pb][:], lhsT=wd[i2][:, kt, :], rhs=acth[:, kt, 512 * tt:512 * tt + 512], start=(kt == 0), stop=(kt == 43)),
                            reads=[wdb[i2], achq[kt // 11]], writes=[PSB[pb]], sig=(kt == 43))
                    S.op("dve", lambda e, pb=pb, ti=ti: e.tensor_tensor(out=x2s[ti][:], in0=PS[pb][:], in1=xl9[ti][:], op=ALU.add),
                         reads=[PSB[pb], xl9b[ti]], writes=[x2sb[ti]])
                    S.dma("sp", X2T[c][:, o0:o0 + 512], x2s[ti][:], reads=[x2sb[ti]], writes=[], sem_buf=x2sb[ti])
                    S.op("act", lambda e, ti=ti: e.activation(out=sq9[ti][:], in_=x2s[ti][:], func=AF.Square),
                         reads=[x2sb[ti]], writes=[sq9b[ti]])
                    S.op("dve", lambda e, o0=o0, ti=ti: e.tensor_tensor(
                        out=ssa2[:, o0:o0 + 512], in0=ssa2[:, o0:o0 + 512], in1=sq9[ti][:], op=ALU.add),
                        reads=[sq9b[ti], ssa2b], writes=[ssa2b])
        rstd_from(ssa2, ssa2b, rstd2, rst2b, [(0, 512), (512, 512), (1024, 512), (1536, 512)])
        S.barrier()
        x2l = [self.sb("x2l%d" % i, [128, 16, 128], F32, (12 + 8 * i) * KB) for i in range(2)]
        x2lb = S.bufs(2, "x2l")
        otmps = [self.sb("otmp%d" % i, [128, 16, 128], F32, (28 + 32 * i) * KB) for i in range(2)]
        otbs = S.bufs(2, "otmp")
        oTs = [self.sb("oT%d" % i, [128, 16, 128], F32, (36 + 32 * i) * KB) for i in range(2)]
        oTbs = S.bufs(2, "oT")
        orow = [self.sb("orow%d" % i, [128, D], F32, (44 + 8 * i) * KB) for i in range(2)]
        orb = S.bufs(2, "orow")
        X2v = X2T.rearrange("k p c -> p k c")
        rcol = [self.sb("rcol%d" % i, [128, 1], F32, 76 * KB + 64 * i) for i in range(2)]
        rcolb = S.bufs(2, "rcol")
        S.dma("sp", x2l[0][:], X2v[:, :, 0:128], writes=[x2lb[0]], sem_buf=x2lb[0])
        for tblk in range(16):
            i2 = tblk % 2
            c0 = 128 * tblk
            otmp, otb, oT, oTb = otmps[i2], otbs[i2], oTs[i2], oTbs[i2]
            if tblk + 1 < 16:
                S.dma("sp", x2l[1 - i2][:], X2v[:, :, c0 + 128:c0 + 256], writes=[x2lb[1 - i2]], sem_buf=x2lb[1 - i2])
            pr = nb()
            S.op("pe", lambda e, pr=pr, c0=c0: e.matmul(PS[pr][:, 0:1], lhsT=rstd2[0:1, c0:c0 + 128], rhs=onesf[0:1, 0:1],
                                                         start=True, stop=True), reads=[rst2b, cbuf], writes=[PSB[pr]], sig=True)
            S.op("act", lambda e, pr=pr, i2=i2: e.activation(out=rcol[i2][:], in_=PS[pr][:, 0:1], func=AF.Copy),
                 reads=[PSB[pr]], writes=[rcolb[i2]])
            S.op("dve", lambda e, i2=i2, oT=oT: e.tensor_tensor(
                out=oT[:], in0=x2l[i2][:], in1=g3c[:].unsqueeze(2).to_broadcast([128, 16, 128]), op=ALU.mult),
                reads=[x2lb[i2], cbuf], writes=[oTb])
            pbs = []
            for j in range(4):
                pb = nb()
                pbs.append(pb)
                for kk in range(4):
                    kt = 4 * j + kk
                    S.op("pe", lambda e, pb=pb, kk=kk, kt=kt, oT=oT: e.transpose(
                        out=PS[pb][:, 128 * kk:128 * kk + 128], in_=oT[:, kt, :], identity=ident[:]),
                        reads=[oTb, cbuf], writes=[PSB[pb]], sig=(kk == 3))
            for j in range(4):
                pb = pbs[j]
                if j % 2 == 0:
                    S.op("act", lambda e, pb=pb, j=j, i2=i2: e.activation(out=orow[i2][:, 512 * j:512 * j + 512], in_=PS[pb][:], func=AF.Copy,
                                                                         scale=rcol[i2][:, 0:1]),
                         reads=[PSB[pb], rcolb[i2]], writes=[orb[i2]])
                else:
                    S.op("dve", lambda e, pb=pb, j=j, i2=i2: e.tensor_scalar_mul(out=orow[i2][:, 512 * j:512 * j + 512], in0=PS[pb][:],
                                                                                scalar1=rcol[i2][:, 0:1]),
                         reads=[PSB[pb], rcolb[i2]], writes=[orb[i2]])
            S.dma("sp", out[c0:c0 + 128, :], orow[i2][:], reads=[orb[i2]], writes=[], sem_buf=orb[i2])
        finals.extend(orb)
        return S, finals

    def finish(self, S, final_bufs):
        toks = []
        for b in final_bufs:
            if b.dsem is not None:
                toks.append([b.dsem, b.dsem.cnt])
        S.emit(toks)
        return self.nc


def _abias_table():
    qi = np.arange(128)[:, None]
    si = np.arange(256)[None, :]
    dist = qi + 128 - si
    band = (dist >= 0) & (dist < 128)
    slopes = 2.0 ** (-8.0 * np.arange(1, 17, dtype=np.float32) / 16)
    t = -slopes[None, :, None] * dist[:, None, :].astype(np.float32)
    t = np.where(band[:, None, :], t, np.float32(-30000.0))
    return np.ascontiguousarray(t.astype(np.float32))


def make_in_maps(inputs):
    x = np.asarray(inputs["x"], dtype=np.float32)
    maps = []
    ident = np.eye(128, dtype=np.float32)
    ab = _abias_table()
    for core in range(NCORES):
        b, c = core // 4, core % 4
        t1 = 2048 * (c + 1)
        xw = np.zeros((WIN, D), np.float32)
        xw[WIN - t1:] = x[b, :t1]
        mask = np.zeros((WIN,), np.float32)
        mask[WIN - t1:] = 1.0
        m = {"xw": xw, "maskw": mask, "identity": ident, "abias": ab}
        for k, v in inputs.items():
            if k == "x":
                continue
            v = np.asarray(v, dtype=np.float32)
            m[k] = np.ascontiguousarray(v[0]) if k != "final_norm_g" else np.ascontiguousarray(v)
        maps.append(m)
    return maps


_NC_CACHE = {}


def _get_nc():
    if "nc" not in _NC_CACHE:
        kb = K()
        S, finals = kb.build()
        _NC_CACHE["nc"] = kb.finish(S, finals)
    return _NC_CACHE["nc"]


def kernel(**inputs):
    nc = _get_nc()
    in_maps = make_in_maps(inputs)
    res = run_bass_kernel_spmd(nc, in_maps, core_ids=list(range(NCORES)))
    outs = [np.asarray(r["out"], dtype=np.float32) for r in res.results]
    full = np.stack(outs, 0).reshape(2, 4 * 2048, D)
    return full
```

```python
import contextlib
import math
import numpy as np
import concourse.bass as bass
import concourse.mybir as mybir
from concourse.bass_utils import run_bass_kernel_spmd

F32 = mybir.dt.float32
BF16 = mybir.dt.bfloat16
AF = mybir.ActivationFunctionType
ALU = mybir.AluOpType

D = 2048
NCORES = 8
WIN = 8192
NBLK = 64
MB0 = 46
NMAIN = 2304
OWN0 = 256
NCH0 = MB0 * 8
NCHM = 144
KVALS = list(range(17)) + [64]
NK = len(KVALS)
DFF = 5632
KB = 1024


class Sem:
    def __init__(self, h):
        self.h = h
        self.cnt = 0


class Buf:
    __slots__ = ("name", "w", "r", "dsem")

    def __init__(self, name):
        self.name = name
        self.w = None
        self.r = {}
        self.dsem = None


class Sched:
    ENG = ("pe", "act", "dve", "pool", "sp")

    def __init__(self, nc, stack):
        self.nc = nc
        self.stack = stack
        self.ops = {e: [] for e in self.ENG}
        self.esem = {e: Sem(stack.enter_context(nc.semaphore("es_" + e))) for e in self.ENG}
        self.future = {e: [self.esem[e], None] for e in self.ENG}
        self.last = {e: None for e in self.ENG}
        self.extra = {e: [] for e in self.ENG}
        self.dsems = []
        self.nbuf = 0

    def buf(self, name=None):
        self.nbuf += 1
        return Buf(name or "b%d" % self.nbuf)

    def bufs(self, n, name="b"):
        return [self.buf("%s%d" % (name, i)) for i in range(n)]

    def _dsem(self, b):
        if b.dsem is None:
            b.dsem = Sem(self.stack.enter_context(self.nc.semaphore("ds_%d" % len(self.dsems))))
            self.dsems.append(b.dsem)
        return b.dsem

    def _collect(self, eng, reads, writes):
        waits = list(self.extra[eng])
        self.extra[eng] = []
        for b in reads:
            if b.w is not None:
                waits.append(b.w)
        for b in writes:
            if b.w is not None:
                waits.append(b.w)
            waits.extend(b.r.values())
        out = []
        for t in waits:
            if eng in ("pe", "act", "pool") and t[0] is self.esem[eng]:
                continue
            if t[0] in self.dsems:
                t = [t[0], t[0].cnt]
            out.append(t)
        return out

    def op(self, eng, fn, reads=(), writes=(), sig=True):
        waits = self._collect(eng, reads, writes)
        tok = self.future[eng]
        inc = None
        if sig:
            s = self.esem[eng]
            s.cnt += 1
            tok[1] = s.cnt
            self.future[eng] = [s, None]
            inc = (s, 1)
            self.last[eng] = tok
        self.ops[eng].append((waits, fn, inc))
        for b in writes:
            b.w = tok
            b.r = {}
        for b in reads:
            b.r[id(tok[0])] = tok
        return tok

    def dma(self, eng, out_ap, in_ap, reads=(), writes=(), sem_buf=None, nonc=False):
        waits = self._collect(eng, reads, writes)
        s = self._dsem(sem_buf)
        s.cnt += 16
        tok = [s, s.cnt]
        nc = self.nc

        def fn(e, out_ap=out_ap, in_ap=in_ap):
            if nonc:
                with nc.allow_non_contiguous_dma(reason="small param gather"):
                    return e.dma_start(out=out_ap, in_=in_ap)
            return e.dma_start(out=out_ap, in_=in_ap)

        self.ops[eng].append((waits, fn, (s, 16)))
        for b in writes:
            b.w = tok
            b.r = {}
        for b in reads:
            b.r[id(s)] = tok
        return tok

    def barrier(self):
        toks = []
        for e in self.ENG:
            if self.last[e] is not None:
                toks.append(self.last[e])
        for s in self.dsems:
            if s.cnt:
                toks.append([s, s.cnt])
        for e in self.ENG:
            self.extra[e].extend(toks)

    def emit(self, final_toks):
        nc = self.nc
        for e in self.ENG:
            assert self.future[e][1] is None
        with nc.Block() as block:
            def replay(eng, h):
                waited = {}
                for waits, fn, inc in self.ops[eng]:
                    for t in waits:
                        assert t[1] is not None, "unresolved token"
                        k = id(t[0])
                        if waited.get(k, 0) < t[1]:
                            h.wait_ge(t[0].h, t[1])
                            waited[k] = t[1]
                    ins = fn(h)
                    if inc is not None:
                        ins.then_inc(inc[0].h, inc[1])
                if eng == "sp":
                    for t in final_toks:
                        h.wait_ge(t[0].h, t[0].cnt)

            @block.tensor
            def _(h):
                replay("pe", h)

            @block.scalar
            def _(h):
                replay("act", h)

            @block.vector
            def _(h):
                replay("dve", h)

            @block.gpsimd
            def _(h):
                replay("pool", h)

            @block.sync
            def _(h):
                replay("sp", h)


class K:
    def __init__(self, debug=()):
        self.debug = set(debug)
        self.stack = contextlib.ExitStack()
        self.nc = bass.Bass("TRN2", target_bir_lowering=False)
        self.S = None
        self.stop = None

    def din(self, name, shape, dt=F32):
        return self.nc.dram_tensor(name, list(shape), dt, kind="ExternalInput").ap()

    def dscr(self, name, shape, dt, out=False):
        kind = "ExternalOutput" if (out or name in self.debug) else "Internal"
        return self.nc.dram_tensor(name, list(shape), dt, kind=kind).ap()

    def sb(self, name, shape, dt, off):
        return self.nc.alloc_sbuf_tensor_at(name, list(shape), dt, offset=off + 16 * KB)

    def build(self):
        nc = self.nc
        st = self.stack
        S = self.S = Sched(nc, st)
        dbg = self.debug
        xw = self.din("xw", [WIN, D])
        maskw = self.din("maskw", [WIN])
        g1 = self.din("attn_norm_g", [D])
        w_in = self.din("w_in", [D, 5888])
        b_in = self.din("b_in", [5888])
        sinks = self.din("attn_sinks", [16])
        a_re = self.din("ssm_a_re", [32, 64])
        a_im = self.din("ssm_a_im", [32, 64])
        log_dt = self.din("ssm_log_dt", [32])
        b_re = self.din("ssm_b_re", [32, 64, 16])
        b_im = self.din("ssm_b_im", [32, 64, 16])
        c_re = self.din("ssm_c_re", [32, 16, 64])
        c_im = self.din("ssm_c_im", [32, 16, 64])
        ssm_d = self.din("ssm_d", [512])
        w_glu = self.din("w_glu", [512, 1024])
        b_glu = self.din("b_glu", [1024])
        w_ba = self.din("w_branch_attn", [1024, D])
        w_bs = self.din("w_branch_ssm", [512, D])
        w_out = self.din("w_out", [D, D])
        g2 = self.din("ffn_norm_g", [D])
        w_up = self.din("w_up", [D, 2 * DFF])
        conv_w = self.din("conv_w", [3, DFF])
        conv_b = self.din("conv_b", [DFF])
        w_down = self.din("w_down", [DFF, D])
        g3 = self.din("final_norm_g", [D])
        abias = self.din("abias", [128, 16, 256])
        out = self.nc.dram_tensor("out", [2048, D], F32, kind="ExternalOutput").ap()
        FFd = self.dscr("FFd", [128, 16 * 2 * 16 * 32], BF16)
        TZd = self.dscr("TZd", [128, 16 * 4 * 128], BF16)
        XT = self.dscr("XT", [16, 128, NMAIN], F32)
        HT = self.dscr("HT", [16, 128, NMAIN], BF16)
        YG = self.dscr("YG", [4, 128, NMAIN], BF16) if "YG" in dbg else None
        UM = self.dscr("UM", [4, 128, NMAIN], BF16) if "UM" in dbg else None
        XSd = self.dscr("XSd", [128, 2 * 16 * NCHM], F32) if "XSd" in dbg else None

        o = 0

        def cal(name, shape, dt):
            nonlocal o
            nbytes = int(np.prod(shape[1:])) * (4 if dt == F32 else 2)
            t = self.sb(name, shape, dt, o)
            o += (nbytes + 31) // 32 * 32
            return t

        ident = cal("ident", [128, 128], F32)
        identb = cal("identb", [128, 128], BF16)
        onesb = cal("onesb", [128, 128], BF16)
        onesf = cal("onesf", [128, 128], F32)
        g1c = cal("g1c", [128, 16], F32)
        g2c = cal("g2c", [128, 16], F32)
        g3c = cal("g3c", [128, 16], F32)
        binc = cal("binc", [128, 46], F32)
        bgluc = cal("bgluc", [128, 8], F32)
        cwc = cal("cwc", [128, 3, 44], F32)
        cbc = cal("cbc", [128, 44], F32)
        dcol = cal("dcol", [128, 4], F32)
        AAt = cal("AAt", [128, 2, 16], F32)
        BBt = cal("BBt", [128, 2, 16], F32)
        X4 = [cal("X4a", [128, 3, 16], F32), cal("X4b", [128, 3, 16], F32)]
        P1 = cal("P1", [128, 2, 16], F32)
        P2 = cal("P2", [128, 2, 16], F32)
        A4re = cal("A4re", [128, 16], F32)
        A4im = cal("A4im", [128, 16], F32)
        A16re = cal("A16re", [128, 16], F32)
        A16im = cal("A16im", [128, 16], F32)
        AA64 = cal("AA64", [128, 2, 16], F32)
        BB64 = cal("BB64", [128, 2, 16], F32)
        assert o <= 8 * KB, o
        cbuf = S.buf("consts")
        identity_src = self.din("identity", [128, 128])
        S.dma("sp", ident[:], identity_src, writes=[cbuf], sem_buf=cbuf)
        cbuf2 = S.buf("consts2")
        S.dma("pool", identb[:], identity_src, writes=[cbuf2], sem_buf=cbuf2)
        S.op("dve", lambda e: e.memset(onesb[:], 1.0), writes=[cbuf])
        S.op("dve", lambda e: e.memset(onesf[:], 1.0), writes=[cbuf])
        for t, src in ((g1c, g1), (g2c, g2), (g3c, g3)):
            S.dma("sp", t[:], src.rearrange("(k p) -> p k", p=128), writes=[cbuf], sem_buf=cbuf, nonc=True)
        S.dma("sp", binc[:], b_in.rearrange("(k p) -> p k", p=128), writes=[cbuf], sem_buf=cbuf, nonc=True)
        S.dma("sp", bgluc[:], b_glu.rearrange("(k p) -> p k", p=128), writes=[cbuf], sem_buf=cbuf, nonc=True)
        S.dma("sp", cwc[:], conv_w.rearrange("w (k p) -> p w k", p=128), writes=[cbuf], sem_buf=cbuf, nonc=True)
        S.dma("sp", cbc[:], conv_b.rearrange("(k p) -> p k", p=128), writes=[cbuf], sem_buf=cbuf, nonc=True)
        S.dma("sp", dcol[:], ssm_d.rearrange("(k p) -> p k", p=128), writes=[cbuf], sem_buf=cbuf, nonc=True)

        PS = [st.enter_context(nc.psum_tensor("ps%d" % i, [128, 512], F32)) for i in range(8)]
        PSB = S.bufs(8, "psb")

        base = 12 * KB
        SW = self.sb("SW", [128, 16, 2, 4, 128], BF16, 30 * KB)
        swb = S.buf("SW")
        po = 62 * KB

        def pal(name, shape, dt):
            nonlocal po
            nbytes = int(np.prod(shape[1:])) * (4 if dt == F32 else 2)
            t = self.sb(name, shape, dt, po)
            po += (nbytes + 31) // 32 * 32
            return t

        CAre = pal("CAre", [128, 17, 16, 32], F32)
        CAim = pal("CAim", [128, 17, 16, 32], F32)
        SWT = pal("SWT", [128, 16, 16, 32], F32)
        TMPB = pal("TMPB", [128, 17, 16, 32], F32)
        assert po <= 200 * KB, po
        po = base
        are = pal("are", [128, 16], F32)
        aim = pal("aim", [128, 16], F32)
        ldt = pal("ldt", [128, 16], F32)
        dtv = pal("dtv", [128, 16], F32)
        adr = pal("adr", [128, 16], F32)
        ang = pal("ang", [128, 16], F32)
        KM = pal("KM", [128, NK, 16], F32)
        PWm = pal("PWm", [128, NK, 16], F32)
        ANG = pal("ANG", [128, NK, 16], F32)
        T17 = pal("T17", [128, NK, 16], F32)
        XS17 = self.sb("XS17", [128, NK, 16], F32, 174 * KB)
        NI = self.sb("NI", [128, NK, 16], mybir.dt.int32, 176 * KB)
        PWre = pal("PWre", [128, NK, 16], F32)
        PWim = pal("PWim", [128, NK, 16], F32)
        t16 = [pal("t16_%d" % i, [128, 16], F32) for i in range(6)]
        zre = pal("zre", [128, 16], F32)
        zim = pal("zim", [128, 16], F32)
        BEre = self.sb("BEre", [128, 16, 32], F32, 170 * KB)
        BEim = self.sb("BEim", [128, 16, 32], F32, 172 * KB)
        Bbre = pal("Bbre", [128, 16, 32], F32)
        Bbim = pal("Bbim", [128, 16, 32], F32)
        CEre = pal("CEre", [128, 16, 32], F32)
        CEim = pal("CEim", [128, 16, 32], F32)
        assert po <= 30 * KB, po
        CN4 = [self.sb("CN4re", [128, 4, 128], F32, 196 * KB), self.sb("CN4im", [128, 4, 128], F32, 198 * KB)]
        BZre = self.sb("BZre", [128, 16, 128], F32, 130 * KB)
        BZim = self.sb("BZim", [128, 16, 128], F32, 138 * KB)
        FFs = self.sb("FFs", [128, 16, 16 * 32], BF16, 130 * KB)
        p0 = S.buf("p0")

        PI = math.pi
        pin = []

        def pin_new():
            pin.append(S.buf("pin%d" % len(pin)))
            return pin[-1]

        for gl in range(2):
            sl = slice(64 * gl, 64 * gl + 64)
            S.dma("sp", are[sl, :], a_re.rearrange("(gp gl) p -> gl p gp", gl=2)[gl], writes=[pin_new()], sem_buf=pin[-1], nonc=True)
            S.dma("sp", aim[sl, :], a_im.rearrange("(gp gl) p -> gl p gp", gl=2)[gl], writes=[pin_new()], sem_buf=pin[-1], nonc=True)
            S.dma("sp", ldt[sl, :], log_dt.rearrange("(gp gl) -> gl gp", gl=2)[gl].partition_broadcast(64),
                  writes=[pin_new()], sem_buf=pin[-1], nonc=True)
        S.op("dve", lambda e: e.memset(BEre[:], 0.0), writes=[p0])
        S.op("dve", lambda e: e.memset(BEim[:], 0.0), writes=[p0])
        S.op("dve", lambda e: e.memset(CN4[0][:], 0.0), writes=[p0])
        S.op("dve", lambda e: e.memset(CN4[1][:], 0.0), writes=[p0])
        for gl in range(2):
            sl = slice(64 * gl, 64 * gl + 64)
            for t, src in ((BEre, b_re), (BEim, b_im)):
                S.dma("sp", t[sl, :, 16 * gl:16 * gl + 16], src.rearrange("(gp gl) p h -> gl p gp h", gl=2)[gl],
                      reads=[p0], writes=[pin_new()], sem_buf=pin[-1], nonc=True)
        for q in range(4):
            for gl in range(2):
                for t, src in ((CN4[0], c_re), (CN4[1], c_im)):
                    S.dma("sp", t[32 * q + 16 * gl:32 * q + 16 * gl + 16, :, 64 * gl:64 * gl + 64],
                          src.rearrange("(k q gl) h p -> q gl h k p", q=4, gl=2)[q, gl],
                          reads=[p0], writes=[pin_new()], sem_buf=pin[-1], nonc=True)
        for ki, kv in enumerate(KVALS):
            S.op("dve", lambda e, ki=ki, kv=kv: e.memset(KM[:, ki, :], float(kv)), writes=[p0])
        S.op("act", lambda e: e.activation(out=dtv[:], in_=ldt[:], func=AF.Exp), reads=[p0] + pin, writes=[p0] + pin)
        S.op("dve", lambda e: e.tensor_tensor(out=adr[:], in0=are[:], in1=dtv[:], op=ALU.mult), reads=[p0], writes=[p0])
        S.op("dve", lambda e: e.tensor_tensor(out=ang[:], in0=aim[:], in1=dtv[:], op=ALU.mult), reads=[p0], writes=[p0])

        def bc17(t):
            return t[:].unsqueeze(1).to_broadcast([128, NK, 16])

        S.op("dve", lambda e: e.tensor_tensor(out=T17[:], in0=KM[:], in1=bc17(adr), op=ALU.mult), reads=[p0], writes=[p0])
        S.op("act", lambda e: e.activation(out=PWm[:], in_=T17[:], func=AF.Exp), reads=[p0], writes=[p0])
        S.op("dve", lambda e: e.tensor_tensor(out=ANG[:], in0=KM[:], in1=bc17(ang), op=ALU.mult), reads=[p0], writes=[p0])
        def sincos(dst, shift):
            S.op("dve", lambda e: e.tensor_scalar(out=T17[:], in0=ANG[:], scalar1=1.0 / (2 * PI), scalar2=0.5 + shift / (2 * PI),
                                                   op0=ALU.mult, op1=ALU.add), reads=[p0], writes=[p0])
            S.op("dve", lambda e: e.tensor_copy(out=NI[:], in_=T17[:]), reads=[p0], writes=[p0])
            S.op("dve", lambda e: e.tensor_copy(out=T17[:], in_=NI[:]), reads=[p0], writes=[p0])
            S.op("dve", lambda e: e.tensor_scalar_add(out=XS17[:], in0=ANG[:], scalar1=shift), reads=[p0], writes=[p0])
            S.op("dve", lambda e: e.scalar_tensor_tensor(out=T17[:], in0=T17[:], scalar=-2 * PI, in1=XS17[:],
                                                          op0=ALU.mult, op1=ALU.add), reads=[p0], writes=[p0])
            S.op("dve", lambda e: e.tensor_scalar(out=XS17[:], in0=T17[:], scalar1=-PI, scalar2=2 * PI,
                                                   op0=ALU.is_lt, op1=ALU.mult), reads=[p0], writes=[p0])
            S.op("dve", lambda e: e.tensor_tensor(out=T17[:], in0=T17[:], in1=XS17[:], op=ALU.add), reads=[p0], writes=[p0])
            S.op("dve", lambda e: e.tensor_scalar(out=T17[:], in0=T17[:], scalar1=-PI, scalar2=PI,
                                                   op0=ALU.max, op1=ALU.min), reads=[p0], writes=[p0])
            S.op("act", lambda e: e.activation(out=dst[:], in_=T17[:], func=AF.Sin), reads=[p0], writes=[p0])

        sincos(PWim, 0.0)
        sincos(PWre, 0.5 * PI)
        S.op("dve", lambda e: e.tensor_tensor(out=PWre[:], in0=PWre[:], in1=PWm[:], op=ALU.mult), reads=[p0], writes=[p0])
        S.op("dve", lambda e: e.tensor_tensor(out=PWim[:], in0=PWim[:], in1=PWm[:], op=ALU.mult), reads=[p0], writes=[p0])
        nr, ni, den, ta, tb, tc = [t[:] for t in t16]

        def dv(fn):
            S.op("dve", fn, reads=[p0], writes=[p0])

        dv(lambda e: e.tensor_scalar_add(out=nr, in0=PWre[:, 1, :], scalar1=-1.0))
        dv(lambda e: e.tensor_copy(out=ni, in_=PWim[:, 1, :]))
        dv(lambda e: e.tensor_tensor(out=den, in0=are[:], in1=are[:], op=ALU.mult))
        dv(lambda e: e.tensor_tensor(out=ta, in0=aim[:], in1=aim[:], op=ALU.mult))
        dv(lambda e: e.tensor_tensor(out=den, in0=den, in1=ta, op=ALU.add))
        dv(lambda e: e.reciprocal(out=den, in_=den))
        dv(lambda e: e.tensor_tensor(out=ta, in0=nr, in1=are[:], op=ALU.mult))
        dv(lambda e: e.tensor_tensor(out=tb, in0=ni, in1=aim[:], op=ALU.mult))
        dv(lambda e: e.tensor_tensor(out=ta, in0=ta, in1=tb, op=ALU.add))
        dv(lambda e: e.tensor_tensor(out=zre[:], in0=ta, in1=den, op=ALU.mult))
        dv(lambda e: e.tensor_tensor(out=ta, in0=ni, in1=are[:], op=ALU.mult))
        dv(lambda e: e.tensor_tensor(out=tb, in0=nr, in1=aim[:], op=ALU.mult))
        dv(lambda e: e.tensor_tensor(out=ta, in0=ta, in1=tb, op=ALU.subtract))
        dv(lambda e: e.tensor_tensor(out=zim[:], in0=ta, in1=den, op=ALU.mult))
        dv(lambda e: e.tensor_copy(out=AAt[:, 0, :], in_=PWre[:, 16, :]))
        dv(lambda e: e.tensor_copy(out=AAt[:, 1, :], in_=PWre[:, 16, :]))
        dv(lambda e: e.tensor_copy(out=BBt[:, 1, :], in_=PWim[:, 16, :]))
        dv(lambda e: e.tensor_scalar_mul(out=BBt[:, 0, :], in0=PWim[:, 16, :], scalar1=-1.0))
        dv(lambda e: e.tensor_copy(out=A4re[:], in_=PWre[:, 4, :]))
        dv(lambda e: e.tensor_copy(out=A4im[:], in_=PWim[:, 4, :]))
        dv(lambda e: e.tensor_copy(out=A16re[:], in_=PWre[:, 16, :]))
        dv(lambda e: e.tensor_copy(out=A16im[:], in_=PWim[:, 16, :]))
        dv(lambda e: e.tensor_copy(out=AA64[:, 0, :], in_=PWre[:, 17, :]))
        dv(lambda e: e.tensor_copy(out=AA64[:, 1, :], in_=PWre[:, 17, :]))
        dv(lambda e: e.tensor_copy(out=BB64[:, 1, :], in_=PWim[:, 17, :]))
        dv(lambda e: e.tensor_scalar_mul(out=BB64[:, 0, :], in0=PWim[:, 17, :], scalar1=-1.0))

        def bc32(t):
            return t[:].unsqueeze(2).to_broadcast([128, 16, 32])

        T32a = TMPB[:, 0, :, :]
        T32b = TMPB[:, 1, :, :]
        dv(lambda e: e.tensor_tensor(out=T32a, in0=BEre[:], in1=bc32(zre), op=ALU.mult))
        dv(lambda e: e.tensor_tensor(out=T32b, in0=BEim[:], in1=bc32(zim), op=ALU.mult))
        dv(lambda e: e.tensor_tensor(out=Bbre[:], in0=T32a, in1=T32b, op=ALU.subtract))
        dv(lambda e: e.tensor_tensor(out=T32a, in0=BEim[:], in1=bc32(zre), op=ALU.mult))
        dv(lambda e: e.tensor_tensor(out=T32b, in0=BEre[:], in1=bc32(zim), op=ALU.mult))
        dv(lambda e: e.tensor_tensor(out=Bbim[:], in0=T32a, in1=T32b, op=ALU.add))
        for i, (cn, ce) in enumerate(((CN4[0], CEre), (CN4[1], CEim))):
            for k in range(4):
                S.op("pe", lambda e, cn=cn, k=k: e.transpose(out=PS[0][:, 128 * k:128 * k + 128], in_=cn[:, k, :], identity=ident[:]),
                     reads=[p0, cbuf], writes=[PSB[0]], sig=(k == 3))
            S.op("act", lambda e, ce=ce: e.activation(out=ce[:].rearrange("p g c -> p (g c)"), in_=PS[0][:], func=AF.Copy),
                 reads=[PSB[0]], writes=[p0])

        def bck(t):
            return t[:].unsqueeze(1).to_broadcast([128, 17, 16, 32])

        def bcc(t):
            return t[:, 0:17, :].unsqueeze(3).to_broadcast([128, 17, 16, 32])

        dv(lambda e: e.tensor_tensor(out=CAre[:], in0=bck(CEre), in1=bcc(PWre), op=ALU.mult))
        dv(lambda e: e.tensor_tensor(out=TMPB[:], in0=bck(CEim), in1=bcc(PWim), op=ALU.mult))
        dv(lambda e: e.tensor_tensor(out=CAre[:], in0=CAre[:], in1=TMPB[:], op=ALU.subtract))
        dv(lambda e: e.tensor_tensor(out=CAim[:], in0=bck(CEre), in1=bcc(PWim), op=ALU.mult))
        dv(lambda e: e.tensor_tensor(out=TMPB[:], in0=bck(CEim), in1=bcc(PWre), op=ALU.mult))
        dv(lambda e: e.tensor_tensor(out=CAim[:], in0=CAim[:], in1=TMPB[:], op=ALU.add))
        FFv = FFd.rearrange("p (t r x) -> p t r x", t=16, r=2)
        ffb = p0
        for ri in range(2):
            src = (CAre if ri == 0 else CAim)[:, 1:17, :, :].rearrange("p t g c -> p t (g c)")
            S.op("act", lambda e, src=src, ri=ri: e.activation(out=FFs[:], in_=src, func=AF.Copy, scale=(1.0 if ri == 0 else -1.0)),
                 reads=[p0], writes=[ffb])
            S.dma("sp", FFv[:, :, ri, :], FFs[:], reads=[ffb], writes=[p0], sem_buf=ffb)
        def bcs(t):
            return t[:].unsqueeze(1).to_broadcast([128, 16, 16, 32])

        def bcp(t):
            return t[:, 0:16, :].unsqueeze(3).to_broadcast([128, 16, 16, 32])

        TM16 = TMPB[:, 0:16, :, :]
        for ri in range(2):
            if ri == 0:
                dv(lambda e: e.tensor_tensor(out=SWT[:], in0=bcs(Bbre), in1=bcp(PWre), op=ALU.mult))
                dv(lambda e: e.tensor_tensor(out=TM16, in0=bcs(Bbim), in1=bcp(PWim), op=ALU.mult))
                dv(lambda e: e.tensor_tensor(out=SWT[:], in0=SWT[:], in1=TM16, op=ALU.subtract))
            else:
                dv(lambda e: e.tensor_tensor(out=SWT[:], in0=bcs(Bbre), in1=bcp(PWim), op=ALU.mult))
                dv(lambda e: e.tensor_tensor(out=TM16, in0=bcs(Bbim), in1=bcp(PWre), op=ALU.mult))
                dv(lambda e: e.tensor_tensor(out=SWT[:], in0=SWT[:], in1=TM16, op=ALU.add))
            for kk in range(16):
                pb = kk % 2 + 1
                for k in range(4):
                    S.op("pe", lambda e, kk=kk, k=k, pb=pb: e.transpose(
                        out=PS[pb][:, 128 * k:128 * k + 128],
                        in_=SWT[:, kk, 4 * k:4 * k + 4, :].rearrange("p g c -> p (g c)"), identity=ident[:]),
                        reads=[p0, cbuf], writes=[PSB[pb]], sig=(k == 3))
                S.op("act", lambda e, kk=kk, ri=ri, pb=pb: e.activation(
                    out=SW[:, kk, ri, :, :].rearrange("p k c -> p (k c)"), in_=PS[pb][:], func=AF.Copy),
                    reads=[PSB[pb]], writes=[swb])
        dv(lambda e: e.memset(BZre[:], 0.0))
        dv(lambda e: e.memset(BZim[:], 0.0))
        for q in range(4):
            dv(lambda e, q=q: e.tensor_copy(
                out=BZre[:].rearrange("p (k q) c -> p k q c", q=4)[:, :, q, 32 * q:32 * q + 32],
                in_=Bbre[:].rearrange("p (k q) c -> p k q c", q=4)[:, :, q, :]))
            dv(lambda e, q=q: e.tensor_scalar_mul(
                out=BZim[:].rearrange("p (k q) c -> p k q c", q=4)[:, :, q, 32 * q:32 * q + 32],
                in0=Bbim[:].rearrange("p (k q) c -> p k q c", q=4)[:, :, q, :], scalar1=-1.0))
        TZs = self.sb("TZs", [128, 16, 512], BF16, 162 * KB)
        tzb = p0
        TZv = TZd.rearrange("p (l x) -> p l x", l=16)
        for lag in range(16):
            pb = 3 + lag % 2
            for k in range(4):
                for q in range(4):
                    gp = 4 * k + q
                    osl = PS[pb][:, 128 * k + 32 * q:128 * k + 32 * q + 32]
                    S.op("pe", lambda e, osl=osl, gp=gp, lag=lag: e.matmul(
                        osl, lhsT=BZre[:, gp, :], rhs=CAre[:, lag, gp, :], start=True, stop=False, skip_group_check=True),
                        reads=[p0], writes=[PSB[pb]], sig=False)
                    S.op("pe", lambda e, osl=osl, gp=gp, lag=lag: e.matmul(
                        osl, lhsT=BZim[:, gp, :], rhs=CAim[:, lag, gp, :], start=False, stop=True, skip_group_check=True),
                        reads=[p0], writes=[PSB[pb]], sig=(k == 3 and q == 3))
            S.op("act", lambda e, lag=lag, pb=pb: e.activation(out=TZs[:, lag, :], in_=PS[pb][:], func=AF.Copy),
                 reads=[PSB[pb], p0], writes=[tzb])
        S.dma("sp", TZv, TZs[:], reads=[tzb], writes=[p0], sem_buf=tzb)
        S.barrier()

        Wu = self.sb("Wu", [128, 16, 512], BF16, 62 * KB)
        wub = S.buf("Wu")
        xt = [self.sb("xt%d" % i, [128, D], F32, (78 + 8 * i) * KB) for i in range(2)]
        xtb = S.bufs(2, "xt")
        xTs2 = [self.sb("xTs0", [128, 16, 128], F32, 94 * KB)] * 2
        xTb2 = [S.buf("xTs")] * 2
        sq2 = [self.sb("sq0", [128, 16, 128], BF16, 102 * KB), self.sb("sq1", [128, 16, 128], BF16, 196 * KB)]
        sqb2 = S.bufs(2, "sq")
        hTs = [self.sb("hTs%d" % i, [128, 16, 512], BF16, (106 + 16 * i) * KB) for i in range(2)]
        hTb = S.bufs(2, "hTs")
        ub = self.sb("ubatch", [128, 4, 1024], BF16, 138 * KB)
        ubb = S.buf("ubatch")
        Ssb2 = [self.sb("Ssb%d" % i, [128, 64, 2, 16], BF16, (146 + 4 * i) * KB) for i in range(2)]
        Ssbb2 = S.bufs(2, "Ssb")
        s4q = [self.sb("s4q%d" % i, [128, 2, 4, 256], BF16, (20 + 4 * i) * KB) for i in range(2)] + \
              [self.sb("s4q%d" % (2 + i), [128, 2, 4, 256], BF16, (12 + 4 * i) * KB) for i in range(2)]
        s4qb = S.bufs(4, "s4q")
        Yh = self.sb("Yh", [128, 2, 4, 64], F32, 28 * KB)
        HT4 = [self.sb("HT4_%d" % i, [128, 256], F32, (200 + i) * KB) for i in range(4)]
        Zhs = [self.sb("Zh0", [128, 2, 16, 16], F32, 204 * KB), self.sb("Zh1", [128, 2, 16, 16], F32, 8 * KB)]
        zhb = S.bufs(2, "Zh")
        hb = S.buf("horner")

        def cma(eng, o_re, o_im, x_re, x_im, a_re, a_im, s_re, s_im, T, rb, wb):
            t1, t2, t3, t4 = T
            f = lambda fn, r=rb, w=wb: S.op(eng, fn, reads=list(r), writes=list(w))
            f(lambda e: e.tensor_tensor(out=t1, in0=a_re, in1=x_re, op=ALU.mult))
            f(lambda e: e.tensor_tensor(out=t2, in0=a_im, in1=x_im, op=ALU.mult))
            f(lambda e: e.tensor_tensor(out=t3, in0=a_re, in1=x_im, op=ALU.mult))
            f(lambda e: e.tensor_tensor(out=t4, in0=a_im, in1=x_re, op=ALU.mult))
            f(lambda e: e.tensor_tensor(out=t1, in0=t1, in1=t2, op=ALU.subtract))
            f(lambda e: e.tensor_tensor(out=t3, in0=t3, in1=t4, op=ALU.add))
            f(lambda e: e.tensor_tensor(out=o_re, in0=t1, in1=s_re, op=ALU.add))
            f(lambda e: e.tensor_tensor(out=o_im, in0=t3, in1=s_im, op=ALU.add))
        um = self.sb("u_main", [128, 4, NMAIN], BF16, 154 * KB)
        umb = S.buf("u_main")
        Xs = self.sb("Xs", [128, 2, 16, NCHM], BF16, 172 * KB)
        Xsb = S.buf("Xs")
        rsA = [self.sb("rsA%d" % i, [128, 128], F32, 181 * KB + 1024 * i) for i in range(2)]
        rsB = [self.sb("rsB%d" % i, [128, 128], F32, 181 * KB + 1024 * i + 512) for i in range(2)]
        rsb2 = S.bufs(2, "rs")
        xtmp = self.sb("xtmp", [128, 16, 128], F32, 184 * KB)
        xtmpb = S.buf("xtmp")
        mk = [self.sb("mk%d" % i, [128, 512], F32, (192 + 2 * i) * KB) for i in range(2)]
        mkb = S.bufs(2, "mk")
        S.dma("pool", Wu[:], w_in[:, 1280:1792].rearrange("(kt p) c -> p kt c", p=128), writes=[wub], sem_buf=wub)
        stb = S.buf("state")
        S.op("pool", lambda e: e.memset(X4[0][:], 0.0), writes=[stb])
        S.op("pool", lambda e: e.memset(X4[1][:], 0.0), writes=[stb])
        cur = 0
        XTv = XT.rearrange("k p c -> p k c")
        HTv = HT.rearrange("k p c -> p k c")
        deferred = {}

        def ph1_load(b_):
            S.dma("sp", xt[b_ % 2][:], xw[b_ * 128:(b_ + 1) * 128, :], writes=[xtb[b_ % 2]], sem_buf=xtb[b_ % 2])

        junk = [self.sb("junk0", [128, D], BF16, 102 * KB)] * 2
        junkb = [S.buf("junk")] * 2
        HT4b = [self.sb("HT4b_%d" % i, [128, 256], F32, (196 + i) * KB) for i in range(4)]
        Yhb = self.sb("Yhb", [128, 2, 4, 64], F32, 10 * KB)
        hb2 = S.buf("horner2")
        xsb = [self.sb("xsb%d" % i, [128, D], BF16, (184 + 4 * i) * KB) for i in range(2)]
        xsbb = S.bufs(2, "xsb")
        ssqs = [self.sb("ssq%d" % i, [128, 1], F32, 183 * KB + 64 * i) for i in range(2)]
        ssqb = S.bufs(2, "ssq")

        def ph1_sq(b_):
            ii = b_ % 2
            S.op("act", lambda e, ii=ii: e.activation(out=junk[ii][:], in_=xt[ii][:], func=AF.Square, accum_out=ssqs[ii][:]),
                 reads=[xtb[ii]], writes=[junkb[ii], ssqb[ii]])

        def ph1_transposes(b_):
            ii = b_ % 2
            for j in range(4):
                for kk in range(4):
                    kt = 4 * j + kk
                    S.op("pe", lambda e, j=j, kk=kk, kt=kt, ii=ii: e.transpose(
                        out=PS[j][:, 128 * kk:128 * kk + 128], in_=xt[ii][:, 128 * kt:128 * kt + 128], identity=ident[:]),
                        reads=[xtb[ii], cbuf], writes=[PSB[j]], sig=(kk == 3))

        for blk in range(NBLK):
            i2 = blk % 2
            sub = (blk // 4) % 2
            tcol = (blk % 4) * 128
            xTs, xTb = xTs2[i2], xTb2[i2]
            rs1, rs2, rsb = rsA[i2], rsB[i2], rsb2[i2]
            if blk == 0:
                ph1_load(0)
                ph1_sq(0)
            if blk % 4 == 0:
                mi = (blk // 4) % 2
                S.dma("sp", mk[mi][:], maskw[blk * 128:blk * 128 + 512].partition_broadcast(128), writes=[mkb[mi]],
                      sem_buf=mkb[mi], nonc=True)
            if blk + 1 < NBLK:
                ph1_load(blk + 1)
            S.op("dve", lambda e, rs1=rs1, i2=i2: e.tensor_scalar(out=rs1[:, 0:1], in0=ssqs[i2][:], scalar1=1.0 / D, scalar2=1e-6,
                                                                   op0=ALU.mult, op1=ALU.add), reads=[ssqb[i2]], writes=[rsb])
            S.op("act", lambda e, rs1=rs1, rs2=rs2: e.activation(out=rs2[:, 0:1], in_=rs1[:, 0:1], func=AF.Sqrt), reads=[rsb], writes=[rsb])
            S.op("dve", lambda e, rs1=rs1, rs2=rs2: e.reciprocal(out=rs1[:, 0:1], in_=rs2[:, 0:1]), reads=[rsb], writes=[rsb])
            S.op("act", lambda e, rs1=rs1, i2=i2: e.activation(out=xsb[i2][:], in_=xt[i2][:], func=AF.Copy, scale=rs1[:, 0:1]),
                 reads=[xtb[i2], rsb], writes=[xsbb[i2]])
            if blk + 1 < NBLK:
                ph1_sq(blk + 1)
            if blk >= MB0:
                for j in range(4):
                    pbk = 2 + j % 2
                    for kk in range(4):
                        kt = 4 * j + kk
                        S.op("pe", lambda e, pbk=pbk, kk=kk, kt=kt, i2=i2: e.transpose(
                            out=PS[pbk][:, 128 * kk:128 * kk + 128], in_=xt[i2][:, 128 * kt:128 * kt + 128], identity=ident[:]),
                            reads=[xtb[i2], cbuf], writes=[PSB[pbk]], sig=(kk == 3))
                    S.op("act", lambda e, j=j, pbk=pbk, xTs=xTs: e.activation(
                        out=xTs[:, 4 * j:4 * j + 4, :].rearrange("p k c -> p (k c)"), in_=PS[pbk][:], func=AF.Copy),
                        reads=[PSB[pbk]], writes=[xTb])
                col = (blk - MB0) * 128
                S.dma("sp", XTv[:, :, col:col + 128], xTs[:], reads=[xTb], writes=[], sem_buf=xTb)
            for j in range(2):
                jb_ = j + (2 if (blk % 2 == 1 and blk < MB0) else 0)
                psv = PS[jb_][:].bitcast(BF16)
                for kk in range(8):
                    kt = 8 * j + kk
                    S.op("pe", lambda e, psv=psv, kk=kk, kt=kt, i2=i2: e.transpose(
                        out=psv[:, 128 * kk:128 * kk + 128], in_=xsb[i2][:, 128 * kt:128 * kt + 128], identity=identb[:]),
                        reads=[xsbb[i2], cbuf2], writes=[PSB[jb_]], sig=(kk == 7))
                S.op("dve", lambda e, psv=psv, j=j, sub=sub, tcol=tcol: e.tensor_tensor(
                    out=hTs[sub][:, 8 * j:8 * j + 8, tcol:tcol + 128], in0=psv[:, 0:1024].rearrange("p (k c) -> p k c", k=8),
                    in1=g1c[:, 8 * j:8 * j + 8].unsqueeze(2).to_broadcast([128, 8, 128]), op=ALU.mult),
                    reads=[PSB[jb_], cbuf], writes=[hTb[sub]])
            for fn_ in deferred.pop(blk, []):
                fn_()
            if blk >= MB0:
                S.dma("sp", HTv[:, :, col:col + 128], hTs[sub][:, :, tcol:tcol + 128], reads=[hTb[sub]], writes=[],
                      sem_buf=hTb[sub])
            if blk % 4 == 3:
                mi = (blk // 4) % 2
                boff = ((blk // 4) % 2) * 512
                for m in range(4):
                    pb = 4 + m % 2
                    for kt in range(16):
                        S.op("pe", lambda e, m=m, kt=kt, sub=sub, pb=pb: e.matmul(
                            PS[pb][:], lhsT=Wu[:, kt, 128 * m:128 * m + 128], rhs=hTs[sub][:, kt, :],
                            start=(kt == 0), stop=(kt == 15)),
                            reads=[wub, hTb[sub]], writes=[PSB[pb]], sig=(kt == 15))
                    S.op("dve", lambda e, m=m, pb=pb, mi=mi, boff=boff: e.scalar_tensor_tensor(
                        out=ub[:, m, boff:boff + 512], in0=PS[pb][:], scalar=binc[:, 10 + m:11 + m], in1=mk[mi][:],
                        op0=ALU.add, op1=ALU.mult), reads=[PSB[pb], mkb[mi], cbuf], writes=[ubb])
                    if blk >= MB0 - 2:
                        b0 = blk - 3
                        lo = max(b0, MB0)
                        n = (blk + 1 - lo) * 128
                        so = boff + (lo - b0) * 128
                        do = (lo - MB0) * 128
                        S.op("act", lambda e, m=m, so=so, do=do, n=n: e.activation(
                            out=um[:, m, do:do + n], in_=ub[:, m, so:so + n], func=AF.Copy), reads=[ubb], writes=[umb])
            if blk % 8 == 7:
                bi = blk // 8
                Ssb, Ssbb = Ssb2[bi % 2], Ssbb2[bi % 2]
                Zh, zb = Zhs[bi % 2], zhb[bi % 2]
                def level01(q, bi=bi, Ssb=Ssb, Ssbb=Ssbb):
                    s4, s4b = s4q[q], s4qb[q]
                    for k in range(4):
                        pq = 6 + k % 2
                        for ri in range(2):
                            for b4 in range(4):
                                S.op("pe", lambda e, q=q, k=k, ri=ri, b4=b4, pq=pq: e.matmul(
                                    PS[pq][:, 256 * ri:256 * ri + 256], lhsT=SW[32 * q:32 * q + 32, 3 - b4, ri, k, :],
                                    rhs=ub[32 * q:32 * q + 32, k, b4:1024:4], start=(b4 == 0), stop=(b4 == 3),
                                    tile_position=(32 * q, 0), skip_group_check=True),
                                    reads=[swb, ubb], writes=[PSB[pq]], sig=(ri == 1 and b4 == 3))
                        S.op("act", lambda e, k=k, pq=pq, s4=s4: e.activation(
                            out=s4[:, :, k, :], in_=PS[pq][:].rearrange("p (r c) -> p r c", r=2), func=AF.Copy),
                            reads=[PSB[pq]], writes=[s4b])
                    s4v = s4[:].rearrange("p r k (c a) -> p r k c a", a=4)
                    a4r = A4re[:].rearrange("p (k q) -> p q k", q=4)[:, q, :].unsqueeze(2).to_broadcast([128, 4, 64])
                    a4i = A4im[:].rearrange("p (k q) -> p q k", q=4)[:, q, :].unsqueeze(2).to_broadcast([128, 4, 64])
                    onp = True
                    en_ = "pool" if onp else "dve"
                    Y_ = Yh if onp else Yhb
                    hb_ = hb if onp else hb2
                    tq = [t[:].rearrange("p (k c) -> p k c", k=4) for t in (HT4 if onp else HT4b)]
                    Sq = Ssb[:].rearrange("p c r (k q) -> p r q k c", q=4)
                    cma(en_, Y_[:, 0], Y_[:, 1], s4v[:, 0, :, :, 0], s4v[:, 1, :, :, 0], a4r, a4i,
                        s4v[:, 0, :, :, 1], s4v[:, 1, :, :, 1], tq, [s4b, hb_, cbuf], [hb_])
                    cma(en_, Y_[:, 0], Y_[:, 1], Y_[:, 0], Y_[:, 1], a4r, a4i,
                        s4v[:, 0, :, :, 2], s4v[:, 1, :, :, 2], tq, [s4b, hb_, cbuf], [hb_])
                    cma(en_, Sq[:, 0, q], Sq[:, 1, q], Y_[:, 0], Y_[:, 1], a4r, a4i,
                        s4v[:, 0, :, :, 3], s4v[:, 1, :, :, 3], tq, [s4b, hb_, cbuf], [hb_, Ssbb])

                def batch_tail(bi=bi, Ssb=Ssb, Ssbb=Ssbb, Zh=Zh, zb=zb):
                    nonlocal cur
                    if bi < 5:
                        Sv = Ssb[:].rearrange("p (m a) r g -> p a r m g", a=4)
                        a16r = A16re[:].unsqueeze(1).to_broadcast([128, 16, 16])
                        a16i = A16im[:].unsqueeze(1).to_broadcast([128, 16, 16])
                        tz = [t[:].rearrange("p (m g) -> p m g", m=16) for t in HT4]
                        cma("pool", Zh[:, 0], Zh[:, 1], Sv[:, 0, 0], Sv[:, 0, 1], a16r, a16i, Sv[:, 1, 0], Sv[:, 1, 1], tz,
                            [Ssbb, hb, cbuf], [hb, zb])
                        cma("pool", Zh[:, 0], Zh[:, 1], Zh[:, 0], Zh[:, 1], a16r, a16i, Sv[:, 2, 0], Sv[:, 2, 1], tz,
                            [Ssbb, hb, cbuf, zb], [hb, zb])
                        cma("pool", Zh[:, 0], Zh[:, 1], Zh[:, 0], Zh[:, 1], a16r, a16i, Sv[:, 3, 0], Sv[:, 3, 1], tz,
                            [Ssbb, hb, cbuf, zb], [hb, zb])
                        for m in range(16):
                            xa, xb = X4[cur], X4[1 - cur]
                            pls = lambda fn, r=(stb, cbuf), w=(stb,): S.op("pool", fn, reads=list(r), writes=list(w))
                            pls(lambda e, xa=xa: e.tensor_tensor(out=P1[:], in0=AA64[:], in1=xa[:, 0:2, :], op=ALU.mult))
                            pls(lambda e, xa=xa: e.tensor_tensor(out=P2[:], in0=BB64[:], in1=xa[:, 1:3, :], op=ALU.mult))
                            pls(lambda e: e.tensor_tensor(out=P1[:], in0=P1[:], in1=P2[:], op=ALU.add))
                            S.op("pool", lambda e, xb=xb, m=m, Zh=Zh: e.tensor_tensor(out=xb[:, 0:2, :], in0=P1[:], in1=Zh[:, :, m, :], op=ALU.add),
                                 reads=[stb, zb], writes=[stb])
                            pls(lambda e, xb=xb: e.tensor_copy(out=xb[:, 2, :], in_=xb[:, 0, :]))
                            cur = 1 - cur
                    else:
                        jb = bi * 64
                        for jj in range(64):
                            j = jb + jj
                            xa, xb = X4[cur], X4[1 - cur]
                            if j >= NCH0:
                                S.op("pool", lambda e, xa=xa, j=j: e.tensor_copy(out=Xs[:, :, :, j - NCH0], in_=xa[:, 0:2, :]),
                                     reads=[stb], writes=[Xsb])
                            pls = lambda fn, r=(stb, cbuf), w=(stb,): S.op("pool", fn, reads=list(r), writes=list(w))
                            pls(lambda e, xa=xa: e.tensor_tensor(out=P1[:], in0=AAt[:], in1=xa[:, 0:2, :], op=ALU.mult))
                            pls(lambda e, xa=xa: e.tensor_tensor(out=P2[:], in0=BBt[:], in1=xa[:, 1:3, :], op=ALU.mult))
                            pls(lambda e: e.tensor_tensor(out=P1[:], in0=P1[:], in1=P2[:], op=ALU.add))
                            S.op("pool", lambda e, xb=xb, jj=jj, Ssb=Ssb: e.tensor_tensor(out=xb[:, 0:2, :], in0=P1[:], in1=Ssb[:, jj, :, :], op=ALU.add),
                                 reads=[stb, Ssbb], writes=[stb])
                            pls(lambda e, xb=xb: e.tensor_copy(out=xb[:, 2, :], in_=xb[:, 0, :]))
                            cur = 1 - cur
                level01(0)
                level01(1)
                if blk + 2 < NBLK:
                    deferred.setdefault(blk + 1, []).append(lambda f=level01: f(2))
                    deferred.setdefault(blk + 2, []).append(lambda f=level01: f(3))
                    deferred.setdefault(blk + 2, []).append(batch_tail)
                else:
                    level01(2)
                    level01(3)
                    batch_tail()
        if UM is not None:
            S.dma("sp", UM.rearrange("k p c -> p k c"), um[:], reads=[umb], writes=[], sem_buf=umb)
        S.barrier()
        finals = []
        if UM is not None:
            finals.append(umb)
        if self.stop == "ph1":
            return S, finals
        FF = self.sb("FF", [128, 16, 2, 16, 32], BF16, 62 * KB)
        TZ = self.sb("TZ", [128, 16, 4, 128], BF16, 94 * KB)
        ffb2 = S.buf("FF")
        tzb2 = S.buf("TZ")
        S.dma("sp", FF[:].rearrange("p t r g c -> p (t r g c)"), FFd, writes=[ffb2], sem_buf=ffb2)
        S.dma("sp", TZ[:].rearrange("p l k c -> p (l k c)"), TZd, writes=[tzb2], sem_buf=tzb2)
        yg = self.sb("yg", [128, 4, NMAIN], BF16, 12 * KB)
        ygb = S.buf("yg")
        ytmp = [self.sb("ytmp%d" % i, [128, NCHM], F32, 110 * KB + i * 1024) for i in range(2)]
        ytb = S.bufs(2, "ytmp")
        it = 0
        for k in range(4):
            for tau in range(16):
                pb = it % 4
                yi = it % 2
                it += 1
                for s in range(tau + 1):
                    S.op("pe", lambda e, pb=pb, tau=tau, s=s, k=k: e.matmul(
                        PS[pb][:, 0:NCHM], lhsT=TZ[:, tau - s, k, :], rhs=um[:, k, s:NMAIN:16],
                        start=(s == 0), stop=False, skip_group_check=True),
                        reads=[tzb2, umb], writes=[PSB[pb]], sig=False)
                for q in range(4):
                    for ri in range(2):
                        last = (q == 3 and ri == 1)
                        S.op("pe", lambda e, pb=pb, tau=tau, q=q, ri=ri, k=k, last=last: e.matmul(
                            PS[pb][32 * q:32 * q + 32, 0:NCHM], lhsT=FF[:, tau, ri, 4 * k + q, :], rhs=Xs[:, ri, 4 * k + q, :],
                            start=False, stop=last, tile_position=(0, 32 * q), skip_group_check=True),
                            reads=[ffb2, Xsb], writes=[PSB[pb]], sig=last)
                S.op("dve", lambda e, pb=pb, tau=tau, k=k, yi=yi: e.scalar_tensor_tensor(
                    out=ytmp[yi][:], in0=um[:, k, tau:NMAIN:16], scalar=dcol[:, k:k + 1], in1=PS[pb][:, 0:NCHM],
                    op0=ALU.mult, op1=ALU.add), reads=[PSB[pb], umb, cbuf], writes=[ytb[yi]])
                S.op("act", lambda e, tau=tau, k=k, yi=yi: e.activation(
                    out=yg[:, k, tau:NMAIN:16], in_=ytmp[yi][:], func=AF.Gelu), reads=[ytb[yi]], writes=[ygb])
        if YG is not None:
            S.dma("sp", YG.rearrange("k p c -> p k c"), yg[:], reads=[ygb], writes=[], sem_buf=ygb)
            finals.append(ygb)
        S.barrier()
        if self.stop == "ph1b":
            return S, finals
        self.pbn = 0

        def nb():
            self.pbn = (self.pbn + 1) % 8
            return self.pbn

        MT = [(0, 512), (512, 512), (1024, 512), (1536, 512), (2048, 256)]
        VT = [(128, 512), (640, 512), (1152, 512), (1664, 512), (2176, 128)]

        def loadw(eng, dst, dbuf, src, nkt):
            S.dma(eng, dst[:, 0:nkt, :], src.rearrange("(kt p) c -> p kt c", p=128), writes=[dbuf], sem_buf=dbuf)

        SG = self.dscr("SG", [32, 128, NMAIN], BF16)
        X1T = self.dscr("X1T", [16, 128, NMAIN], F32)
        ACTd = self.dscr("ACTd", [44, 128, 2048], BF16)
        X2T = self.dscr("X2T", [16, 128, 2048], F32)
        qT = self.sb("qT", [128, 8, NMAIN], BF16, 30 * KB)
        qb = S.buf("qT")
        kd = [self.sb("kd%d" % g, [128, NMAIN], BF16, 66 * KB + g * 4608) for g in range(2)]
        kdb = S.bufs(2, "kd")
        hTm = self.sb("hTm", [128, 16, NMAIN], BF16, 75 * KB)
        hTmb = S.buf("hTm")
        wt = [self.sb("wt%d" % i, [128, 16, 128], BF16, (147 + 4 * i) * KB) for i in range(2)]
        wtb = S.bufs(2, "wt")
        vsb = self.sb("vsb", [128, 18, 128], BF16, 155 * KB)
        vsbb = S.buf("vsb")
        gst = [self.sb("gst%d" % i, [128, 512], BF16, (160 + i) * KB) for i in range(2)]
        gstb = S.bufs(2, "gst")
        bvb = self.sb("bvb", [128, 128], F32, 162 * KB)
        bkd = self.sb("bkd", [128, 2], F32, 163 * KB)
        sinkc = self.sb("sinkc", [128, 16], F32, 163 * KB + 64)
        NM = self.sb("NM", [128, 128], F32, 164 * KB)
        c2b = S.buf("c2")
        S.dma("sp", hTm[:], HT.rearrange("k p c -> p k c"), writes=[hTmb], sem_buf=hTmb)
        S.dma("sp", bvb[:], b_in[1152:1280].partition_broadcast(128), writes=[c2b], sem_buf=c2b, nonc=True)
        for g in range(2):
            for hh in range(2):
                S.dma("sp", bkd[64 * hh:64 * hh + 64, g:g + 1], b_in[1024 + 64 * g:1088 + 64 * g].rearrange("(p o) -> p o", o=1),
                      writes=[c2b], sem_buf=c2b, nonc=True)
        S.dma("sp", sinkc[:], sinks.partition_broadcast(128), writes=[c2b], sem_buf=c2b, nonc=True)
        S.dma("sp", NM[:], maskw[MB0 * 128 + 128:MB0 * 128 + 256].partition_broadcast(128), writes=[c2b], sem_buf=c2b, nonc=True)
        S.op("dve", lambda e: e.tensor_scalar(out=NM[:], in0=NM[:], scalar1=-1.0, scalar2=30000.0, op0=ALU.add, op1=ALU.mult),
             reads=[c2b], writes=[c2b])
        wi = 0
        jobs = [("q", cb) for cb in range(8)] + [("k", g) for g in range(2)] + [("g", cb) for cb in range(14, 46)]
        gi = 0
        for kind, cb in jobs:
            w = wt[wi % 2]
            wb = wtb[wi % 2]
            wi += 1
            if kind == "k":
                for hh in range(2):
                    S.dma("pool", w[:, :, 64 * hh:64 * hh + 64],
                          w_in[:, 1024 + 64 * cb:1088 + 64 * cb].rearrange("(kt p) c -> p kt c", p=128), writes=[wb], sem_buf=wb)
            else:
                loadw("pool", w, wb, w_in[:, 128 * cb:128 * cb + 128], 16)
            for (c0, n) in (MT if kind == "k" else VT):
                pb = nb()
                for kt in range(16):
                    S.op("pe", lambda e, pb=pb, w=w, kt=kt, c0=c0, n=n: e.matmul(
                        PS[pb][:, 0:n], lhsT=w[:, kt, :], rhs=hTm[:, kt, c0:c0 + n], start=(kt == 0), stop=(kt == 15)),
                        reads=[wb, hTmb], writes=[PSB[pb]], sig=(kt == 15))
                if kind == "q":
                    S.op("act", lambda e, pb=pb, cb=cb, c0=c0, n=n: e.activation(
                        out=qT[:, cb, c0:c0 + n], in_=PS[pb][:, 0:n], func=AF.Identity, bias=binc[:, cb:cb + 1]),
                        reads=[PSB[pb], cbuf], writes=[qb])
                elif kind == "k":
                    S.op("act", lambda e, pb=pb, cb=cb, c0=c0, n=n: e.activation(
                        out=kd[cb][:, c0:c0 + n], in_=PS[pb][:, 0:n], func=AF.Identity, bias=bkd[:, cb:cb + 1]),
                        reads=[PSB[pb], c2b], writes=[kdb[cb]])
                else:
                    gs_ = gst[gi % 2]
                    gsb = gstb[gi % 2]
                    gi += 1
                    S.op("act", lambda e, pb=pb, cb=cb, n=n, gs_=gs_: e.activation(
                        out=gs_[:, 0:n], in_=PS[pb][:, 0:n], func=AF.Sigmoid, bias=binc[:, cb:cb + 1]),
                        reads=[PSB[pb], cbuf], writes=[gsb])
                    S.dma("sp", SG[cb - 14][:, c0:c0 + n], gs_[:, 0:n], reads=[gsb], writes=[], sem_buf=gsb)
        w = wt[wi % 2]
        wb = wtb[wi % 2]
        wi += 1
        loadw("pool", w, wb, w_in[:, 1152:1280], 16)
        for mb in range(18):
            pb = nb()
            for kt in range(16):
                S.op("pe", lambda e, pb=pb, w=w, kt=kt, mb=mb: e.matmul(
                    PS[pb][:, 0:128], lhsT=hTm[:, kt, 128 * mb:128 * mb + 128], rhs=w[:, kt, :], start=(kt == 0), stop=(kt == 15)),
                    reads=[wb, hTmb], writes=[PSB[pb]], sig=(kt == 15))
            S.op("dve", lambda e, pb=pb, mb=mb: e.tensor_tensor(out=vsb[:, mb, :], in0=PS[pb][:, 0:128], in1=bvb[:], op=ALU.add),
                 reads=[PSB[pb], c2b], writes=[vsbb])
        S.barrier()
        if self.stop == "ph2":
            return S, finals
        attnT = self.sb("attnT", [128, 8, NMAIN], BF16, 75 * KB)
        atb = S.buf("attnT")
        atbs = S.bufs(8, "attnTk")
        AB = self.sb("AB", [128, 16, 256], F32, 111 * KB)
        abb = S.buf("AB")
        scS = [self.sb("sc0", [128, 16, 256], F32, 127 * KB), self.sb("sc1", [128, 16, 256], F32, 174 * KB)]
        scbS = [S.bufs(16, "sc0_"), S.bufs(16, "sc1_")]
        pnS = [self.sb("pn0", [128, 16, 256], BF16, 143 * KB), self.sb("pn1", [128, 16, 256], BF16, 190 * KB)]
        pnbS = [S.bufs(16, "pn0_"), S.bufs(16, "pn1_")]
        pTsS = [self.sb("pTs0", [128, 16, 256], BF16, 165 * KB), self.sb("pTs1", [128, 16, 256], BF16, 198 * KB)]
        pTbS = [S.bufs(16, "pT0_"), S.bufs(16, "pT1_")]
        vecS = []
        for i in range(2):
            vo = (173 if i == 0 else 151) * KB
            vecS.append([self.sb("av%d_%d" % (i, t), [128, 16], F32, vo + 64 * t) for t in range(5)])
        vbS = S.bufs(2, "attvec")
        S.dma("sp", AB[:], abias, writes=[abb], sem_buf=abb)
        def att_setup(mb):
            c0 = 128 * mb
            return (c0,) + (scS[mb % 2], scbS[mb % 2], pnS[mb % 2], pnbS[mb % 2], pTsS[mb % 2], pTbS[mb % 2]) + tuple(vecS[mb % 2]) + (vbS[mb % 2],)

        def stageA(mb):
            c0, sc, scb, pn, pnb, pTs, pTb, mx, nmx, rsum, es, rden, vb_ = att_setup(mb)
            for base in (0, 4, 8, 12):
                for par in (0, 1):
                    h0 = base + par
                    g = h0 // 8
                    r0 = 64 * par
                    pb = nb()
                    for i_, h in enumerate((h0, h0 + 2)):
                        S.op("pe", lambda e, pb=pb, h=h, g=g, r0=r0, i_=i_: e.matmul(
                            PS[pb][:, 256 * i_:256 * i_ + 256], lhsT=qT[r0:r0 + 64, h // 2, c0:c0 + 128],
                            rhs=kd[g][r0:r0 + 64, c0 - 128:c0 + 128], start=True, stop=True, skip_group_check=True),
                            reads=[qb, kdb[g]], writes=[PSB[pb]], sig=(i_ == 1))
                    S.op("dve", lambda e, pb=pb, h0=h0: e.scalar_tensor_tensor(
                        out=sc[:, h0:h0 + 3:2, :], in0=PS[pb][:].rearrange("p (h c) -> p h c", h=2), scalar=0.125,
                        in1=AB[:, h0:h0 + 3:2, :], op0=ALU.mult, op1=ALU.add),
                        reads=[PSB[pb], abb], writes=[scb[h0], scb[h0 + 2]])
                    if mb == 2:
                        for h in (h0, h0 + 2):
                            S.op("dve", lambda e, h=h: e.tensor_tensor(out=sc[:, h, 0:128], in0=sc[:, h, 0:128], in1=NM[:], op=ALU.add),
                                 reads=[scb[h], c2b], writes=[scb[h]])
                for h in (base + 1, base + 3):
                    S.op("dve", lambda e, h=h: e.reduce_max(out=mx[:, h - 1:h + 1], in_=sc[:, h - 1:h + 1, :],
                                                             axis=mybir.AxisListType.X),
                         reads=[scb[h - 1], scb[h]], writes=[vb_])
            S.op("dve", lambda e, mx=mx: e.tensor_tensor(out=mx[:], in0=mx[:], in1=sinkc[:], op=ALU.max), reads=[vb_, c2b], writes=[vb_])
            S.op("dve", lambda e, mx=mx, nmx=nmx: e.tensor_scalar_mul(out=nmx[:], in0=mx[:], scalar1=-1.0), reads=[vb_], writes=[vb_])
            S.op("dve", lambda e, es=es, nmx=nmx: e.tensor_tensor(out=es[:], in0=sinkc[:], in1=nmx[:], op=ALU.add), reads=[vb_, c2b], writes=[vb_])
            S.op("act", lambda e, es=es: e.activation(out=es[:], in_=es[:], func=AF.Exp), reads=[vb_], writes=[vb_])
            for h in range(16):
                S.op("act", lambda e, h=h, sc=sc, nmx=nmx, rsum=rsum: e.activation(out=sc[:, h, :], in_=sc[:, h, :], func=AF.Exp, bias=nmx[:, h:h + 1],
                                                          accum_out=rsum[:, h:h + 1]), reads=[scb[h], vb_], writes=[scb[h], vb_])

        def stageB(mb):
            c0, sc, scb, pn, pnb, pTs, pTb, mx, nmx, rsum, es, rden, vb_ = att_setup(mb)
            S.op("dve", lambda e, rden=rden, rsum=rsum, es=es: e.tensor_tensor(out=rden[:], in0=rsum[:], in1=es[:], op=ALU.add), reads=[vb_], writes=[vb_])
            S.op("dve", lambda e, rden=rden: e.reciprocal(out=rden[:], in_=rden[:]), reads=[vb_], writes=[vb_])
            banks = {}

            def b_first(hp):
                pb = nb()
                banks[hp] = pb
                psv = PS[pb][:].bitcast(BF16)
                S.op("dve", lambda e, hp=hp: e.tensor_tensor(
                    out=pn[:, 2 * hp:2 * hp + 2, :], in0=sc[:, 2 * hp:2 * hp + 2, :],
                    in1=rden[:, 2 * hp:2 * hp + 2].unsqueeze(2).to_broadcast([128, 2, 256]), op=ALU.mult),
                    reads=[scb[2 * hp], scb[2 * hp + 1], vb_], writes=[pnb[2 * hp], pnb[2 * hp + 1]])
                for h in (2 * hp, 2 * hp + 1):
                    for hf in range(2):
                        o0 = 256 * (h % 2) + 128 * hf
                        S.op("pe", lambda e, psv=psv, h=h, hf=hf, o0=o0: e.transpose(
                            out=psv[:, o0:o0 + 128], in_=pn[:, h, 128 * hf:128 * hf + 128], identity=identb[:]),
                            reads=[pnb[h], cbuf2], writes=[PSB[pb]], sig=(h % 2 == 1 and hf == 1))
                S.op("act", lambda e, psv=psv, hp=hp: e.activation(
                    out=pTs[:, 2 * hp:2 * hp + 2, :].rearrange("p h c -> p (h c)"), in_=psv[:, 0:512], func=AF.Copy),
                    reads=[PSB[pb]], writes=[pTb[2 * hp]])

            def b_second(hp):
                pb2 = nb()
                for h in (2 * hp, 2 * hp + 1):
                    g = h // 8
                    r0 = 64 * (h % 2)
                    for hf in range(2):
                        S.op("pe", lambda e, pb2=pb2, h=h, hf=hf, g=g, r0=r0: e.matmul(
                            PS[pb2][r0:r0 + 64, 0:128], lhsT=vsb[:, mb - 1 + hf, 64 * g:64 * g + 64],
                            rhs=pTs[:, h, 128 * hf:128 * hf + 128], start=(hf == 0), stop=(hf == 1), skip_group_check=True),
                            reads=[vsbb, pTb[2 * hp]], writes=[PSB[pb2]], sig=(h % 2 == 1 and hf == 1))
                S.op("act", lambda e, pb2=pb2, hp=hp: e.activation(
                    out=attnT[:, hp, c0:c0 + 128], in_=PS[pb2][:, 0:128], func=AF.Copy),
                    reads=[PSB[pb2]], writes=[atbs[hp]])

            for i in range(9):
                if i < 8:
                    b_first(i)
                if i >= 1:
                    b_second(i - 1)

        stageA(1)
        for mb in range(1, 18):
            if mb + 1 < 18:
                stageA(mb + 1)
            stageB(mb)
        if "ATT" in dbg:
            ATT = self.dscr("ATT", [8, 128, NMAIN], BF16)
            S.dma("sp", ATT.rearrange("k p c -> p k c"), attnT[:], reads=[atb] + atbs, writes=[], sem_buf=atb)
            finals.append(atb)
        S.barrier()
        if self.stop == "ph3":
            return S, finals
        ssmT = self.sb("ssmT", [128, 4, NMAIN], BF16, 30 * KB)
        ssb = S.buf("ssmT")
        wg2 = [self.sb("wg2_%d" % i, [128, 4, 256], BF16, (48 + 2 * i) * KB) for i in range(2)]
        wg2b = S.bufs(2, "wg2")
        sgt = [self.sb("sgt%d" % i, [128, 512], F32, (52 + 2 * i) * KB) for i in range(2)]
        sgtb = S.bufs(2, "sgt")
        it = 0
        for c in range(4):
            w = wg2[c % 2]
            wb = wg2b[c % 2]
            S.dma("pool", w[:, :, 0:128], w_glu[:, 128 * c:128 * c + 128].rearrange("(kt p) c -> p kt c", p=128), writes=[wb], sem_buf=wb)
            S.dma("pool", w[:, :, 128:256], w_glu[:, 512 + 128 * c:640 + 128 * c].rearrange("(kt p) c -> p kt c", p=128),
                  writes=[wb], sem_buf=wb)
            for (c0, n) in VT:
                pv, pg = nb(), nb()
                for kt in range(4):
                    S.op("pe", lambda e, pv=pv, w=w, kt=kt, c0=c0, n=n: e.matmul(
                        PS[pv][:, 0:n], lhsT=w[:, kt, 0:128], rhs=yg[:, kt, c0:c0 + n], start=(kt == 0), stop=(kt == 3)),
                        reads=[wb, ygb], writes=[PSB[pv]], sig=(kt == 3))
                for kt in range(4):
                    S.op("pe", lambda e, pg=pg, w=w, kt=kt, c0=c0, n=n: e.matmul(
                        PS[pg][:, 0:n], lhsT=w[:, kt, 128:256], rhs=yg[:, kt, c0:c0 + n], start=(kt == 0), stop=(kt == 3)),
                        reads=[wb, ygb], writes=[PSB[pg]], sig=(kt == 3))
                si = it % 2
                it += 1
                S.op("act", lambda e, pg=pg, c=c, n=n, si=si: e.activation(
                    out=sgt[si][:, 0:n], in_=PS[pg][:, 0:n], func=AF.Sigmoid, bias=bgluc[:, 4 + c:5 + c]),
                    reads=[PSB[pg], cbuf], writes=[sgtb[si]])
                S.op("dve", lambda e, pv=pv, c=c, c0=c0, n=n, si=si: e.scalar_tensor_tensor(
                    out=ssmT[:, c, c0:c0 + n], in0=PS[pv][:, 0:n], scalar=bgluc[:, c:c + 1], in1=sgt[si][:, 0:n],
                    op0=ALU.add, op1=ALU.mult), reads=[PSB[pv], sgtb[si], cbuf], writes=[ssb])
        S.barrier()
        mgT = self.sb("mgT", [128, 16, NMAIN], BF16, 111 * KB)
        mgb = S.buf("mgT")
        wa = [self.sb("wa%d" % i, [128, 8, 128], BF16, (48 + 2 * i) * KB) for i in range(2)]
        wab = S.bufs(2, "wa")
        ws_ = [self.sb("ws%d" % i, [128, 4, 128], BF16, (52 + i) * KB) for i in range(2)]
        wsb = S.bufs(2, "ws")
        sga = [self.sb("sga%d" % i, [128, NMAIN], BF16, 54 * KB + i * 4608) for i in range(2)]
        sgab = S.bufs(2, "sga")
        sgs = [self.sb("sgs%d" % i, [128, NMAIN], BF16, 63 * KB + i * 4608) for i in range(2)]
        sgsb = S.bufs(2, "sgs")
        t1 = [self.sb("t1_%d" % i, [128, 512], F32, (183 + 2 * i) * KB) for i in range(2)]
        t1b = S.bufs(2, "t1")
        t2 = [self.sb("t2_%d" % i, [128, 512], F32, (187 + 2 * i) * KB) for i in range(2)]
        t2b = S.bufs(2, "t2")
        it = 0
        for c in range(16):
            i2 = c % 2
            loadw("pool", wa[i2], wab[i2], w_ba[:, 128 * c:128 * c + 128], 8)
            loadw("pool", ws_[i2], wsb[i2], w_bs[:, 128 * c:128 * c + 128], 4)
            S.dma("sp", sga[i2][:], SG[c], writes=[sgab[i2]], sem_buf=sgab[i2])
            S.dma("sp", sgs[i2][:], SG[16 + c], writes=[sgsb[i2]], sem_buf=sgsb[i2])
            for (c0, n) in VT:
                pa, ps_ = nb(), nb()
                for kt in range(8):
                    S.op("pe", lambda e, pa=pa, i2=i2, kt=kt, c0=c0, n=n: e.matmul(
                        PS[pa][:, 0:n], lhsT=wa[i2][:, kt, :], rhs=attnT[:, kt, c0:c0 + n], start=(kt == 0), stop=(kt == 7)),
                        reads=[wab[i2], atbs[kt]], writes=[PSB[pa]], sig=(kt == 7))
                for kt in range(4):
                    S.op("pe", lambda e, ps_=ps_, i2=i2, kt=kt, c0=c0, n=n: e.matmul(
                        PS[ps_][:, 0:n], lhsT=ws_[i2][:, kt, :], rhs=ssmT[:, kt, c0:c0 + n], start=(kt == 0), stop=(kt == 3)),
                        reads=[wsb[i2], ssb], writes=[PSB[ps_]], sig=(kt == 3))
                ti = it % 2
                it += 1
                S.op("dve", lambda e, pa=pa, i2=i2, c0=c0, n=n, ti=ti: e.tensor_tensor(
                    out=t1[ti][:, 0:n], in0=PS[pa][:, 0:n], in1=sga[i2][:, c0:c0 + n], op=ALU.mult),
                    reads=[PSB[pa], sgab[i2]], writes=[t1b[ti]])
                S.op("dve", lambda e, ps_=ps_, i2=i2, c0=c0, n=n, ti=ti: e.tensor_tensor(
                    out=t2[ti][:, 0:n], in0=PS[ps_][:, 0:n], in1=sgs[i2][:, c0:c0 + n], op=ALU.mult),
                    reads=[PSB[ps_], sgsb[i2]], writes=[t2b[ti]])
                S.op("dve", lambda e, c=c, c0=c0, n=n, ti=ti: e.tensor_tensor(
                    out=mgT[:, c, c0:c0 + n], in0=t1[ti][:, 0:n], in1=t2[ti][:, 0:n], op=ALU.add),
                    reads=[t1b[ti], t2b[ti]], writes=[mgb])
        S.barrier()
        wo = [self.sb("wo%d" % i, [128, 16, 128], BF16, (12 + 4 * i) * KB) for i in range(2)]
        wob = S.bufs(2, "wo")
        xl = [self.sb("xl%d" % i, [128, 512], F32, (20 + 2 * i) * KB) for i in range(2)]
        xlb = S.bufs(2, "xl")
        x1s = [self.sb("x1s%d" % i, [128, 512], F32, (24 + 2 * i) * KB) for i in range(2)]
        x1sb = S.bufs(2, "x1s")
        sqt = [self.sb("sqt%d" % i, [128, 512], F32, (28 + 2 * i) * KB) for i in range(2)]
        sqtb = S.bufs(2, "sqt")
        ssacc = self.sb("ssacc", [128, NMAIN], F32, 32 * KB)
        ssab = S.buf("ssacc")
        rstd = self.sb("rstd", [128, NMAIN], F32, 183 * KB)
        rstb = S.buf("rstd")
        h2T = self.sb("h2T", [128, 16, 2050], BF16, 42 * KB)
        h2b = S.buf("h2T")
        S.op("dve", lambda e: e.memset(ssacc[:], 0.0), writes=[ssab])
        it = 0
        for c in range(16):
            i2 = c % 2
            loadw("pool", wo[i2], wob[i2], w_out[:, 128 * c:128 * c + 128], 16)
            for (c0, n) in VT:
                ti = it % 2
                it += 1
                S.dma("pool", xl[ti][:, 0:n], XT[c][:, c0:c0 + n], writes=[xlb[ti]], sem_buf=xlb[ti])
                pb = nb()
                for kt in range(16):
                    S.op("pe", lambda e, pb=pb, i2=i2, kt=kt, c0=c0, n=n: e.matmul(
                        PS[pb][:, 0:n], lhsT=wo[i2][:, kt, :], rhs=mgT[:, kt, c0:c0 + n], start=(kt == 0), stop=(kt == 15)),
                        reads=[wob[i2], mgb], writes=[PSB[pb]], sig=(kt == 15))
                S.op("dve", lambda e, pb=pb, n=n, ti=ti: e.tensor_tensor(
                    out=x1s[ti][:, 0:n], in0=PS[pb][:, 0:n], in1=xl[ti][:, 0:n], op=ALU.add),
                    reads=[PSB[pb], xlb[ti]], writes=[x1sb[ti]])
                S.dma("sp", X1T[c][:, c0:c0 + n], x1s[ti][:, 0:n], reads=[x1sb[ti]], writes=[], sem_buf=x1sb[ti])
                S.op("act", lambda e, n=n, ti=ti: e.activation(out=sqt[ti][:, 0:n], in_=x1s[ti][:, 0:n], func=AF.Square),
                     reads=[x1sb[ti]], writes=[sqtb[ti]])
                lo_ = max(c0, 254)
                S.op("act", lambda e, c=c, c0=c0, n=n, ti=ti, lo_=lo_: e.activation(
                    out=h2T[:, c, lo_ - 254:c0 + n - 254], in_=x1s[ti][:, lo_ - c0:n], func=AF.Copy, scale=g2c[:, c:c + 1]),
                    reads=[x1sb[ti], cbuf], writes=[h2b])
                S.op("dve", lambda e, c0=c0, n=n, ti=ti: e.tensor_tensor(
                    out=ssacc[:, c0:c0 + n], in0=ssacc[:, c0:c0 + n], in1=sqt[ti][:, 0:n], op=ALU.add),
                    reads=[sqtb[ti], ssab], writes=[ssab])

        def rstd_from(acc, accb, dst, dstb, tiles):
            for (c0, n) in tiles:
                pb = nb()
                S.op("pe", lambda e, pb=pb, c0=c0, n=n: e.matmul(PS[pb][:, 0:n], lhsT=onesf[:], rhs=acc[:, c0:c0 + n],
                                                                   start=True, stop=True),
                     reads=[accb, cbuf], writes=[PSB[pb]], sig=True)
                S.op("dve", lambda e, pb=pb, c0=c0, n=n: e.tensor_scalar(
                    out=dst[:, c0:c0 + n], in0=PS[pb][:, 0:n], scalar1=1.0 / D, scalar2=1e-6, op0=ALU.mult, op1=ALU.add),
                    reads=[PSB[pb]], writes=[dstb])
                S.op("act", lambda e, c0=c0, n=n: e.activation(out=dst[:, c0:c0 + n], in_=dst[:, c0:c0 + n], func=AF.Sqrt),
                     reads=[dstb], writes=[dstb])
                S.op("dve", lambda e, c0=c0, n=n: e.reciprocal(out=dst[:, c0:c0 + n], in_=dst[:, c0:c0 + n]),
                     reads=[dstb], writes=[dstb])

        rstd_from(ssacc, ssab, rstd, rstb, VT)
        for c in range(16):
            S.op("dve", lambda e, c=c: e.tensor_tensor(out=h2T[:, c, :], in0=h2T[:, c, :], in1=rstd[:, 254:2304], op=ALU.mult),
                 reads=[rstb, h2b], writes=[h2b])
        S.barrier()
        wu2 = [self.sb("wu2_%d" % i, [128, 16, 256], BF16, (111 + 8 * i) * KB) for i in range(2)]
        wu2b = S.bufs(2, "wu2")
        gbufs = [self.sb("gbuf%d" % i, [128, 2050], F32, (127 + 9 * i) * KB) for i in range(2)]
        gbbs = S.bufs(2, "gbuf")
        vbufs = [self.sb("vbuf%d" % i, [128, 2048], F32, (145 + 8 * i) * KB) for i in range(2)]
        vbbs = S.bufs(2, "vbuf")
        tcvs = [self.sb("tcv%d" % i, [128, 2048], F32, (161 + 8 * i) * KB) for i in range(2)]
        tcbs = S.bufs(2, "tcv")
        tgs = [self.sb("tg%d" % i, [128, 2048], F32, (177 + 8 * i) * KB) for i in range(2)]
        tgbs = S.bufs(2, "tg")
        ast = [self.sb("ast%d" % i, [128, 2048], BF16, (193 + 4 * i) * KB) for i in range(2)]
        astb = S.bufs(2, "ast")
        mk2 = self.sb("mk2", [128, 2], F32, 201 * KB)
        mk2b = S.buf("mk2")
        S.dma("sp", mk2[:], maskw[MB0 * 128 + 254:MB0 * 128 + 256].partition_broadcast(128), writes=[mk2b], sem_buf=mk2b, nonc=True)
        for j in range(44):
            i2 = j % 2
            w = wu2[i2]
            wb = wu2b[i2]
            gbuf, gbb, vbuf, vbb, tcv, tcb, tg, tgb = gbufs[i2], gbbs[i2], vbufs[i2], vbbs[i2], tcvs[i2], tcbs[i2], tgs[i2], tgbs[i2]
            S.dma("pool", w[:, :, 0:128], w_up[:, 128 * j:128 * j + 128].rearrange("(kt p) c -> p kt c", p=128), writes=[wb], sem_buf=wb)
            S.dma("pool", w[:, :, 128:256], w_up[:, DFF + 128 * j:DFF + 128 * j + 128].rearrange("(kt p) c -> p kt c", p=128),
                  writes=[wb], sem_buf=wb)
            ph = nb()
            for kt in range(16):
                S.op("pe", lambda e, ph=ph, w=w, kt=kt: e.matmul(PS[ph][:, 0:2], lhsT=w[:, kt, 128:256], rhs=h2T[:, kt, 0:2],
                                                                  start=(kt == 0), stop=(kt == 15)),
                     reads=[wb, h2b], writes=[PSB[ph]], sig=(kt == 15))
            S.op("dve", lambda e, ph=ph, gbuf=gbuf: e.tensor_tensor(out=gbuf[:, 0:2], in0=PS[ph][:, 0:2], in1=mk2[:], op=ALU.mult),
                 reads=[PSB[ph], mk2b], writes=[gbb])
            for t in range(4):
                pv, pg = nb(), nb()
                for kt in range(16):
                    S.op("pe", lambda e, pv=pv, w=w, kt=kt, t=t: e.matmul(
                        PS[pv][:], lhsT=w[:, kt, 0:128], rhs=h2T[:, kt, 2 + 512 * t:514 + 512 * t], start=(kt == 0), stop=(kt == 15)),
                        reads=[wb, h2b], writes=[PSB[pv]], sig=(kt == 15))
                for kt in range(16):
                    S.op("pe", lambda e, pg=pg, w=w, kt=kt, t=t: e.matmul(
                        PS[pg][:], lhsT=w[:, kt, 128:256], rhs=h2T[:, kt, 2 + 512 * t:514 + 512 * t], start=(kt == 0), stop=(kt == 15)),
                        reads=[wb, h2b], writes=[PSB[pg]], sig=(kt == 15))
                S.op("act", lambda e, pg=pg, t=t, gbuf=gbuf: e.activation(out=gbuf[:, 2 + 512 * t:514 + 512 * t], in_=PS[pg][:], func=AF.Copy),
                     reads=[PSB[pg]], writes=[gbb])
                S.op("dve", lambda e, pv=pv, t=t, vbuf=vbuf: e.tensor_copy(out=vbuf[:, 512 * t:512 * t + 512], in_=PS[pv][:]),
                     reads=[PSB[pv]], writes=[vbb])
            S.op("dve", lambda e, j=j, tcv=tcv, gbuf=gbuf: e.tensor_scalar(out=tcv[:], in0=gbuf[:, 0:2048], scalar1=cwc[:, 0, j:j + 1], scalar2=cbc[:, j:j + 1],
                                                        op0=ALU.mult, op1=ALU.add), reads=[gbb, cbuf], writes=[tcb])
            S.op("dve", lambda e, j=j, tcv=tcv, gbuf=gbuf: e.scalar_tensor_tensor(out=tcv[:], in0=gbuf[:, 1:2049], scalar=cwc[:, 1, j:j + 1], in1=tcv[:],
                                                               op0=ALU.mult, op1=ALU.add), reads=[gbb, tcb, cbuf], writes=[tcb])
            S.op("dve", lambda e, j=j, tcv=tcv, gbuf=gbuf: e.scalar_tensor_tensor(out=tcv[:], in0=gbuf[:, 2:2050], scalar=cwc[:, 2, j:j + 1], in1=tcv[:],
                                                               op0=ALU.mult, op1=ALU.add), reads=[gbb, tcb, cbuf], writes=[tcb])
            S.op("act", lambda e, tg=tg, tcv=tcv: e.activation(out=tg[:], in_=tcv[:], func=AF.Gelu), reads=[tcb], writes=[tgb])
            S.op("dve", lambda e, i2=i2, vbuf=vbuf, tg=tg: e.tensor_tensor(out=ast[i2][:], in0=vbuf[:], in1=tg[:], op=ALU.mult),
                 reads=[vbb, tgb], writes=[astb[i2]])
            S.dma("sp", ACTd[j], ast[i2][:], reads=[astb[i2]], writes=[], sem_buf=astb[i2])
        S.barrier()
        acth = self.sb("acth", [128, 44, 1024], BF16, 12 * KB)
        achb = S.buf("acth")
        achq = S.bufs(4, "acthq")
        wd = [self.sb("wd%d" % i, [128, 44, 128], BF16, (100 + 11 * i) * KB) for i in range(2)]
        wdb = S.bufs(2, "wd")
        xl9 = [self.sb("xl9_%d" % i, [128, 512], F32, (122 + 2 * i) * KB) for i in range(2)]
        xl9b = S.bufs(2, "xl9")
        x2s = [self.sb("x2s%d" % i, [128, 512], F32, (126 + 2 * i) * KB) for i in range(2)]
        x2sb = S.bufs(2, "x2s")
        sq9 = [self.sb("sq9_%d" % i, [128, 512], F32, (130 + 2 * i) * KB) for i in range(2)]
        sq9b = S.bufs(2, "sq9")
        ssa2 = self.sb("ssa2", [128, 2048], F32, 134 * KB)
        ssa2b = S.buf("ssa2")
        rstd2 = self.sb("rstd2", [128, 2048], F32, 142 * KB)
        rst2b = S.buf("rstd2")
        S.op("dve", lambda e: e.memset(ssa2[:], 0.0), writes=[ssa2b])
        it = 0
        wi = 0
        for hf in range(2):
            for qq in range(4):
                S.dma("sp", acth[:, 11 * qq:11 * qq + 11, :], ACTd.rearrange("j p c -> p j c")[:, 11 * qq:11 * qq + 11, 1024 * hf:1024 * hf + 1024],
                      writes=[achq[qq]], sem_buf=achq[qq])
            for c in range(16):
                i2 = wi % 2
                wi += 1
                loadw("pool", wd[i2], wdb[i2], w_down[:, 128 * c:128 * c + 128], 44)
                for tt in range(2):
                    o0 = 1024 * hf + 512 * tt
                    ti = it % 2
                    it += 1
                    S.dma("pool", xl9[ti][:], X1T[c][:, OWN0 + o0:OWN0 + o0 + 512], writes=[xl9b[ti]], sem_buf=xl9b[ti])
                    pb = nb()
                    for kt in range(44):
                        S.op("pe", lambda e, pb=pb, i2=i2, kt=kt, tt=tt: e.matmul(
                            PS[pb][:], lhsT=wd[i2][:, kt, :], rhs=acth[:, kt, 512 * tt:512 * tt + 512], start=(kt == 0), stop=(kt == 43)),
                            reads=[wdb[i2], achq[kt // 11]], writes=[PSB[pb]], sig=(kt == 43))
                    S.op("dve", lambda e, pb=pb, ti=ti: e.tensor_tensor(out=x2s[ti][:], in0=PS[pb][:], in1=xl9[ti][:], op=ALU.add),
                         reads=[PSB[pb], xl9b[ti]], writes=[x2sb[ti]])
                    S.dma("sp", X2T[c][:, o0:o0 + 512], x2s[ti][:], reads=[x2sb[ti]], writes=[], sem_buf=x2sb[ti])
                    S.op("act", lambda e, ti=ti: e.activation(out=sq9[ti][:], in_=x2s[ti][:], func=AF.Square),
                         reads=[x2sb[ti]], writes=[sq9b[ti]])
                    S.op("dve", lambda e, o0=o0, ti=ti: e.tensor_tensor(
                        out=ssa2[:, o0:o0 + 512], in0=ssa2[:, o0:o0 + 512], in1=sq9[ti][:], op=ALU.add),
                        reads=[sq9b[ti], ssa2b], writes=[ssa2b])
        rstd_from(ssa2, ssa2b, rstd2, rst2b, [(0, 512), (512, 512), (1024, 512), (1536, 512)])
        S.barrier()
        x2l = [self.sb("x2l%d" % i, [128, 16, 128], F32, (12 + 8 * i) * KB) for i in range(2)]
        x2lb = S.bufs(2, "x2l")
        otmps = [self.sb("otmp%d" % i, [128, 16, 128], F32, (28 + 32 * i) * KB) for i in range(2)]
        otbs = S.bufs(2, "otmp")
        oTs = [self.sb("oT%d" % i, [128, 16, 128], F32, (36 + 32 * i) * KB) for i in range(2)]
        oTbs = S.bufs(2, "oT")
        orow = [self.sb("orow%d" % i, [128, D], F32, (44 + 8 * i) * KB) for i in range(2)]
        orb = S.bufs(2, "orow")
        X2v = X2T.rearrange("k p c -> p k c")
        rcol = [self.sb("rcol%d" % i, [128, 1], F32, 76 * KB + 64 * i) for i in range(2)]
        rcolb = S.bufs(2, "rcol")
        S.dma("sp", x2l[0][:], X2v[:, :, 0:128], writes=[x2lb[0]], sem_buf=x2lb[0])
        for tblk in range(16):
            i2 = tblk % 2
            c0 = 128 * tblk
            otmp, otb, oT, oTb = otmps[i2], otbs[i2], oTs[i2], oTbs[i2]
            if tblk + 1 < 16:
                S.dma("sp", x2l[1 - i2][:], X2v[:, :, c0 + 128:c0 + 256], writes=[x2lb[1 - i2]], sem_buf=x2lb[1 - i2])
            pr = nb()
            S.op("pe", lambda e, pr=pr, c0=c0: e.matmul(PS[pr][:, 0:1], lhsT=rstd2[0:1, c0:c0 + 128], rhs=onesf[0:1, 0:1],
                                                         start=True, stop=True), reads=[rst2b, cbuf], writes=[PSB[pr]], sig=True)
            S.op("act", lambda e, pr=pr, i2=i2: e.activation(out=rcol[i2][:], in_=PS[pr][:, 0:1], func=AF.Copy),
                 reads=[PSB[pr]], writes=[rcolb[i2]])
            S.op("dve", lambda e, i2=i2, oT=oT: e.tensor_tensor(
                out=oT[:], in0=x2l[i2][:], in1=g3c[:].unsqueeze(2).to_broadcast([128, 16, 128]), op=ALU.mult),
                reads=[x2lb[i2], cbuf], writes=[oTb])
            pbs = []
            for j in range(4):
                pb = nb()
                pbs.append(pb)
                for kk in range(4):
                    kt = 4 * j + kk
                    S.op("pe", lambda e, pb=pb, kk=kk, kt=kt, oT=oT: e.transpose(
                        out=PS[pb][:, 128 * kk:128 * kk + 128], in_=oT[:, kt, :], identity=ident[:]),
                        reads=[oTb, cbuf], writes=[PSB[pb]], sig=(kk == 3))
            for j in range(4):
                pb = pbs[j]
                if j % 2 == 0:
                    S.op("act", lambda e, pb=pb, j=j, i2=i2: e.activation(out=orow[i2][:, 512 * j:512 * j + 512], in_=PS[pb][:], func=AF.Copy,
                                                                         scale=rcol[i2][:, 0:1]),
                         reads=[PSB[pb], rcolb[i2]], writes=[orb[i2]])
                else:
                    S.op("dve", lambda e, pb=pb, j=j, i2=i2: e.tensor_scalar_mul(out=orow[i2][:, 512 * j:512 * j + 512], in0=PS[pb][:],
                                                                                scalar1=rcol[i2][:, 0:1]),
                         reads=[PSB[pb], rcolb[i2]], writes=[orb[i2]])
            S.dma("sp", out[c0:c0 + 128, :], orow[i2][:], reads=[orb[i2]], writes=[], sem_buf=orb[i2])
        finals.extend(orb)
        return S, finals

    def finish(self, S, final_bufs):
        toks = []
        for b in final_bufs:
            if b.dsem is not None:
                toks.append([b.dsem, b.dsem.cnt])
        S.emit(toks)
        return self.nc


def _abias_table():
    qi = np.arange(128)[:, None]
    si = np.arange(256)[None, :]
    dist = qi + 128 - si
    band = (dist >= 0) & (dist < 128)
    slopes = 2.0 ** (-8.0 * np.arange(1, 17, dtype=np.float32) / 16)
    t = -slopes[None, :, None] * dist[:, None, :].astype(np.float32)
    t = np.where(band[:, None, :], t, np.float32(-30000.0))
    return np.ascontiguousarray(t.astype(np.float32))


def make_in_maps(inputs):
    x = np.asarray(inputs["x"], dtype=np.float32)
    maps = []
    ident = np.eye(128, dtype=np.float32)
    ab = _abias_table()
    for core in range(NCORES):
        b, c = core // 4, core % 4
        t1 = 2048 * (c + 1)
        xw = np.zeros((WIN, D), np.float32)
        xw[WIN - t1:] = x[b, :t1]
        mask = np.zeros((WIN,), np.float32)
        mask[WIN - t1:] = 1.0
        m = {"xw": xw, "maskw": mask, "identity": ident, "abias": ab}
        for k, v in inputs.items():
            if k == "x":
                continue
            v = np.asarray(v, dtype=np.float32)
            m[k] = np.ascontiguousarray(v[0]) if k != "final_norm_g" else np.ascontiguousarray(v)
        maps.append(m)
    return maps


_NC_CACHE = {}


def _get_nc():
    if "nc" not in _NC_CACHE:
        kb = K()
        S, finals = kb.build()
        _NC_CACHE["nc"] = kb.finish(S, finals)
    return _NC_CACHE["nc"]


def kernel(**inputs):
    nc = _get_nc()
    in_maps = make_in_maps(inputs)
    res = run_bass_kernel_spmd(nc, in_maps, core_ids=list(range(NCORES)))
    outs = [np.asarray(r["out"], dtype=np.float32) for r in res.results]
    full = np.stack(outs, 0).reshape(2, 4 * 2048, D)
    return full
```
